# Optimizing a Trainium2 kernel written in Bass

```python
import jax, jax.numpy as jnp
from jax import lax
import numpy as np

D_MODEL = 2048
BATCH = 8
SEQ = 2048
DEPTH = 1
DEC_BATCH = 128
DEC_SEQ = 4
PAST_LEN = 2048
PAGE_SIZE = 128

N_META = 16
MIX = D_MODEL
SB_WIDTH = MIX // 2
SB_HEADS = 8
SB_HD = SB_WIDTH // SB_HEADS
SB_BIAS_INIT = -8.0
GLA_WIDTH = MIX - SB_WIDTH
GLA_HEADS = 4
GLA_KEY = GLA_WIDTH // 2
GLA_DK = GLA_KEY // GLA_HEADS
GLA_DV = GLA_WIDTH // GLA_HEADS
GLA_RANK = 16
GLA_TAU = 16.0
GLA_CHUNK = 64
Q_BLOCK = 128
EPS = 1e-6
SPLIT_SIZES = (SB_WIDTH, SB_WIDTH, SB_WIDTH, SB_WIDTH, GLA_KEY, GLA_KEY, GLA_WIDTH, GLA_WIDTH, GLA_RANK)
N_IN = sum(SPLIT_SIZES)

kernel_name = 'stick_breaking_gla_hymba_step'


def rms_norm(x, g):
    xf = x.astype(jnp.float32)
    var = jnp.mean(xf * xf, axis=-1, keepdims=True)
    return (xf * lax.rsqrt(var + EPS) * g.astype(jnp.float32)).astype(x.dtype)


def branch_inputs(h, w_in, w_alpha, b_alpha):
    B, L, _ = h.shape
    u = h @ w_in
    idx = [int(i) for i in np.cumsum(SPLIT_SIZES)[:-1]]
    sq, sk, sv, sg, gq, gk, gv, gg, ga = jnp.split(u, idx, axis=-1)
    sb = tuple(t.reshape(B, L, SB_HEADS, SB_HD) for t in (sq, sk, sv))
    to_bh = lambda t: t.reshape(B, L, GLA_HEADS, -1).transpose(0, 2, 1, 3)
    log_f = jax.nn.log_sigmoid((ga @ w_alpha + b_alpha).astype(jnp.float32)) / GLA_TAU
    gla = (to_bh(gq) * (GLA_DK ** -0.5), to_bh(gk), to_bh(gv), to_bh(log_f))
    return sb, sg, gla, gg


def branch_output(o_sb, sg, o_gla, gg, gla_norm_g, w_out):
    B, L = sg.shape[:2]
    o_gla = rms_norm(o_gla, gla_norm_g)
    o_gla = o_gla.transpose(0, 2, 1, 3).reshape(B, L, GLA_WIDTH)
    o_sb = o_sb.reshape(B, L, SB_WIDTH)
    mixed = jnp.concatenate([o_sb * jax.nn.silu(sg), o_gla * jax.nn.silu(gg)], axis=-1)
    return mixed @ w_out


def stick_break_weights(z, mask):
    sp = jnp.where(mask, jax.nn.softplus(z), 0.0)
    later = lax.cumsum(sp, axis=z.ndim - 1, reverse=True) - sp
    return jnp.where(mask, jnp.exp(jax.nn.log_sigmoid(z) - later), 0.0)


def sb_logits(q, k, bias):
    z = jnp.einsum('bqhd,bkhd->bhqk', q, k).astype(jnp.float32) * (SB_HD ** -0.5)
    return z + bias.astype(jnp.float32)[None, :, None, None]


def sb_attend(qb, qpos, k, v, kpos, bias):
    a = stick_break_weights(sb_logits(qb, k, bias), kpos[None, :] < qpos[:, None])
    return jnp.einsum('bhqk,bkhd->bqhd', a.astype(v.dtype), v)


def sb_prompt(q, k, v, bias):
    B, L = q.shape[:2]
    pos = jnp.arange(L)
    o_meta = sb_attend(q[:, :N_META], pos[:N_META], k[:, :N_META], v[:, :N_META], pos[:N_META], bias)
    nb = (L - N_META) // Q_BLOCK
    qb = q[:, N_META:].reshape(B, nb, Q_BLOCK, SB_HEADS, SB_HD).transpose(1, 0, 2, 3, 4)
    starts = N_META + jnp.arange(nb) * Q_BLOCK
    o_real = lax.map(lambda a: sb_attend(a[0], a[1] + jnp.arange(Q_BLOCK), k, v, pos, bias), (qb, starts))
    o_real = o_real.transpose(1, 0, 2, 3, 4).reshape(B, L - N_META, SB_HEADS, SB_HD)
    return jnp.concatenate([o_meta, o_real], axis=1)


def sb_sample(q, k_new, v_new, k_past, v_past, bias):
    P, T = k_past.shape[1], q.shape[1]
    z = jnp.concatenate([sb_logits(q, k_past, bias), sb_logits(q, k_new, bias)], axis=-1)
    mask = jnp.arange(P + T)[None, :] < (P + jnp.arange(T))[:, None]
    a = stick_break_weights(z, mask)
    return (jnp.einsum('bhqk,bkhd->bqhd', a[..., :P].astype(v_past.dtype), v_past)
            + jnp.einsum('bhqk,bkhd->bqhd', a[..., P:].astype(v_new.dtype), v_new))


def gla_chunk(S, q, k, v, g):
    C = q.shape[2]
    b = jnp.cumsum(g, axis=2)
    causal = jnp.tril(jnp.ones((C, C), dtype=bool))
    diff = b[:, :, :, None, :] - b[:, :, None, :, :]
    decay = jnp.where(causal[None, None, :, :, None], jnp.exp(jnp.minimum(diff, 0.0)), 0.0)
    attn = jnp.einsum('bhtk,bhsk,bhtsk->bhts', q, k, decay)
    o = jnp.einsum('bhts,bhsv->bhtv', attn, v) + jnp.einsum('bhtk,bhkv->bhtv', q * jnp.exp(b), S)
    b_last = b[:, :, -1:, :]
    S_new = (jnp.exp(b_last[:, :, 0, :])[..., None] * S
             + jnp.einsum('bhsk,bhsv->bhkv', k * jnp.exp(b_last - b), v))
    return S_new, o


def gla_prompt(q, k, v, g):
    B, H, L, _ = q.shape
    S0 = jnp.zeros((B, H, GLA_DK, GLA_DV), jnp.float32)
    S1, o_meta = gla_chunk(S0, q[:, :, :N_META], k[:, :, :N_META], v[:, :, :N_META], g[:, :, :N_META])
    nc = (L - N_META) // GLA_CHUNK
    to_chunks = lambda t: t[:, :, N_META:].reshape(B, H, nc, GLA_CHUNK, t.shape[-1]).transpose(2, 0, 1, 3, 4)
    S_end, o_real = lax.scan(lambda S, xs: gla_chunk(S, *xs), S1,
                             (to_chunks(q), to_chunks(k), to_chunks(v), to_chunks(g)))
    o_real = o_real.transpose(1, 2, 0, 3, 4).reshape(B, H, L - N_META, GLA_DV)
    return jnp.concatenate([o_meta, o_real], axis=2), S_end


def setup_inputs(seed: int = 0) -> dict:
    key = jax.random.key(seed)
    ks = jax.random.split(key, 16)
    n_pages = PAST_LEN // PAGE_SIZE
    n_pool = (5 * DEC_BATCH * n_pages) // 4
    nrm = lambda k, s, sc: jax.random.normal(k, s, jnp.float32) * sc
    page_table = jax.random.permutation(ks[0], n_pool)[:DEC_BATCH * n_pages].reshape(DEC_BATCH, n_pages).astype(jnp.int32)
    return {
        'x_prompt': nrm(ks[1], (BATCH, SEQ, D_MODEL), 1.0),
        'x_sample': nrm(ks[2], (DEC_BATCH, DEC_SEQ, D_MODEL), 1.0),
        'cache_k': nrm(ks[3], (DEPTH, n_pool, PAGE_SIZE, SB_HEADS, SB_HD), 1.0),
        'cache_v': nrm(ks[4], (DEPTH, n_pool, PAGE_SIZE, SB_HEADS, SB_HD), 1.0),
        'state_gla': nrm(ks[5], (DEPTH, DEC_BATCH, GLA_HEADS, GLA_DK, GLA_DV), 0.5),
        'page_table': page_table,
        'meta_tokens': nrm(ks[6], (N_META, D_MODEL), 1.0),
        'norm_pre_g': 1.0 + nrm(ks[7], (DEPTH, D_MODEL), 0.02),
        'w_in': nrm(ks[8], (DEPTH, D_MODEL, N_IN), D_MODEL ** -0.5),
        'sb_bias': SB_BIAS_INIT + nrm(ks[14], (DEPTH, SB_HEADS), 0.1),
        'w_alpha': nrm(ks[9], (DEPTH, GLA_RANK, GLA_KEY), GLA_RANK ** -0.5),
        'b_alpha': nrm(ks[10], (DEPTH, GLA_KEY), 0.1),
        'gla_norm_g': 1.0 + nrm(ks[11], (DEPTH, GLA_DV), 0.02),
        'w_out': nrm(ks[12], (DEPTH, MIX, D_MODEL), MIX ** -0.5),
        'norm_post_g': 1.0 + nrm(ks[13], (DEPTH, D_MODEL), 0.02),
    }


def reference(x_prompt, x_sample, cache_k, cache_v, state_gla, page_table, meta_tokens,
              norm_pre_g, w_in, sb_bias, w_alpha, b_alpha, gla_norm_g, w_out, norm_post_g):
    B = x_prompt.shape[0]
    meta = jnp.broadcast_to(meta_tokens[None].astype(x_prompt.dtype), (B, N_META, D_MODEL))
    x = jnp.concatenate([meta, x_prompt], axis=1)
    k_p, v_p, s_p = [], [], []
    for l in range(DEPTH):
        h = rms_norm(x, norm_pre_g[l])
        (q, k, v), sg, (gq, gk, gv, gf), gg = branch_inputs(h, w_in[l], w_alpha[l], b_alpha[l])
        o_sb = sb_prompt(q, k, v, sb_bias[l])
        o_gla, s_end = gla_prompt(gq, gk, gv, gf)
        x = x + rms_norm(branch_output(o_sb, sg, o_gla, gg, gla_norm_g[l], w_out[l]), norm_post_g[l])
        k_p.append(k)
        v_p.append(v)
        s_p.append(s_end)
    y_prompt = x[:, N_META:]

    DB = x_sample.shape[0]
    n_past = page_table.shape[1] * PAGE_SIZE
    xs = x_sample
    k_s, v_s, s_s = [], [], []
    for l in range(DEPTH):
        h = rms_norm(xs, norm_pre_g[l])
        (q, k, v), sg, (gq, gk, gv, gf), gg = branch_inputs(h, w_in[l], w_alpha[l], b_alpha[l])
        k_past = cache_k[l][page_table].reshape(DB, n_past, SB_HEADS, SB_HD)
        v_past = cache_v[l][page_table].reshape(DB, n_past, SB_HEADS, SB_HD)
        o_sb = sb_sample(q, k, v, k_past, v_past, sb_bias[l])
        s_new, o_gla = gla_chunk(state_gla[l], gq, gk, gv, gf)
        xs = xs + rms_norm(branch_output(o_sb, sg, o_gla, gg, gla_norm_g[l], w_out[l]), norm_post_g[l])
        k_s.append(k)
        v_s.append(v)
        s_s.append(s_new)
    y_sample = xs
    return (y_prompt, y_sample, jnp.stack(k_p), jnp.stack(v_p), jnp.stack(s_p),
            jnp.stack(k_s), jnp.stack(v_s), jnp.stack(s_s))
```

```python
import contextlib
import numpy as np
import concourse.bass as bass
import concourse.mybir as mybir
from concourse.bass_utils import run_bass_kernel_spmd

F32 = mybir.dt.float32
BF16 = mybir.dt.bfloat16
I32 = mybir.dt.int32
U8 = mybir.dt.uint8
AF = mybir.ActivationFunctionType
ALU = mybir.AluOpType

D = 2048
KC = 16
NMETA = 16
SEQ = 2048
LP = NMETA + SEQ
NS = 64
NT = LP + NS
NIN = 7184
NPOOL_ROWS = 2560 * 128
EPS = 1e-6
NEG = -30000.0
SCALE = 128 ** -0.5
TILES = [(0, 16)] + [(16 + 128 * i, 128) for i in range(16)] + [(LP, NS)]
TCH = [(0, 512), (512, 512), (1024, 512), (1536, 512), (2048, NT - 2048)]
ENGS = ["sync", "scalar", "gpsimd", "vector", "tensor"]

C_ID, C_MA, C_MS, C_TU, C_TL, C_CA, C_TUS, C_TLS, C_CAS, C_MB, C_HM, C_RS = (
    0, 128, 256, 512, 640, 768, 896, 960, 1024, 1088, 1104, 1112)
NCST = 1128
STAGE = 99
SUB = 'abcde'
NH_SB = 8


class Op:
    __slots__ = ("eng", "fn", "deps", "needed", "count", "dma", "dtok")

    def __init__(self, eng, fn):
        self.eng, self.fn = eng, fn
        self.deps, self.needed, self.count, self.dma, self.dtok = [], False, None, None, None


class DSlot:
    def __init__(self, name):
        self.name, self.count, self.sem = name, 0, None


class Prog:
    def __init__(self):
        self.q = {e: [] for e in ENGS}
        self.lastw, self.readers, self.slots = {}, {}, []
        self.nsig = {e: 0 for e in ENGS}
        self.dma_ops = []

    def slot(self, name):
        s = DSlot(name)
        self.slots.append(s)
        return s

    def _adddep(self, op, d):
        if d is op:
            return
        if d.dma is None and op.dma is None and d.eng == "tensor" and op.eng == "tensor":
            return
        op.deps.append((d, d.dma.count if d.dma is not None else None))
        if d.dma is None:
            d.needed = True

    def _track(self, op, reads, writes):
        writes = list(writes) + [r for r in reads if r.startswith("pb")]
        reads = [r for r in reads if not r.startswith("pb")]
        deps = []
        for r in reads:
            w = self.lastw.get(r)
            if w is not None:
                deps.append(w)
        for w_ in writes:
            w = self.lastw.get(w_)
            if w is not None:
                deps.append(w)
            deps.extend(self.readers.get(w_, ()))
        seen = set()
        for d in deps:
            if id(d) in seen:
                continue
            seen.add(id(d))
            self._adddep(op, d)
        for r in reads:
            self.readers.setdefault(r, []).append(op)
        for w_ in writes:
            self.lastw[w_] = op
            self.readers[w_] = []

    def op(self, eng, fn, reads=(), writes=()):
        o = Op(eng, fn)
        self._track(o, reads, writes)
        self.q[eng].append(o)
        return o

    def dma(self, eng, fn, slot, reads=(), writes=()):
        o = Op(eng, fn)
        o.dma = slot
        self._track(o, reads, writes)
        slot.count += 16
        o.dtok = slot.count
        self.q[eng].append(o)
        self.dma_ops.append(o)
        return o

    def barrier(self):
        lasts = []
        for e in ENGS:
            for o in reversed(self.q[e]):
                if o.dma is None and o.fn is not None:
                    lasts.append(o)
                    break
        dmas = list(self.dma_ops)
        self.dma_ops = []
        for e in ENGS:
            b = Op(e, None)
            for d in lasts:
                if d.eng != e:
                    b.deps.append((d, None))
                    d.needed = True
            for d in dmas:
                b.deps.append((d, d.dma.count))
            self.q[e].append(b)
        self.lastw, self.readers = {}, {}

    def emit(self, nc):
        for e in ENGS:
            c = 0
            for o in self.q[e]:
                if o.dma is None and o.needed:
                    c += 1
                    o.count = c
            self.nsig[e] = c
        with contextlib.ExitStack() as st:
            esem = {e: st.enter_context(nc.semaphore("es_" + e)) for e in ENGS}
            for s in self.slots:
                s.sem = st.enter_context(nc.semaphore("ds_" + s.name))
            block = st.enter_context(nc.Block())
            prog = self

            def make(ename):
                def body(eng):
                    seen = {}
                    for o in prog.q[ename]:
                        need = {}
                        for d, dval in o.deps:
                            if d.dma is not None:
                                key, val, sem = ("d", id(d.dma)), dval, d.dma.sem
                            else:
                                key, val, sem = ("e", d.eng), d.count, esem[d.eng]
                            if need.get(key, (None, -1))[1] < val:
                                need[key] = (sem, val)
                        for key, (sem, val) in need.items():
                            if seen.get(key, -1) >= val:
                                continue
                            seen[key] = val
                            eng.wait_ge(sem, val)
                        if o.fn is None:
                            continue
                        ins = o.fn(eng)
                        if o.dma is not None:
                            ins.then_inc(o.dma.sem, 16)
                        elif o.needed:
                            ins.then_inc(esem[ename], 1)
                    if ename == "sync":
                        for s in prog.slots:
                            if s.count > 0:
                                eng.wait_ge(s.sem, s.count)
                        for e2 in ENGS:
                            if prog.nsig[e2] > 0:
                                eng.wait_ge(esem[e2], prog.nsig[e2])
                return body

            block.sync(make("sync"))
            block.scalar(make("scalar"))
            block.gpsimd(make("gpsimd"))
            block.vector(make("vector"))
            block.tensor(make("tensor"))


class Arena:
    def __init__(self, t, nbytes):
        self.t, self.n, self.off = t, nbytes, 0

    def reset(self):
        self.off = 0

    def get(self, nbytes, dt, shape=None):
        o = (self.off + 31) // 32 * 32
        assert o + nbytes <= self.n, ("arena overflow", o + nbytes, self.n)
        self.off = o + nbytes
        v = self.t[:, o:o + nbytes].bitcast(dt)
        if shape is not None:
            names = " ".join("d%d" % i for i in range(len(shape)))
            v = v.rearrange("p (%s) -> p %s" % (names, names), **{"d%d" % i: s for i, s in enumerate(shape)})
        return v


def build_program():
    nc = bass.Bass("TRN2", target_bir_lowering=False)
    dram = lambda n, s, d, k: nc.dram_tensor(n, s, d, kind=k).ap()
    x_all = dram("x_all", [NT, D], F32, "ExternalInput")
    ck = dram("ck", [NPOOL_ROWS, 1024], F32, "ExternalInput")
    cv = dram("cv", [NPOOL_ROWS, 1024], F32, "ExternalInput")
    pt = dram("pt", [1, 256], I32, "ExternalInput")
    state = dram("state", [16, 4, 128, 256], F32, "ExternalInput")
    w_in = dram("w_in", [D, NIN], F32, "ExternalInput")
    w_out = dram("w_out", [D, D], F32, "ExternalInput")
    gpre_d = dram("gpre", [128, KC], F32, "ExternalInput")
    gpost_d = dram("gpost", [D], F32, "ExternalInput")
    sbias_d = dram("sbias", [8], F32, "ExternalInput")
    sbrow_d = dram("sbrow", [128, 1], F32, "ExternalInput")
    walpha_d = dram("walpha", [17, 512], F32, "ExternalInput")
    gnorm_d = dram("gnorm", [128, 2], F32, "ExternalInput")
    cst_d = dram("cst", [128, NCST], F32, "ExternalInput")
    y_p = dram("y_p", [SEQ, D], F32, "ExternalOutput")
    y_s = dram("y_s", [NS, D], F32, "ExternalOutput")
    nk_p = dram("nk_p", [LP, 1024], F32, "ExternalOutput")
    nv_p = dram("nv_p", [LP, 1024], F32, "ExternalOutput")
    ns_p = dram("ns_p", [4, 128, 256], F32, "ExternalOutput")
    nk_s = dram("nk_s", [NS, 1024], F32, "ExternalOutput")
    nv_s = dram("nv_s", [NS, 1024], F32, "ExternalOutput")
    ns_s = dram("ns_s", [16, 4, 128, 256], F32, "ExternalOutput")
    w_in_v = w_in.rearrange("(k p) n -> p k n", p=128)
    w_out_v = w_out.rearrange("(k p) n -> p k n", p=128)

    RC_BYTES = 59392
    dbg = dram("dbg", [128, NT], F32, "ExternalOutput")
    dbg2 = dram("dbg2", [128, 16, NT - NMETA], F32, "ExternalOutput") if STAGE == 77 else None
    with contextlib.ExitStack() as st:
        sb = lambda n, s, d: st.enter_context(nc.sbuf_tensor(n, s, d))
        RA = sb("RA", [128, KC * NT], BF16)
        hT = RA[:, :].rearrange("p (k t) -> p k t", k=KC)
        wout = RA[:, 0:KC * D].rearrange("p (k n) -> p k n", k=KC)
        RB = sb("RB", [128, KC * (NT - NMETA)], BF16)
        mixT = RB[:, :].rearrange("p (k t) -> p k t", k=KC)
        NMX = NT - NMETA
        cst = sb("cst_t", [128, NCST], F32)
        identb = sb("identb", [128, 128], BF16)
        maskab = sb("maskab", [128, 128], BF16)
        maskSb = sb("maskSb", [128, 256], BF16)
        ones_f = sb("ones_f", [128, 512], F32)
        gpre = sb("gpre_t", [128, KC], F32)
        sbias = sb("sbias_t", [128, 8], F32)
        sbrow = sb("sbrow_t", [128, 1], F32)
        gnorm = sb("gnorm_t", [128, 2], F32)
        stat = sb("stat", [128, 16], F32)
        bh = sb("bh", [128, 1], F32)
        sqT_s = sb("sqT_s", [128, 8, NS], BF16)
        skT_s = sb("skT_s", [128, 8, NS], BF16)
        sv_s = sb("sv_s", [NS, 8, 128], BF16)
        sgate_s = sb("sgate_s", [128, 8, NS], BF16)
        arena_t = sb("arena", [128, RC_BYTES], U8)
        AR = Arena(arena_t, RC_BYTES)
        pb = [st.enter_context(nc.psum_tensor("pb%d" % i, [128, 512], F32)) for i in range(8)]
        pbh = [p[:, :].bitcast(BF16) for p in pb]

        P = Prog()
        ident_f = cst[:, C_ID:C_ID + 128]

        s_c = P.slot("cst")
        P.dma("sync", lambda e: e.dma_start(out=cst[:, :], in_=cst_d), s_c, writes=["cst"])
        P.dma("sync", lambda e: e.dma_start(out=gpre[:, :], in_=gpre_d), s_c, writes=["gpre"])
        P.dma("sync", lambda e: e.dma_start(out=sbias[:, :], in_=sbias_d.partition_broadcast(128)), s_c, writes=["sbias"])
        P.dma("sync", lambda e: e.dma_start(out=sbrow[:, :], in_=sbrow_d), s_c, writes=["sbrow"])
        P.dma("sync", lambda e: e.dma_start(out=gnorm[:, :], in_=gnorm_d), s_c, writes=["gnorm"])
        P.op("vector", lambda e: e.tensor_copy(identb[:, :], cst[:, C_ID:C_ID + 128]), reads=["cst"], writes=["identb"])
        P.op("vector", lambda e: e.tensor_copy(maskab[:, :], cst[:, C_MA:C_MA + 128]), reads=["cst"], writes=["maskab"])
        P.op("vector", lambda e: e.tensor_copy(maskSb[:, :], cst[:, C_MS:C_MS + 256]), reads=["cst"], writes=["maskSb"])
        P.op("gpsimd", lambda e: e.memset(ones_f[:, :], 1.0), writes=["ones_f"])

        AR.reset()
        xt = [AR.get(D * 4, F32) for _ in range(2)]
        xn = [AR.get(D * 2, BF16) for _ in range(2)]
        s_x = [P.slot("x0"), P.slot("x1")]
        for t, (t0, n) in enumerate(TILES):
            j = t % 2
            P.dma("sync", lambda e, j=j, t0=t0, n=n: e.dma_start(out=xt[j][:n, :], in_=x_all[t0:t0 + n, :]), s_x[j],
                  writes=["xt%d" % j])
            P.op("scalar", lambda e, j=j, n=n: e.activation(out=xn[j][:n, :], in_=xt[j][:n, :], func=AF.Square,
                                                           accum_out=stat[:n, 0:1]),
                 reads=["xt%d" % j], writes=["xn%d" % j, "st0"])
            P.op("scalar", lambda e, n=n: e.activation(out=stat[:n, 1:2], in_=stat[:n, 0:1], func=AF.Ln, bias=EPS, scale=1.0 / D),
                 reads=["st0"], writes=["st1"])
            P.op("scalar", lambda e, n=n: e.activation(out=stat[:n, 2:3], in_=stat[:n, 1:2], func=AF.Exp, scale=-0.5),
                 reads=["st1"], writes=["st2"])
            P.op("vector", lambda e, j=j, n=n: e.tensor_scalar(xn[j][:n, :], xt[j][:n, :], stat[:n, 2:3], None, ALU.mult),
                 reads=["xt%d" % j, "st2"], writes=["xn%d" % j])
            for half in range(2):
                bk = 2 * j + half
                for kk in range(8):
                    k = 8 * half + kk
                    P.op("tensor", lambda e, bk=bk, kk=kk, k=k, j=j, n=n: e.transpose(
                        pbh[bk][:, kk * 128:kk * 128 + n], xn[j][:n, k * 128:(k + 1) * 128], identb[:n, :n]),
                        reads=["xn%d" % j, "identb"], writes=["pb%d" % bk])
                P.op("vector", lambda e, bk=bk, half=half, t0=t0, n=n: e.tensor_tensor(
                    hT[:, 8 * half:8 * half + 8, t0:t0 + n],
                    pbh[bk].rearrange("p (a b) -> p a b", a=8)[:, :, 0:n],
                    gpre[:, 8 * half:8 * half + 8].unsqueeze(2).to_broadcast([128, 8, n]), ALU.mult),
                    reads=["pb%d" % bk, "gpre"], writes=["hT"])
        P.barrier()
        s_dbg = P.slot("dbg")
        P.dma("gpsimd", lambda e: e.dma_start(out=dbg, in_=hT[:, 3, :]), s_dbg, reads=["hT"])

        s_w = [P.slot("w%d" % i) for i in range(3)]
        wctr = [0]
        ipb = [0]

        def load_w(wring, c0, ncols):
            i = wctr[0] % 3
            wctr[0] += 1
            P.dma("gpsimd", lambda e, i=i, c0=c0, ncols=ncols: e.dma_start(out=wring[i][:, :, 0:ncols], in_=w_in_v[:, :, c0:c0 + ncols]),
                  s_w[i], writes=["wr%d" % i])
            return i

        def fm(wring, wi, ncols, evac):
            for (c0, m) in TCH:
                bk = ipb[0] % 2
                ipb[0] += 1
                for k in range(KC):
                    P.op("tensor", lambda e, bk=bk, k=k, c0=c0, m=m: e.matmul(
                        pb[bk][:ncols, :m], lhsT=wring[wi][:, k, 0:ncols], rhs=hT[:, k, c0:c0 + m], start=(k == 0), stop=(k == KC - 1)),
                        reads=["wr%d" % wi, "hT"], writes=["pb%d" % bk])
                evac(pb[bk][:ncols, :m], c0, m, "pb%d" % bk)

        def tm(wring, wi, ncols, evac):
            for t, (t0, n) in enumerate(TILES):
                bk = ipb[0] % 2
                ipb[0] += 1
                for k in range(KC):
                    P.op("tensor", lambda e, bk=bk, k=k, t0=t0, n=n: e.matmul(
                        pb[bk][:n, :ncols], lhsT=hT[:, k, t0:t0 + n], rhs=wring[wi][:, k, 0:ncols], start=(k == 0), stop=(k == KC - 1)),
                        reads=["wr%d" % wi, "hT"], writes=["pb%d" % bk])
                evac(pb[bk][:n, :ncols], t, "pb%d" % bk)

        def sb_chain(T_, n, K, zfill, bias_ap, bias_reg):
            nch = (K + 511) // 512
            for c in range(nch):
                c0 = c * 512
                m = min(512, K - c0)
                bk = 2 + (T_["zctr"][0] % 4)
                T_["zctr"][0] += 1
                bn = "pb%d" % bk
                zfill(c, c0, m, pb[bk], bn)
                e_ = T_["esp"][c % 2]
                en = "esp%d" % (c % 2)
                pc = T_["pc"][c % 2]
                pn = "pc%d" % (c % 2)
                P.op("scalar", lambda e, e_=e_, bk=bk, m=m: e.activation(out=e_[:n, :m], in_=pb[bk][:n, :m], func=AF.Exp,
                                                                       bias=bias_ap, scale=SCALE),
                     reads=[bn, bias_reg], writes=[en])
                P.op("scalar", lambda e, e_=e_, m=m: e.activation(out=e_[:n, :m], in_=e_[:n, :m], func=AF.Identity, bias=1.0, scale=1.0),
                     reads=[en], writes=[en])
                if c == 0:
                    P.op("vector", lambda e, pc=pc: e.memset(pc[:n, 0:1], 1.0), writes=[pn])
                else:
                    pp = T_["pc"][(c - 1) % 2]
                    P.op("vector", lambda e, pc=pc, pp=pp: e.tensor_copy(pc[:n, 0:1], pp[:n, 512:513]),
                         reads=["pc%d" % ((c - 1) % 2)], writes=[pn])
                P.op("vector", lambda e, pc=pc, e_=e_, m=m: e.tensor_tensor_scan(
                    pc[:n, 1:m + 1], e_[:n, :m], ones_f[:n, :m], pc[:n, 0:1], ALU.mult, ALU.mult),
                    reads=[pn, en, "ones_f"], writes=[pn])
                P.op("vector", lambda e, pc=pc, c0=c0, m=m: e.tensor_tensor(
                    T_["w"][:n, c0:c0 + m], pc[:n, 1:m + 1], pc[:n, 0:m], ALU.subtract),
                    reads=[pn], writes=["wrow"])
            ml = K - (nch - 1) * 512
            pl = T_["pc"][(nch - 1) % 2]
            P.op("vector", lambda e: e.reciprocal(stat[:n, 4:5], pl[:n, ml:ml + 1]),
                 reads=["pc%d" % ((nch - 1) % 2)], writes=["st4"])
            P.op("vector", lambda e: e.tensor_scalar(T_["a"][:n, :K], T_["w"][:n, :K], stat[:n, 4:5], None, ALU.mult),
                 reads=["wrow", "st4"], writes=["arow"])

        def chain_temps():
            return {"esp": [AR.get(512 * 4, F32) for _ in range(2)],
                    "pc": [AR.get(513 * 4, F32) for _ in range(2)],
                    "w": AR.get(2112 * 4, F32), "a": AR.get(2112 * 2, BF16),
                    "aT": AR.get(17 * 128 * 2, BF16, [17, 128]), "zctr": [0]}

        def transpose_blocks(T_, n, blocks):
            for g0 in range(0, len(blocks), 8):
                grp = blocks[g0:g0 + 8]
                bk = 6 + ((g0 // 8) % 2)
                for i, (col0, m) in enumerate(grp):
                    P.op("tensor", lambda e, bk=bk, i=i, col0=col0, m=m: e.transpose(
                        pbh[bk][:m, i * 128:i * 128 + n], T_["a"][:n, col0:col0 + m], identb[:n, :n]),
                        reads=["arow", "identb"], writes=["pb%d" % bk])
                ng = len(grp)
                P.op("vector", lambda e, bk=bk, g0=g0, ng=ng: e.tensor_copy(
                    T_["aT"][:, g0:g0 + ng, :n], pbh[bk].rearrange("p (a b) -> p a b", a=8)[:, 0:ng, 0:n]),
                    reads=["pb%d" % bk], writes=["aT%d" % (g0 // 8)])

        def gate_chunk(ap, c0, m, bn, kidx, gate, gate2, sg_save=None):
            lo = max(c0, NMETA)
            mm = c0 + m - lo
            src = ap[:, lo - c0:lo - c0 + mm]
            P.op("scalar", lambda e: e.activation(out=gate[:, :mm], in_=src, func=AF.Exp, scale=-1.0), reads=[bn], writes=["gate"])
            P.op("vector", lambda e: e.tensor_scalar(gate[:, :mm], gate[:, :mm], 1.0, None, ALU.add), reads=["gate"], writes=["gate"])
            P.op("vector", lambda e: e.reciprocal(gate[:, :mm], gate[:, :mm]), reads=["gate"], writes=["gate"])
            P.op("vector", lambda e: e.tensor_tensor(gate2[:, :mm], src, gate[:, :mm], ALU.mult), reads=[bn, "gate"], writes=["gate2"])
            ma = mm
            if sg_save is not None and c0 + m > LP:
                ma = LP - lo
                P.op("vector", lambda e: e.tensor_copy(sgate_s[:, sg_save, :], gate2[:, ma:ma + NS]), reads=["gate2"], writes=["sgate_s"])
            P.op("vector", lambda e: e.tensor_tensor(mixT[:, kidx, lo - NMETA:lo - NMETA + ma], mixT[:, kidx, lo - NMETA:lo - NMETA + ma],
                                                     gate2[:, :ma], ALU.mult), reads=["gate2", "mixT%d" % kidx], writes=["mixT%d" % kidx])

        AR.reset()
        wring = [AR.get(KC * 128 * 2, BF16, [KC, 128]) for _ in range(3)]
        sqT = AR.get(NT * 2, BF16)
        skT = AR.get(NT * 2, BF16)
        svh = AR.get(18 * 128 * 2, BF16, [18, 128])
        kvst = [AR.get(128 * 4, F32) for _ in range(2)]
        gate = AR.get(512 * 4, F32)
        gate2 = AR.get(512 * 4, F32)
        CT = chain_temps()
        s_kv = [P.slot("kv0"), P.slot("kv1")]
        kvc = [0]

        for h in range(NH_SB if STAGE >= 2 else 0):
            wq = load_w(wring, h * 128, 128)
            wk = load_w(wring, 1024 + h * 128, 128)
            wv = load_w(wring, 2048 + h * 128, 128)

            def ev_q(ap, c0, m, bn, h=h):
                P.op("scalar", lambda e: e.copy(sqT[:, c0:c0 + m], ap), reads=[bn], writes=["sqT"])
                if c0 == 2048:
                    P.op("vector", lambda e: e.tensor_copy(sqT_s[:, h, :], ap[:, LP - 2048:LP - 2048 + NS]), reads=[bn], writes=["sqT_s"])

            def ev_k(ap, c0, m, bn, h=h):
                P.op("scalar", lambda e: e.copy(skT[:, c0:c0 + m], ap), reads=[bn], writes=["skT"])
                if c0 == 2048:
                    P.op("vector", lambda e: e.tensor_copy(skT_s[:, h, :], ap[:, LP - 2048:LP - 2048 + NS]), reads=[bn], writes=["skT_s"])

            if 'b' in SUB:
                fm(wring, wq, 128, ev_q)
            if 'c' in SUB:
                fm(wring, wk, 128, ev_k)

            def ev_ktm(ap, t, bn, h=h):
                t0, n = TILES[t]
                j = kvc[0] % 2
                kvc[0] += 1
                P.op("scalar", lambda e: e.copy(kvst[j][:n, :], ap), reads=[bn], writes=["kvst%d" % j])
                dst = nk_p[t0:t0 + n, h * 128:(h + 1) * 128] if t < 17 else nk_s[:, h * 128:(h + 1) * 128]
                P.dma("sync", lambda e: e.dma_start(out=dst, in_=kvst[j][:n, :]), s_kv[j], reads=["kvst%d" % j])

            def ev_vtm(ap, t, bn, h=h):
                t0, n = TILES[t]
                j = kvc[0] % 2
                kvc[0] += 1
                P.op("scalar", lambda e: e.copy(kvst[j][:n, :], ap), reads=[bn], writes=["kvst%d" % j])
                dst = nv_p[t0:t0 + n, h * 128:(h + 1) * 128] if t < 17 else nv_s[:, h * 128:(h + 1) * 128]
                P.dma("sync", lambda e: e.dma_start(out=dst, in_=kvst[j][:n, :]), s_kv[j], reads=["kvst%d" % j])
                P.op("vector", lambda e: e.tensor_copy(svh[:n, t, :], ap), reads=[bn], writes=["svh"])
                if t == 17:
                    P.op("vector", lambda e: e.tensor_copy(sv_s[:, h, :], ap), reads=[bn], writes=["sv_s"])

            if 'd' in SUB:
                tm(wring, wk, 128, ev_ktm)
            if 'e' in SUB:
                tm(wring, wv, 128, ev_vtm)

            P.op("vector", lambda e, h=h: e.tensor_copy(bh[:, :], sbias[:, h:h + 1]), reads=["sbias"], writes=["bh"])
            for i in range(17 if STAGE >= 3 else 0):
                t0, n = TILES[i]
                K = t0 + n
                d0 = t0

                def zfill(c, c0, m, bank, bn, t0=t0, n=n, K=K, d0=d0):
                    lo, hi = max(c0, d0), min(c0 + m, K)
                    has_mask = lo < hi
                    P.op("tensor", lambda e: e.matmul(bank[:n, :m], lhsT=sqT[:, t0:t0 + n], rhs=skT[:, c0:c0 + m],
                                                      start=True, stop=not has_mask),
                         reads=["sqT", "skT"], writes=[bn])
                    if has_mask:
                        P.op("tensor", lambda e: e.matmul(bank[:n, lo - c0:hi - c0], lhsT=identb[:n, :n], rhs=maskab[:n, lo - d0:hi - d0],
                                                          start=False, stop=True),
                             reads=["identb", "maskab"], writes=[bn])

                sb_chain(CT, n, K, zfill, bh[:n, 0:1], "bh")
                blocks = [(0, 16)] + [(16 + 128 * b, 128) for b in range(i)]
                transpose_blocks(CT, n, blocks)
                ob = i % 2
                for kb, (col0, m) in enumerate(blocks):
                    P.op("tensor", lambda e, kb=kb, m=m, ob=ob, n=n: e.matmul(
                        pb[ob][:, :n], lhsT=svh[:m, kb, :], rhs=CT["aT"][:m, kb, :n], start=(kb == 0), stop=(kb == len(blocks) - 1)),
                        reads=["svh", "aT%d" % (kb // 8)], writes=["pb%d" % ob])
                if i >= 1:
                    P.op("scalar", lambda e, ob=ob, t0=t0, n=n, h=h: e.copy(mixT[:, h, t0 - NMETA:t0 - NMETA + n], pb[ob][:, :n]),
                         reads=["pb%d" % ob], writes=["mixT%d" % h])

            wg = load_w(wring, 3072 + h * 128, 128)

            def ev_g(ap, c0, m, bn, h=h):
                gate_chunk(ap, c0, m, bn, h, gate, gate2, sg_save=h)

            if STAGE >= 4:
                fm(wring, wg, 128, ev_g)
        P.barrier()
        if STAGE >= 3:
            P.dma("gpsimd", lambda e: e.dma_start(out=dbg[:, 0:NMX], in_=mixT[:, 0, :]), s_dbg, reads=["mixT0"])
            P.barrier()

        AR.reset()
        wring = [AR.get(KC * 128 * 2, BF16, [KC, 128]) for _ in range(3)]
        gqT = AR.get(NT * 2, BF16)
        gkT = AR.get(NT * 2, BF16)
        gk_tm = AR.get(18 * 128 * 2, BF16, [18, 128])
        gv_tm = AR.get(18 * 256 * 2, BF16, [18, 256])
        gaT = AR.get(NT * 2, BF16)
        walb = AR.get(512 * 2, BF16)
        Sf = AR.get(256 * 4, F32)
        Sb = AR.get(256 * 2, BF16)
        lsp = AR.get(128 * 4, F32)
        eb = AR.get(128 * 4, F32)
        enb = AR.get(128 * 4, F32)
        rex = AR.get(128 * 4, F32)
        qt = AR.get(128 * 2, BF16)
        kt = AR.get(128 * 2, BF16)
        attn = AR.get(128 * 2, BF16)
        khat = AR.get(128 * 2, BF16)
        osq = AR.get(2 * 128 * 4, F32, [2, 128])
        rstd = AR.get(128 * 4, F32)
        gate = AR.get(512 * 4, F32)
        gate2 = AR.get(512 * 4, F32)
        S0f = [AR.get(256 * 4, F32) for _ in range(2)]
        S0b = [AR.get(256 * 2, BF16) for _ in range(2)]
        snew = [AR.get(256 * 4, F32) for _ in range(2)]
        khm = AR.get(16 * 128 * 2, BF16, [16, 128])
        s_wa = P.slot("wa")
        s_s0 = [P.slot("s0a"), P.slot("s0b")]
        s_s0b = [P.slot("s0ba"), P.slot("s0bb")]
        s_sn = [P.slot("sna"), P.slot("snb")]
        s_nsp = P.slot("nsp")
        s0c = [0]
        NG = 4 if STAGE >= 6 else 0
        if NG:
            P.dma("gpsimd", lambda e: e.dma_start(out=walb[0:17, :], in_=walpha_d), s_wa, writes=["walb"])
            P.op("gpsimd", lambda e: e.memset(gaT[0:32, :], 1.0), writes=["gaT"])
            wa = load_w(wring, 7168, 16)
            fm(wring, wa, 16, lambda ap, c0, m, bn: P.op("scalar", lambda e: e.copy(gaT[0:16, c0:c0 + m], ap), reads=[bn], writes=["gaT"]))
        for g in range(NG if NH_SB == 8 else min(NG, 1)):
            wq = load_w(wring, 4096 + g * 128, 128)
            wk = load_w(wring, 4608 + g * 128, 128)
            fm(wring, wq, 128, lambda ap, c0, m, bn: P.op("scalar", lambda e: e.copy(gqT[:, c0:c0 + m], ap), reads=[bn], writes=["gqT"]))
            fm(wring, wk, 128, lambda ap, c0, m, bn: P.op("scalar", lambda e: e.copy(gkT[:, c0:c0 + m], ap), reads=[bn], writes=["gkT"]))
            tm(wring, wk, 128, lambda ap, t, bn: P.op("scalar", lambda e: e.copy(gk_tm[:TILES[t][1], t, :], ap), reads=[bn], writes=["gk_tm"]))
            for jj in range(2):
                wv = load_w(wring, 5120 + g * 256 + jj * 128, 128)
                tm(wring, wv, 128, lambda ap, t, bn, jj=jj: P.op("scalar", lambda e: e.copy(gv_tm[:TILES[t][1], t, jj * 128:(jj + 1) * 128], ap),
                                                              reads=[bn], writes=["gv_tm"]))
            P.op("vector", lambda e: e.memset(Sf[:, :], 0.0), writes=["Sf"])
            P.op("vector", lambda e: e.memset(Sb[:, :], 0.0), writes=["Sb"])
            for t, (t0, n) in enumerate(TILES):
                samp = (t == 17)
                cTU, cTL, cCA = (C_TUS, C_TLS, C_CAS) if samp else (C_TU, C_TL, C_CA)
                P.op("tensor", lambda e, t0=t0, n=n, g=g: e.matmul(pb[3][:n, 128:256], lhsT=gaT[0:17, t0:t0 + n], rhs=walb[0:17, g * 128:(g + 1) * 128],
                                                                  start=True, stop=True), reads=["gaT", "walb"], writes=["pb3"])
                P.op("scalar", lambda e, n=n: e.activation(out=lsp[:n, :], in_=pb[3][:n, 128:256], func=AF.Exp, scale=-1.0), reads=["pb3"], writes=["lsp"])
                P.op("scalar", lambda e, n=n: e.activation(out=lsp[:n, :], in_=lsp[:n, :], func=AF.Ln, bias=1.0, scale=1.0), reads=["lsp"], writes=["lsp"])
                P.op("tensor", lambda e, n=n, cTU=cTU: e.matmul(pb[2][:, 0:n], lhsT=lsp[:n, :], rhs=cst[:n, cTU:cTU + n], start=True, stop=True),
                     reads=["lsp", "cst"], writes=["pb2"])
                P.op("tensor", lambda e, n=n, cTL=cTL: e.matmul(pb[2][:n, 128:256], lhsT=cst[:n, cTL:cTL + n], rhs=lsp[:n, :], start=True, stop=True),
                     reads=["lsp", "cst"], writes=["pb2"])
                P.op("scalar", lambda e, n=n: e.activation(out=eb[:, :n], in_=pb[2][:, 0:n], func=AF.Exp), reads=["pb2"], writes=["eb"])
                P.op("scalar", lambda e, n=n: e.activation(out=enb[:, :n], in_=pb[2][:, 0:n], func=AF.Exp, scale=-1.0), reads=["pb2"], writes=["enb"])
                P.op("scalar", lambda e, n=n: e.activation(out=rex[:n, :], in_=pb[2][:n, 128:256], func=AF.Exp), reads=["pb2"], writes=["rex"])
                P.op("vector", lambda e, t0=t0, n=n: e.scalar_tensor_tensor(qt[:, :n], gqT[:, t0:t0 + n], SCALE, eb[:, :n], ALU.mult, ALU.mult),
                     reads=["gqT", "eb"], writes=["qt"])
                P.op("vector", lambda e, t0=t0, n=n: e.tensor_tensor(kt[:, :n], gkT[:, t0:t0 + n], enb[:, :n], ALU.mult), reads=["gkT", "enb"], writes=["kt"])
                P.op("vector", lambda e, t=t, n=n: e.tensor_tensor(khat[:n, :], gk_tm[:n, t, :], rex[:n, :], ALU.mult), reads=["gk_tm", "rex"], writes=["khat"])
                P.op("tensor", lambda e, n=n: e.matmul(pb[3][:n, 0:n], lhsT=kt[:, :n], rhs=qt[:, :n], start=True, stop=True), reads=["kt", "qt"], writes=["pb3"])
                P.op("vector", lambda e, n=n, cCA=cCA: e.tensor_tensor(attn[:n, :n], pb[3][:n, 0:n], cst[:n, cCA:cCA + n], ALU.mult),
                     reads=["pb3", "cst"], writes=["attn"])
                for j in range(2):
                    P.op("tensor", lambda e, j=j, t=t, n=n: e.matmul(pb[4 + j][:, 0:n], lhsT=gv_tm[:n, t, j * 128:(j + 1) * 128], rhs=attn[:n, :n],
                                                                    start=True, stop=False), reads=["gv_tm", "attn"], writes=["pb%d" % (4 + j)])
                if not samp:
                    chunks = [(0, 16)] if n == 16 else [(0, 64), (64, 128)]
                    for ci, (a_, b_) in enumerate(chunks):
                        last = ci == len(chunks) - 1
                        for j in range(2):
                            P.op("tensor", lambda e, j=j, a_=a_, b_=b_, last=last: e.matmul(
                                pb[4 + j][:, a_:b_], lhsT=Sb[:, j * 128:(j + 1) * 128], rhs=qt[:, a_:b_], start=False, stop=last),
                                reads=["Sb", "qt"], writes=["pb%d" % (4 + j)])
                        P.op("tensor", lambda e, a_=a_, b_=b_, t=t: e.matmul(pb[6][:, 0:256], lhsT=khat[a_:b_, :], rhs=gv_tm[a_:b_, t, :], start=True, stop=True),
                             reads=["khat", "gv_tm"], writes=["pb6"])
                        P.op("vector", lambda e, b_=b_: e.scalar_tensor_tensor(Sf[:, :], Sf[:, :], eb[:, b_ - 1:b_], pb[6][:, 0:256], ALU.mult, ALU.add),
                             reads=["Sf", "eb", "pb6"], writes=["Sf"])
                        P.op("scalar", lambda e: e.copy(Sb[:, :], Sf[:, :]), reads=["Sf"], writes=["Sb"])
                    if t == 16:
                        P.dma("sync", lambda e, g=g: e.dma_start(out=ns_p[g], in_=Sf[:, :]), s_nsp, reads=["Sf"])
                else:
                    for b in range(16):
                        jb = s0c[0] % 2
                        s0c[0] += 1
                        P.dma("sync", lambda e, jb=jb, b=b, g=g: e.dma_start(out=S0f[jb][:, :], in_=state[b, g]), s_s0[jb], writes=["S0f%d" % jb])
                        P.dma("gpsimd", lambda e, jb=jb, b=b, g=g: e.dma_start(out=S0b[jb][:, :], in_=state[b, g]), s_s0b[jb], writes=["S0b%d" % jb])
                        for j in range(2):
                            P.op("tensor", lambda e, j=j, jb=jb, b=b: e.matmul(
                                pb[4 + j][:, 4 * b:4 * b + 4], lhsT=S0b[jb][:, j * 128:(j + 1) * 128], rhs=qt[:, 4 * b:4 * b + 4], start=False, stop=(b == 15)),
                                reads=["S0b%d" % jb, "qt"], writes=["pb%d" % (4 + j)])
                        if b == 0:
                            P.op("vector", lambda e: e.tensor_tensor(khm[:NS, :, :], khat[:NS, :].unsqueeze(1).to_broadcast([NS, 16, 128]),
                                                                     cst[:NS, C_MB:C_MB + 16].unsqueeze(2).to_broadcast([NS, 16, 128]), ALU.mult),
                                 reads=["khat", "cst"], writes=["khm"])
                        P.op("tensor", lambda e, b=b: e.matmul(pb[6][:, 0:256], lhsT=khm[:NS, b, :], rhs=gv_tm[:NS, 17, :], start=True, stop=True),
                             reads=["khm", "gv_tm"], writes=["pb6"])
                        P.op("vector", lambda e, jb=jb, b=b: e.scalar_tensor_tensor(snew[jb][:, :], S0f[jb][:, :], eb[:, 4 * b + 3:4 * b + 4], pb[6][:, 0:256],
                                                                                    ALU.mult, ALU.add), reads=["S0f%d" % jb, "eb", "pb6"], writes=["snew%d" % jb])
                        P.dma("sync", lambda e, jb=jb, b=b, g=g: e.dma_start(out=ns_s[b, g], in_=snew[jb][:, :]), s_sn[jb], reads=["snew%d" % jb])
                if t >= 1:
                    for j in range(2):
                        P.op("scalar", lambda e, j=j, n=n: e.activation(out=osq[:, j, :n], in_=pb[4 + j][:, 0:n], func=AF.Square),
                             reads=["pb%d" % (4 + j)], writes=["osq%d" % j])
                    for j in range(2):
                        P.op("tensor", lambda e, j=j, n=n: e.matmul(pb[7][:, 0:n], lhsT=ones_f[:, 0:128], rhs=osq[:, j, :n], start=(j == 0), stop=(j == 1)),
                             reads=["ones_f", "osq%d" % j], writes=["pb7"])
                    P.op("scalar", lambda e, n=n: e.activation(out=rstd[:, :n], in_=pb[7][:, 0:n], func=AF.Ln, bias=EPS, scale=1.0 / 256), reads=["pb7"], writes=["rstd"])
                    P.op("scalar", lambda e, n=n: e.activation(out=rstd[:, :n], in_=rstd[:, :n], func=AF.Exp, scale=-0.5), reads=["rstd"], writes=["rstd"])
                    for j in range(2):
                        kidx = 8 + 2 * g + j
                        P.op("vector", lambda e, j=j, n=n, kidx=kidx, t0=t0: e.scalar_tensor_tensor(
                            mixT[:, kidx, t0 - NMETA:t0 - NMETA + n], pb[4 + j][:, 0:n], gnorm[:, j:j + 1], rstd[:, :n], ALU.mult, ALU.mult),
                            reads=["pb%d" % (4 + j), "gnorm", "rstd"], writes=["mixT%d" % kidx])
            for j in range(2):
                wgg = load_w(wring, 6144 + g * 256 + j * 128, 128)
                fm(wring, wgg, 128, lambda ap, c0, m, bn, kidx=8 + 2 * g + j: gate_chunk(ap, c0, m, bn, kidx, gate, gate2))
        P.barrier()
        if STAGE >= 6:
            P.dma("gpsimd", lambda e: e.dma_start(out=dbg[:, 0:NMX], in_=mixT[:, 8, :]), s_dbg, reads=["mixT8"])
            P.barrier()
        for kk in range(8 + 2 * (NG if NH_SB == 8 else min(NG, 1)), 16):
            P.op("gpsimd", lambda e, kk=kk: e.memset(mixT[:, kk, :], 0.0), writes=["mixT%d" % kk])
        if STAGE < 7:
            P.op("gpsimd", lambda e: e.memset(mixT[:, 0:8, LP - NMETA:NMX], 0.0), reads=["mixT%d" % h for h in range(8)],
                 writes=["mixT%d" % h for h in range(8)])
        P.barrier()

        if STAGE >= 7:
            AR.reset()
            ptb = AR.get(256 * 4, I32)
            idxt = AR.get(256 * 4, I32)
            iotf = AR.get(4, F32)
            Kbuf = [AR.get(2 * 1024 * 2, BF16, [2, 1024]) for _ in range(2)]
            Vbuf = [AR.get(2 * 1024 * 2, BF16, [2, 1024]) for _ in range(2)]
            KT = [AR.get(8 * 256 * 2, BF16, [8, 256]) for _ in range(2)]
            qm = AR.get(8 * 128 * 2, BF16, [8, 128])
            osel = AR.get(8 * 128 * 4, F32, [8, 128])
            CT3 = chain_temps()
            s_pt = P.slot("pt")
            s_kb = [P.slot("kb0"), P.slot("kb1")]
            s_vb = [P.slot("vb0"), P.slot("vb1")]
            P.dma("sync", lambda e: e.dma_start(out=ptb[:, :], in_=pt.rearrange("a n -> (a n)").partition_broadcast(128)), s_pt, writes=["ptb"])
            P.op("gpsimd", lambda e: e.iota(iotf[:, 0:1], pattern=[[0, 1]], base=0, channel_multiplier=1, allow_small_or_imprecise_dtypes=True),
                 writes=["iotf"])
            P.op("vector", lambda e: e.tensor_scalar(idxt[:, :], ptb[:, :], 128.0, iotf[:, 0:1], ALU.mult, ALU.add), reads=["ptb", "iotf"], writes=["idxt"])
            kctr = [0]
            vctr = [0]
            for G in range(4):
                P.op("vector", lambda e: e.memset(qm[:, :, :], 0.0), writes=["qm"])
                for h in range(8):
                    P.op("vector", lambda e, h=h, G=G: e.tensor_copy(
                        qm[:, h, :].rearrange("p (b x) -> p b x", b=4)[:, :, 4 * h:4 * h + 4],
                        sqT_s[:, h, 16 * G:16 * G + 16].rearrange("p (b q) -> p b q", b=4)), reads=["sqT_s"], writes=["qm"])

                def zfill(c, c0, m, bank, bn, G=G):
                    if c < 4:
                        for b4 in range(4):
                            bl = 4 * G + b4
                            for half in range(2):
                                ks = kctr[0] % 2
                                kctr[0] += 1
                                for p in range(2):
                                    col = bl * 16 + 4 * c + 2 * half + p
                                    P.dma("gpsimd", lambda e, ks=ks, p=p, col=col: e.indirect_dma_start(
                                        out=Kbuf[ks][:, p, :], out_offset=None, in_=ck,
                                        in_offset=bass.IndirectOffsetOnAxis(ap=idxt[:, col:col + 1], axis=0)),
                                        s_kb[ks], reads=["idxt"], writes=["Kbuf%d" % ks])
                                for h in range(8):
                                    for p in range(2):
                                        P.op("tensor", lambda e, ks=ks, h=h, p=p: e.transpose(
                                            pbh[6 + h // 4][:, (h % 4) * 256 + p * 128:(h % 4) * 256 + (p + 1) * 128],
                                            Kbuf[ks][:, p, h * 128:(h + 1) * 128], identb[:, :]),
                                            reads=["Kbuf%d" % ks, "identb"], writes=["pb%d" % (6 + h // 4)])
                                P.op("scalar", lambda e, ks=ks: e.copy(KT[ks][:, 0:4, :], pbh[6].rearrange("p (a b) -> p a b", a=4)),
                                     reads=["pb6"], writes=["KT%da" % ks])
                                P.op("vector", lambda e, ks=ks: e.tensor_copy(KT[ks][:, 4:8, :], pbh[7].rearrange("p (a b) -> p a b", a=4)),
                                     reads=["pb7"], writes=["KT%db" % ks])
                                for h in range(8):
                                    P.op("tensor", lambda e, ks=ks, h=h, b4=b4, half=half: e.matmul(
                                        bank[32 * b4:32 * b4 + 32, half * 256:(half + 1) * 256], lhsT=qm[:, h, 32 * b4:32 * b4 + 32],
                                        rhs=KT[ks][:, h, :], start=(h == 0), stop=(h == 7), tile_position=(0, 32 * b4)),
                                        reads=["qm", "KT%d%s" % (ks, "a" if h < 4 else "b")], writes=[bn])
                    else:
                        for h in range(8):
                            P.op("tensor", lambda e, h=h: e.matmul(bank[:, 0:NS], lhsT=qm[:, h, :], rhs=skT_s[:, h, :], start=(h == 0), stop=False),
                                 reads=["qm", "skT_s"], writes=[bn])
                        P.op("tensor", lambda e: e.matmul(bank[:, 0:NS], lhsT=identb[:, :], rhs=maskSb[:, 64 * G:64 * G + 64], start=False, stop=True),
                             reads=["identb", "maskSb"], writes=[bn])

                sb_chain(CT3, 128, 2048 + NS, zfill, sbrow[:, 0:1], "sbrow")
                blocks = [(128 * j, 128) for j in range(16)] + [(2048, NS)]
                transpose_blocks(CT3, 128, blocks)
                for b4 in range(4):
                    bl = 4 * G + b4
                    for jp in range(8):
                        vs = vctr[0] % 2
                        vctr[0] += 1
                        for p in range(2):
                            col = bl * 16 + 2 * jp + p
                            P.dma("gpsimd", lambda e, vs=vs, p=p, col=col: e.indirect_dma_start(
                                out=Vbuf[vs][:, p, :], out_offset=None, in_=cv,
                                in_offset=bass.IndirectOffsetOnAxis(ap=idxt[:, col:col + 1], axis=0)),
                                s_vb[vs], reads=["idxt"], writes=["Vbuf%d" % vs])
                        for p in range(2):
                            j = 2 * jp + p
                            for hb in range(2):
                                P.op("tensor", lambda e, vs=vs, p=p, j=j, hb=hb, b4=b4: e.matmul(
                                    pb[hb][32 * b4:32 * b4 + 32, :], lhsT=CT3["aT"][:, j, 32 * b4:32 * b4 + 32], rhs=Vbuf[vs][:, p, hb * 512:(hb + 1) * 512],
                                    start=(j == 0), stop=False, tile_position=(0, 32 * b4)),
                                    reads=["Vbuf%d" % vs, "aT%d" % (j // 8)], writes=["pb%d" % hb])
                    for hb in range(2):
                        P.op("tensor", lambda e, hb=hb, b4=b4: e.matmul(
                            pb[hb][32 * b4:32 * b4 + 32, :], lhsT=CT3["aT"][:NS, 16, 32 * b4:32 * b4 + 32],
                            rhs=sv_s[:, 4 * hb:4 * hb + 4, :].rearrange("p a b -> p (a b)"), start=False, stop=True, tile_position=(0, 32 * b4)),
                            reads=["sv_s", "aT2"], writes=["pb%d" % hb])
                for hb in range(2):
                    P.op("vector", lambda e, hb=hb: e.tensor_tensor(
                        osel[:, 4 * hb:4 * hb + 4, :], pb[hb][:, :].rearrange("p (a b) -> p a b", a=4),
                        cst[:, C_HM + 4 * hb:C_HM + 4 * hb + 4].unsqueeze(2).to_broadcast([128, 4, 128]), ALU.mult),
                        reads=["pb%d" % hb, "cst"], writes=["osel%d" % hb])
                for h in range(8):
                    P.op("tensor", lambda e, h=h: e.matmul(pb[2][:, 16 * h:16 * h + 16], lhsT=osel[:, h, :], rhs=cst[:, C_RS:C_RS + 16], start=True, stop=True),
                         reads=["osel%d" % (h // 4), "cst"], writes=["pb2"])
                P.op("vector", lambda e, G=G: e.tensor_tensor(
                    mixT[:, 0:8, LP - NMETA + 16 * G:LP - NMETA + 16 * G + 16], pb[2][:, 0:128].rearrange("p (a b) -> p a b", a=8),
                    sgate_s[:, :, 16 * G:16 * G + 16], ALU.mult), reads=["pb2", "sgate_s"], writes=["mixT%d" % h for h in range(8)])
            P.barrier()
            P.dma("gpsimd", lambda e: e.dma_start(out=dbg[:, 0:NMX], in_=mixT[:, 0, :]), s_dbg, reads=["mixT0"])
            P.barrier()

        if STAGE == 77:
            for kk in range(16):
                for hf in range(2):
                    P.dma("gpsimd", lambda e, kk=kk, hf=hf: e.dma_start(out=dbg2[:, kk, hf * 1056:(hf + 1) * 1056], in_=mixT[:, kk, hf * 1056:(hf + 1) * 1056]),
                          s_dbg, reads=["mixT%d" % kk])
            P.barrier()

        AR.reset()
        xo = [AR.get(D * 4, F32) for _ in range(2)]
        yo = [AR.get(D * 4, F32) for _ in range(2)]
        gpost = AR.get(D * 4, F32)
        s_wo = P.slot("wo")
        s_gp = P.slot("gp")
        s_xo = [P.slot("xo0"), P.slot("xo1")]
        s_yo = [P.slot("yo0"), P.slot("yo1")]
        for q4 in range(4):
            P.dma("gpsimd", lambda e, q4=q4: e.dma_start(out=wout[:, 4 * q4:4 * q4 + 4, :], in_=w_out_v[:, 4 * q4:4 * q4 + 4, :]),
                  s_wo, writes=["wout"])
        P.dma("sync", lambda e: e.dma_start(out=gpost[:, :], in_=gpost_d.partition_broadcast(128)), s_gp, writes=["gpost"])
        for t in range(1, 18 if STAGE >= 5 else 1):
            t0, n = TILES[t]
            j = t % 2
            m0 = t0 - NMETA
            P.dma("sync", lambda e, j=j, t0=t0, n=n: e.dma_start(out=xo[j][:n, :], in_=x_all[t0:t0 + n, :]), s_xo[j], writes=["xo%d" % j])
            for dc in range(4):
                bk = 4 * j + dc
                for k in range(KC):
                    P.op("tensor", lambda e, bk=bk, k=k, dc=dc, m0=m0, n=n: e.matmul(
                        pb[bk][:n, :], lhsT=mixT[:, k, m0:m0 + n], rhs=wout[:, k, dc * 512:(dc + 1) * 512], start=(k == 0), stop=(k == KC - 1)),
                        reads=["wout"] + ["mixT%d" % kk for kk in range(16)], writes=["pb%d" % bk])
                P.op("scalar", lambda e, bk=bk, dc=dc, j=j, n=n: e.activation(out=yo[j][:n, dc * 512:(dc + 1) * 512], in_=pb[bk][:n, :],
                                                                           func=AF.Square, accum_out=stat[:n, 8 + dc:9 + dc]),
                     reads=["pb%d" % bk], writes=["yo%d" % j, "st8%d" % dc])
            P.op("vector", lambda e, n=n: e.reduce_sum(stat[:n, 12:13], stat[:n, 8:12], axis=mybir.AxisListType.X),
                 reads=["st8%d" % dc for dc in range(4)], writes=["st12"])
            P.op("scalar", lambda e, n=n: e.activation(out=stat[:n, 13:14], in_=stat[:n, 12:13], func=AF.Ln, bias=EPS, scale=1.0 / D),
                 reads=["st12"], writes=["st13"])
            P.op("scalar", lambda e, n=n: e.activation(out=stat[:n, 14:15], in_=stat[:n, 13:14], func=AF.Exp, scale=-0.5),
                 reads=["st13"], writes=["st14"])
            for dc in range(4):
                bk = 4 * j + dc
                P.op("vector", lambda e, bk=bk, dc=dc, j=j, n=n: e.scalar_tensor_tensor(
                    yo[j][:n, dc * 512:(dc + 1) * 512], pb[bk][:n, :], stat[:n, 14:15], gpost[:n, dc * 512:(dc + 1) * 512], ALU.mult, ALU.mult),
                    reads=["pb%d" % bk, "st14", "gpost"], writes=["yo%d" % j])
            P.op("gpsimd", lambda e, j=j, n=n: e.tensor_tensor(yo[j][:n, :], yo[j][:n, :], xo[j][:n, :], ALU.add),
                 reads=["yo%d" % j, "xo%d" % j], writes=["yo%d" % j])
            dst = y_p[t0 - NMETA:t0 - NMETA + n, :] if t < 17 else y_s[:, :]
            P.dma("sync", lambda e, j=j, n=n, dst=dst: e.dma_start(out=dst, in_=yo[j][:n, :]), s_yo[j], reads=["yo%d" % j])

        P.emit(nc)
    return nc


def make_consts():
    c = np.zeros((128, NCST), np.float32)
    i = np.arange(128)
    c[:, C_ID:C_ID + 128] = np.eye(128, dtype=np.float32)
    c[:, C_MA:C_MA + 128] = np.where(i[None, :] < i[:, None], 0.0, NEG)
    b4, hh, qq = i // 32, (i // 4) % 8, i % 4
    col = np.arange(64)
    for G in range(4):
        ok = ((col[None, :] // 4) == (4 * G + b4)[:, None]) & ((col[None, :] % 4) < qq[:, None])
        c[:, C_MS + 64 * G:C_MS + 64 * (G + 1)] = np.where(ok, 0.0, NEG)
    same = (i[:, None] // 64) == (i[None, :] // 64)
    c[:, C_TU:C_TU + 128] = np.where(same & (i[:, None] <= i[None, :]), -1.0 / 16, 0.0)
    c[:, C_TL:C_TL + 128] = np.where(same & (i[:, None] > i[None, :]), -1.0 / 16, 0.0)
    c[:, C_CA:C_CA + 128] = np.where(same & (i[:, None] <= i[None, :]), 1.0, 0.0)
    j = np.arange(64)
    same4 = (j[:, None] // 4) == (j[None, :] // 4)
    c[:64, C_TUS:C_TUS + 64] = np.where(same4 & (j[:, None] <= j[None, :]), -1.0 / 16, 0.0)
    c[:64, C_TLS:C_TLS + 64] = np.where(same4 & (j[:, None] > j[None, :]), -1.0 / 16, 0.0)
    c[:64, C_CAS:C_CAS + 64] = np.where(same4 & (j[:, None] <= j[None, :]), 1.0, 0.0)
    c[:64, C_MB:C_MB + 16] = (j[:, None] // 4 == np.arange(16)[None, :])
    c[:, C_HM:C_HM + 8] = (hh[:, None] == np.arange(8)[None, :])
    c[:, C_RS:C_RS + 16] = ((b4 * 4 + qq)[:, None] == np.arange(16)[None, :])
    return c


_NC_CACHE = {}


def kernel(x_prompt, x_sample, cache_k, cache_v, state_gla, page_table, meta_tokens,
           norm_pre_g, w_in, sb_bias, w_alpha, b_alpha, gla_norm_g, w_out, norm_post_g):
    f32 = lambda a: np.ascontiguousarray(np.asarray(a, dtype=np.float32))
    x_prompt, x_sample, meta_tokens = f32(x_prompt), f32(x_sample), f32(meta_tokens)
    ckr = f32(cache_k).reshape(NPOOL_ROWS, 1024)
    cvr = f32(cache_v).reshape(NPOOL_ROWS, 1024)
    state_gla = f32(state_gla)
    page_table = np.ascontiguousarray(np.asarray(page_table, dtype=np.int32))
    w_in_ = f32(w_in)[0]
    w_out_ = f32(w_out)[0]
    gpre = f32(norm_pre_g)[0].reshape(KC, 128).T.copy()
    gpost = f32(norm_post_g)[0].reshape(D)
    sbias = f32(sb_bias)[0].reshape(8)
    sbrow = np.ascontiguousarray(np.tile(np.repeat(sbias, 4), 4).reshape(128, 1))
    walpha = np.concatenate([f32(w_alpha)[0], f32(b_alpha)[0].reshape(1, 512)], axis=0)
    gnorm = f32(gla_norm_g)[0].reshape(2, 128).T.copy()
    cst = make_consts()
    if "nc" not in _NC_CACHE:
        _NC_CACHE["nc"] = build_program()
    nc = _NC_CACHE["nc"]
    in_maps = []
    for c in range(8):
        x_all = np.concatenate([meta_tokens, x_prompt[c], x_sample[16 * c:16 * c + 16].reshape(NS, D)], axis=0)
        in_maps.append({
            "x_all": x_all, "ck": ckr, "cv": cvr,
            "pt": page_table[16 * c:16 * c + 16].reshape(1, 256),
            "state": state_gla[0, 16 * c:16 * c + 16],
            "w_in": w_in_, "w_out": w_out_, "gpre": gpre, "gpost": gpost, "sbias": sbias, "sbrow": sbrow,
            "walpha": walpha, "gnorm": gnorm, "cst": cst,
        })
    res = run_bass_kernel_spmd(nc, in_maps, core_ids=list(range(8)))
    R = res.results
    y_prompt = np.stack([R[c]["y_p"] for c in range(8)], axis=0)
    y_sample = np.concatenate([R[c]["y_s"].reshape(16, 4, D) for c in range(8)], axis=0)
    nk_p = np.stack([R[c]["nk_p"].reshape(LP, 8, 128) for c in range(8)], axis=0)[None]
    nv_p = np.stack([R[c]["nv_p"].reshape(LP, 8, 128) for c in range(8)], axis=0)[None]
    ns_p = np.stack([R[c]["ns_p"] for c in range(8)], axis=0)[None]
    nk_s = np.concatenate([R[c]["nk_s"].reshape(16, 4, 8, 128) for c in range(8)], axis=0)[None]
    nv_s = np.concatenate([R[c]["nv_s"].reshape(16, 4, 8, 128) for c in range(8)], axis=0)[None]
    ns_s = np.concatenate([R[c]["ns_s"] for c in range(8)], axis=0)[None]
    return (y_prompt, y_sample, nk_p, nv_p, ns_p, nk_s, nv_s, ns_s)
```

```python
import contextlib
import numpy as np
import concourse.bass as bass
import concourse.mybir as mybir
from concourse.bass_utils import run_bass_kernel_spmd

F32 = mybir.dt.float32
BF16 = mybir.dt.bfloat16
I32 = mybir.dt.int32
U8 = mybir.dt.uint8
AF = mybir.ActivationFunctionType
ALU = mybir.AluOpType

D = 2048
KC = 16
NMETA = 16
SEQ = 2048
LP = NMETA + SEQ
NS = 64
NT = LP + NS
NIN = 7184
NPOOL_ROWS = 2560 * 128
EPS = 1e-6
NEG = -30000.0
SCALE = 128 ** -0.5
TILES = [(0, 16)] + [(16 + 128 * i, 128) for i in range(16)] + [(LP, NS)]
TCH = [(0, 512), (512, 512), (1024, 512), (1536, 512), (2048, NT - 2048)]
ENGS = ["sync", "scalar", "gpsimd", "vector", "tensor"]

C_ID, C_MA, C_MS, C_TU, C_TL, C_CA, C_TUS, C_TLS, C_CAS, C_MB, C_HM, C_RS = (
    0, 128, 256, 512, 640, 768, 896, 960, 1024, 1088, 1104, 1112)
NCST = 1128
STAGE = 99
SUB = 'abcde'
NH_SB = 8


class Op:
    __slots__ = ("eng", "fn", "deps", "needed", "count", "dma", "dtok")

    def __init__(self, eng, fn):
        self.eng, self.fn = eng, fn
        self.deps, self.needed, self.count, self.dma, self.dtok = [], False, None, None, None


class DSlot:
    def __init__(self, name):
        self.name, self.count, self.sem = name, 0, None


class Prog:
    def __init__(self):
        self.q = {e: [] for e in ENGS}
        self.lastw, self.readers, self.slots = {}, {}, []
        self.nsig = {e: 0 for e in ENGS}
        self.dma_ops = []

    def slot(self, name):
        s = DSlot(name)
        self.slots.append(s)
        return s

    def _adddep(self, op, d):
        if d is op:
            return
        if d.dma is None and op.dma is None and d.eng == "tensor" and op.eng == "tensor":
            return
        op.deps.append((d, d.dma.count if d.dma is not None else None))
        if d.dma is None:
            d.needed = True

    def _track(self, op, reads, writes):
        writes = list(writes) + [r for r in reads if r.startswith("pb")]
        reads = [r for r in reads if not r.startswith("pb")]
        deps = []
        for r in reads:
            w = self.lastw.get(r)
            if w is not None:
                deps.append(w)
        for w_ in writes:
            w = self.lastw.get(w_)
            if w is not None:
                deps.append(w)
            deps.extend(self.readers.get(w_, ()))
        seen = set()
        for d in deps:
            if id(d) in seen:
                continue
            seen.add(id(d))
            self._adddep(op, d)
        for r in reads:
            self.readers.setdefault(r, []).append(op)
        for w_ in writes:
            self.lastw[w_] = op
            self.readers[w_] = []

    def op(self, eng, fn, reads=(), writes=()):
        o = Op(eng, fn)
        self._track(o, reads, writes)
        self.q[eng].append(o)
        return o

    def dma(self, eng, fn, slot, reads=(), writes=()):
        o = Op(eng, fn)
        o.dma = slot
        self._track(o, reads, writes)
        slot.count += 16
        o.dtok = slot.count
        self.q[eng].append(o)
        self.dma_ops.append(o)
        return o

    def barrier(self):
        lasts = []
        for e in ENGS:
            for o in reversed(self.q[e]):
                if o.dma is None and o.fn is not None:
                    lasts.append(o)
                    break
        dmas = list(self.dma_ops)
        self.dma_ops = []
        for e in ENGS:
            b = Op(e, None)
            for d in lasts:
                if d.eng != e:
                    b.deps.append((d, None))
                    d.needed = True
            for d in dmas:
                b.deps.append((d, d.dma.count))
            self.q[e].append(b)
        self.lastw, self.readers = {}, {}

    def emit(self, nc):
        for e in ENGS:
            c = 0
            for o in self.q[e]:
                if o.dma is None and o.needed:
                    c += 1
                    o.count = c
            self.nsig[e] = c
        with contextlib.ExitStack() as st:
            esem = {e: st.enter_context(nc.semaphore("es_" + e)) for e in ENGS}
            for s in self.slots:
                s.sem = st.enter_context(nc.semaphore("ds_" + s.name))
            block = st.enter_context(nc.Block())
            prog = self

            def make(ename):
                def body(eng):
                    seen = {}
                    for o in prog.q[ename]:
                        need = {}
                        for d, dval in o.deps:
                            if d.dma is not None:
                                key, val, sem = ("d", id(d.dma)), dval, d.dma.sem
                            else:
                                key, val, sem = ("e", d.eng), d.count, esem[d.eng]
                            if need.get(key, (None, -1))[1] < val:
                                need[key] = (sem, val)
                        for key, (sem, val) in need.items():
                            if seen.get(key, -1) >= val:
                                continue
                            seen[key] = val
                            eng.wait_ge(sem, val)
                        if o.fn is None:
                            continue
                        ins = o.fn(eng)
                        if o.dma is not None:
                            ins.then_inc(o.dma.sem, 16)
                        elif o.needed:
                            ins.then_inc(esem[ename], 1)
                    if ename == "sync":
                        for s in prog.slots:
                            if s.count > 0:
                                eng.wait_ge(s.sem, s.count)
                        for e2 in ENGS:
                            if prog.nsig[e2] > 0:
                                eng.wait_ge(esem[e2], prog.nsig[e2])
                return body

            block.sync(make("sync"))
            block.scalar(make("scalar"))
            block.gpsimd(make("gpsimd"))
            block.vector(make("vector"))
            block.tensor(make("tensor"))


class Arena:
    def __init__(self, t, nbytes):
        self.t, self.n, self.off = t, nbytes, 0

    def reset(self):
        self.off = 0

    def get(self, nbytes, dt, shape=None):
        o = (self.off + 31) // 32 * 32
        assert o + nbytes <= self.n, ("arena overflow", o + nbytes, self.n)
        self.off = o + nbytes
        v = self.t[:, o:o + nbytes].bitcast(dt)
        if shape is not None:
            names = " ".join("d%d" % i for i in range(len(shape)))
            v = v.rearrange("p (%s) -> p %s" % (names, names), **{"d%d" % i: s for i, s in enumerate(shape)})
        return v


def build_program():
    nc = bass.Bass("TRN2", target_bir_lowering=False)
    dram = lambda n, s, d, k: nc.dram_tensor(n, s, d, kind=k).ap()
    x_all = dram("x_all", [NT, D], F32, "ExternalInput")
    ck = dram("ck", [NPOOL_ROWS, 1024], F32, "ExternalInput")
    cv = dram("cv", [NPOOL_ROWS, 1024], F32, "ExternalInput")
    pt = dram("pt", [1, 256], I32, "ExternalInput")
    state = dram("state", [16, 4, 128, 256], F32, "ExternalInput")
    w_in = dram("w_in", [D, NIN], F32, "ExternalInput")
    w_out = dram("w_out", [D, D], F32, "ExternalInput")
    gpre_d = dram("gpre", [128, KC], F32, "ExternalInput")
    gpost_d = dram("gpost", [D], F32, "ExternalInput")
    sbias_d = dram("sbias", [8], F32, "ExternalInput")
    sbrow_d = dram("sbrow", [128, 1], F32, "ExternalInput")
    walpha_d = dram("walpha", [17, 512], F32, "ExternalInput")
    gnorm_d = dram("gnorm", [128, 2], F32, "ExternalInput")
    cst_d = dram("cst", [128, NCST], F32, "ExternalInput")
    y_p = dram("y_p", [SEQ, D], F32, "ExternalOutput")
    y_s = dram("y_s", [NS, D], F32, "ExternalOutput")
    nk_p = dram("nk_p", [LP, 1024], F32, "ExternalOutput")
    nv_p = dram("nv_p", [LP, 1024], F32, "ExternalOutput")
    ns_p = dram("ns_p", [4, 128, 256], F32, "ExternalOutput")
    nk_s = dram("nk_s", [NS, 1024], F32, "ExternalOutput")
    nv_s = dram("nv_s", [NS, 1024], F32, "ExternalOutput")
    ns_s = dram("ns_s", [16, 4, 128, 256], F32, "ExternalOutput")
    w_in_v = w_in.rearrange("(k p) n -> p k n", p=128)
    w_out_v = w_out.rearrange("(k p) n -> p k n", p=128)

    RC_BYTES = 63488
    dbg = dram("dbg", [128, NT], F32, "ExternalOutput")
    dbg2 = dram("dbg2", [128, 16, NT - NMETA], F32, "ExternalOutput") if STAGE == 77 else None
    with contextlib.ExitStack() as st:
        sb = lambda n, s, d: st.enter_context(nc.sbuf_tensor(n, s, d))
        RA = sb("RA", [128, KC * NT], BF16)
        hT = RA[:, :].rearrange("p (k t) -> p k t", k=KC)
        wout = RA[:, 0:KC * D].rearrange("p (k n) -> p k n", k=KC)
        RB = sb("RB", [128, KC * (NT - NMETA)], BF16)
        mixT = RB[:, :].rearrange("p (k t) -> p k t", k=KC)
        NMX = NT - NMETA
        cst = sb("cst_t", [128, NCST], F32)
        identb = sb("identb", [128, 128], BF16)
        maskab = sb("maskab", [128, 128], BF16)
        maskSb = sb("maskSb", [128, 256], BF16)
        ones_f = sb("ones_f", [128, 512], F32)
        gpre = sb("gpre_t", [128, KC], F32)
        sbias = sb("sbias_t", [128, 8], F32)
        sbrow = sb("sbrow_t", [128, 1], F32)
        gnorm = sb("gnorm_t", [128, 2], F32)
        stat = sb("stat", [128, 16], F32)
        bh = sb("bh", [128, 1], F32)
        sqT_s = sb("sqT_s", [128, 8, NS], BF16)
        skT_s = sb("skT_s", [128, 8, NS], BF16)
        sv_s = sb("sv_s", [NS, 8, 128], BF16)
        sgate_s = sb("sgate_s", [128, 8, NS], BF16)
        arena_t = sb("arena", [128, RC_BYTES], U8)
        AR = Arena(arena_t, RC_BYTES)
        pb = [st.enter_context(nc.psum_tensor("pb%d" % i, [128, 512], F32)) for i in range(8)]
        pbh = [p[:, :].bitcast(BF16) for p in pb]

        P = Prog()
        ident_f = cst[:, C_ID:C_ID + 128]

        s_c = P.slot("cst")
        P.dma("sync", lambda e: e.dma_start(out=cst[:, :], in_=cst_d), s_c, writes=["cst"])
        P.dma("sync", lambda e: e.dma_start(out=gpre[:, :], in_=gpre_d), s_c, writes=["gpre"])
        P.dma("sync", lambda e: e.dma_start(out=sbias[:, :], in_=sbias_d.partition_broadcast(128)), s_c, writes=["sbias"])
        P.dma("sync", lambda e: e.dma_start(out=sbrow[:, :], in_=sbrow_d), s_c, writes=["sbrow"])
        P.dma("sync", lambda e: e.dma_start(out=gnorm[:, :], in_=gnorm_d), s_c, writes=["gnorm"])
        P.op("vector", lambda e: e.tensor_copy(identb[:, :], cst[:, C_ID:C_ID + 128]), reads=["cst"], writes=["identb"])
        P.op("vector", lambda e: e.tensor_copy(maskab[:, :], cst[:, C_MA:C_MA + 128]), reads=["cst"], writes=["maskab"])
        P.op("vector", lambda e: e.tensor_copy(maskSb[:, :], cst[:, C_MS:C_MS + 256]), reads=["cst"], writes=["maskSb"])
        P.op("gpsimd", lambda e: e.memset(ones_f[:, :], 1.0), writes=["ones_f"])

        AR.reset()
        xt = [AR.get(D * 4, F32) for _ in range(2)]
        xn = [AR.get(D * 2, BF16) for _ in range(2)]
        s_x = [P.slot("x0"), P.slot("x1")]
        for t, (t0, n) in enumerate(TILES):
            j = t % 2
            P.dma("sync", lambda e, j=j, t0=t0, n=n: e.dma_start(out=xt[j][:n, :], in_=x_all[t0:t0 + n, :]), s_x[j],
                  writes=["xt%d" % j])
            P.op("scalar", lambda e, j=j, n=n: e.activation(out=xn[j][:n, :], in_=xt[j][:n, :], func=AF.Square,
                                                           accum_out=stat[:n, 0:1]),
                 reads=["xt%d" % j], writes=["xn%d" % j, "st0"])
            P.op("scalar", lambda e, n=n: e.activation(out=stat[:n, 1:2], in_=stat[:n, 0:1], func=AF.Ln, bias=EPS, scale=1.0 / D),
                 reads=["st0"], writes=["st1"])
            P.op("scalar", lambda e, n=n: e.activation(out=stat[:n, 2:3], in_=stat[:n, 1:2], func=AF.Exp, scale=-0.5),
                 reads=["st1"], writes=["st2"])
            P.op("vector", lambda e, j=j, n=n: e.tensor_scalar(xn[j][:n, :], xt[j][:n, :], stat[:n, 2:3], None, ALU.mult),
                 reads=["xt%d" % j, "st2"], writes=["xn%d" % j])
            for half in range(2):
                bk = 2 * j + half
                for kk in range(8):
                    k = 8 * half + kk
                    P.op("tensor", lambda e, bk=bk, kk=kk, k=k, j=j, n=n: e.transpose(
                        pbh[bk][:, kk * 128:kk * 128 + n], xn[j][:n, k * 128:(k + 1) * 128], identb[:n, :n]),
                        reads=["xn%d" % j, "identb"], writes=["pb%d" % bk])
                P.op("vector", lambda e, bk=bk, half=half, t0=t0, n=n: e.tensor_tensor(
                    hT[:, 8 * half:8 * half + 8, t0:t0 + n],
                    pbh[bk].rearrange("p (a b) -> p a b", a=8)[:, :, 0:n],
                    gpre[:, 8 * half:8 * half + 8].unsqueeze(2).to_broadcast([128, 8, n]), ALU.mult),
                    reads=["pb%d" % bk, "gpre"], writes=["hT"])
        P.barrier()
        s_dbg = P.slot("dbg")
        P.dma("gpsimd", lambda e: e.dma_start(out=dbg, in_=hT[:, 3, :]), s_dbg, reads=["hT"])

        s_w = [P.slot("w%d" % i) for i in range(3)]
        wctr = [0]
        ipb = [0]

        def load_w(wring, c0, ncols):
            i = wctr[0] % 3
            wctr[0] += 1
            P.dma("gpsimd", lambda e, i=i, c0=c0, ncols=ncols: e.dma_start(out=wring[i][:, :, 0:ncols], in_=w_in_v[:, :, c0:c0 + ncols]),
                  s_w[i], writes=["wr%d" % i])
            return i

        def fm(wring, wi, ncols, evac):
            for (c0, m) in TCH:
                bk = ipb[0] % 2
                ipb[0] += 1
                for k in range(KC):
                    P.op("tensor", lambda e, bk=bk, k=k, c0=c0, m=m: e.matmul(
                        pb[bk][:ncols, :m], lhsT=wring[wi][:, k, 0:ncols], rhs=hT[:, k, c0:c0 + m], start=(k == 0), stop=(k == KC - 1)),
                        reads=["wr%d" % wi, "hT"], writes=["pb%d" % bk])
                evac(pb[bk][:ncols, :m], c0, m, "pb%d" % bk)

        def tm(wring, wi, ncols, evac):
            for t, (t0, n) in enumerate(TILES):
                bk = ipb[0] % 2
                ipb[0] += 1
                for k in range(KC):
                    P.op("tensor", lambda e, bk=bk, k=k, t0=t0, n=n: e.matmul(
                        pb[bk][:n, :ncols], lhsT=hT[:, k, t0:t0 + n], rhs=wring[wi][:, k, 0:ncols], start=(k == 0), stop=(k == KC - 1)),
                        reads=["wr%d" % wi, "hT"], writes=["pb%d" % bk])
                evac(pb[bk][:n, :ncols], t, "pb%d" % bk)

        def sb_chain_A(T_, n, K, zfill, bias_ap, bias_reg):
            nch = (K + 511) // 512
            ai = T_["actr"][0] % len(T_["a2"])
            T_["actr"][0] += 1
            a_buf = T_["a2"][ai]
            an = "arow%d" % ai
            for c in range(nch):
                c0 = c * 512
                m = min(512, K - c0)
                bk = 2 + (T_["zctr"][0] % T_["znb"])
                T_["zctr"][0] += 1
                bn = "pb%d" % bk
                zfill(c, c0, m, pb[bk], bn)
                e_ = T_["esp"][c % 2]
                en = "esp%d" % (c % 2)
                pc = T_["pc"][c % 2]
                pn = "pc%d" % (c % 2)
                P.op("scalar", lambda e, e_=e_, bk=bk, m=m: e.activation(out=e_[:n, :m], in_=pb[bk][:n, :m], func=AF.Exp,
                                                                       bias=bias_ap, scale=SCALE),
                     reads=[bn, bias_reg], writes=[en])
                P.op("scalar", lambda e, e_=e_, m=m: e.activation(out=e_[:n, :m], in_=e_[:n, :m], func=AF.Identity, bias=1.0, scale=1.0),
                     reads=[en], writes=[en])
                if c == 0:
                    P.op("vector", lambda e, pc=pc: e.memset(pc[:n, 0:1], 1.0), writes=[pn])
                else:
                    pp = T_["pc"][(c - 1) % 2]
                    P.op("vector", lambda e, pc=pc, pp=pp: e.tensor_copy(pc[:n, 0:1], pp[:n, 512:513]),
                         reads=["pc%d" % ((c - 1) % 2)], writes=[pn])
                P.op("vector", lambda e, pc=pc, e_=e_, m=m: e.tensor_tensor_scan(
                    pc[:n, 1:m + 1], e_[:n, :m], ones_f[:n, :m], pc[:n, 0:1], ALU.mult, ALU.mult),
                    reads=[pn, en, "ones_f"], writes=[pn])
                P.op("vector", lambda e, pc=pc, c0=c0, m=m: e.tensor_tensor(
                    T_["w"][:n, c0:c0 + m], pc[:n, 1:m + 1], pc[:n, 0:m], ALU.subtract),
                    reads=[pn], writes=["wrow"])
            return {"n": n, "K": K, "nch": nch, "a": a_buf, "an": an}

        def sb_chain_B(T_, info):
            n, K, nch, a_buf, an = info["n"], info["K"], info["nch"], info["a"], info["an"]
            ml = K - (nch - 1) * 512
            pl = T_["pc"][(nch - 1) % 2]
            P.op("vector", lambda e: e.reciprocal(stat[:n, 4:5], pl[:n, ml:ml + 1]),
                 reads=["pc%d" % ((nch - 1) % 2)], writes=["st4"])
            P.op("vector", lambda e: e.tensor_scalar(a_buf[:n, :K], T_["w"][:n, :K], stat[:n, 4:5], None, ALU.mult),
                 reads=["wrow", "st4"], writes=[an])

        def sb_chain(T_, n, K, zfill, bias_ap, bias_reg):
            info = sb_chain_A(T_, n, K, zfill, bias_ap, bias_reg)
            sb_chain_B(T_, info)
            T_["a"], T_["an"] = info["a"], info["an"]

        def chain_temps(znb=4, na=2):
            return {"esp": [AR.get(512 * 4, F32) for _ in range(2)],
                    "pc": [AR.get(513 * 4, F32) for _ in range(2)],
                    "w": AR.get(2112 * 4, F32), "a2": [AR.get(2112 * 2, BF16) for _ in range(na)], "actr": [0], "a": None, "an": None,
                    "aT": AR.get(17 * 128 * 2, BF16, [17, 128]), "zctr": [0], "znb": znb}

        def transpose_blocks(T_, n, blocks, a_buf=None, an=None):
            if a_buf is None:
                a_buf, an = T_["a"], T_["an"]
            for g0 in range(0, len(blocks), 8):
                grp = blocks[g0:g0 + 8]
                bk = 6 + ((g0 // 8) % 2)
                for i, (col0, m) in enumerate(grp):
                    P.op("tensor", lambda e, bk=bk, i=i, col0=col0, m=m: e.transpose(
                        pbh[bk][:m, i * 128:i * 128 + n], a_buf[:n, col0:col0 + m], identb[:n, :n]),
                        reads=[an, "identb"], writes=["pb%d" % bk])
                ng = len(grp)
                P.op("scalar", lambda e, bk=bk, g0=g0, ng=ng: e.copy(
                    T_["aT"][:, g0:g0 + ng, :n], pbh[bk].rearrange("p (a b) -> p a b", a=8)[:, 0:ng, 0:n]),
                    reads=["pb%d" % bk], writes=["aT%d" % (g0 // 8)])

        def gate_chunk(ap, c0, m, bn, kidx, gate, gate2, sg_save=None):
            lo = max(c0, NMETA)
            mm = c0 + m - lo
            src = ap[:, lo - c0:lo - c0 + mm]
            P.op("scalar", lambda e: e.activation(out=gate[:, :mm], in_=src, func=AF.Exp, scale=-1.0), reads=[bn], writes=["gate"])
            P.op("vector", lambda e: e.tensor_scalar(gate[:, :mm], gate[:, :mm], 1.0, None, ALU.add), reads=["gate"], writes=["gate"])
            P.op("vector", lambda e: e.reciprocal(gate[:, :mm], gate[:, :mm]), reads=["gate"], writes=["gate"])
            P.op("vector", lambda e: e.tensor_tensor(gate2[:, :mm], src, gate[:, :mm], ALU.mult), reads=[bn, "gate"], writes=["gate2"])
            ma = mm
            if sg_save is not None and c0 + m > LP:
                ma = LP - lo
                P.op("vector", lambda e: e.tensor_copy(sgate_s[:, sg_save, :], gate2[:, ma:ma + NS]), reads=["gate2"], writes=["sgate_s"])
            P.op("vector", lambda e: e.tensor_tensor(mixT[:, kidx, lo - NMETA:lo - NMETA + ma], mixT[:, kidx, lo - NMETA:lo - NMETA + ma],
                                                     gate2[:, :ma], ALU.mult), reads=["gate2", "mixT%d" % kidx], writes=["mixT%d" % kidx])

        AR.reset()
        wring = [AR.get(KC * 128 * 2, BF16, [KC, 128]) for _ in range(3)]
        sqT = AR.get(NT * 2, BF16)
        skT = AR.get(NT * 2, BF16)
        svh = AR.get(18 * 128 * 2, BF16, [18, 128])
        kvst = [AR.get(128 * 4, F32) for _ in range(2)]
        gate = AR.get(512 * 4, F32)
        gate2 = AR.get(512 * 4, F32)
        CT = chain_temps()
        s_kv = [P.slot("kv0"), P.slot("kv1")]
        kvc = [0]

        for h in range(NH_SB if STAGE >= 2 else 0):
            wq = load_w(wring, h * 128, 128)
            wk = load_w(wring, 1024 + h * 128, 128)
            wv = load_w(wring, 2048 + h * 128, 128)

            def ev_q(ap, c0, m, bn, h=h):
                P.op("scalar", lambda e: e.copy(sqT[:, c0:c0 + m], ap), reads=[bn], writes=["sqT"])
                if c0 == 2048:
                    P.op("vector", lambda e: e.tensor_copy(sqT_s[:, h, :], ap[:, LP - 2048:LP - 2048 + NS]), reads=[bn], writes=["sqT_s"])

            def ev_k(ap, c0, m, bn, h=h):
                P.op("scalar", lambda e: e.copy(skT[:, c0:c0 + m], ap), reads=[bn], writes=["skT"])
                if c0 == 2048:
                    P.op("vector", lambda e: e.tensor_copy(skT_s[:, h, :], ap[:, LP - 2048:LP - 2048 + NS]), reads=[bn], writes=["skT_s"])

            if 'b' in SUB:
                fm(wring, wq, 128, ev_q)
            if 'c' in SUB:
                fm(wring, wk, 128, ev_k)

            def ev_ktm(ap, t, bn, h=h):
                t0, n = TILES[t]
                j = kvc[0] % 2
                kvc[0] += 1
                P.op("scalar", lambda e: e.copy(kvst[j][:n, :], ap), reads=[bn], writes=["kvst%d" % j])
                dst = nk_p[t0:t0 + n, h * 128:(h + 1) * 128] if t < 17 else nk_s[:, h * 128:(h + 1) * 128]
                P.dma("sync", lambda e: e.dma_start(out=dst, in_=kvst[j][:n, :]), s_kv[j], reads=["kvst%d" % j])

            def ev_vtm(ap, t, bn, h=h):
                t0, n = TILES[t]
                j = kvc[0] % 2
                kvc[0] += 1
                P.op("scalar", lambda e: e.copy(kvst[j][:n, :], ap), reads=[bn], writes=["kvst%d" % j])
                dst = nv_p[t0:t0 + n, h * 128:(h + 1) * 128] if t < 17 else nv_s[:, h * 128:(h + 1) * 128]
                P.dma("sync", lambda e: e.dma_start(out=dst, in_=kvst[j][:n, :]), s_kv[j], reads=["kvst%d" % j])
                P.op("vector", lambda e: e.tensor_copy(svh[:n, t, :], ap), reads=[bn], writes=["svh"])
                if t == 17:
                    P.op("vector", lambda e: e.tensor_copy(sv_s[:, h, :], ap), reads=[bn], writes=["sv_s"])

            if 'd' in SUB:
                tm(wring, wk, 128, ev_ktm)
            if 'e' in SUB:
                tm(wring, wv, 128, ev_vtm)

            P.op("vector", lambda e, h=h: e.tensor_copy(bh[:, :], sbias[:, h:h + 1]), reads=["sbias"], writes=["bh"])
            def mk_zfill(i):
                t0, n = TILES[i]
                K = t0 + n
                d0 = t0

                def zfill(c, c0, m, bank, bn):
                    lo, hi = max(c0, d0), min(c0 + m, K)
                    has_mask = lo < hi
                    P.op("tensor", lambda e: e.matmul(bank[:n, :m], lhsT=sqT[:, t0:t0 + n], rhs=skT[:, c0:c0 + m],
                                                      start=True, stop=not has_mask),
                         reads=["sqT", "skT"], writes=[bn])
                    if has_mask:
                        P.op("tensor", lambda e: e.matmul(bank[:n, lo - c0:hi - c0], lhsT=identb[:n, :n], rhs=maskab[:n, lo - d0:hi - d0],
                                                          start=False, stop=True),
                             reads=["identb", "maskab"], writes=[bn])
                return zfill

            NQ = 17 if STAGE >= 3 else 0
            infos = [None] * (NQ + 1)
            if NQ:
                infos[0] = sb_chain_A(CT, TILES[0][1], TILES[0][0] + TILES[0][1], mk_zfill(0), bh[:TILES[0][1], 0:1], "bh")
            for i in range(NQ):
                t0, n = TILES[i]
                sb_chain_B(CT, infos[i])
                if i + 1 < NQ:
                    t1, n1 = TILES[i + 1]
                    infos[i + 1] = sb_chain_A(CT, n1, t1 + n1, mk_zfill(i + 1), bh[:n1, 0:1], "bh")
                blocks = [(0, 16)] + [(16 + 128 * b, 128) for b in range(i)]
                transpose_blocks(CT, n, blocks, infos[i]["a"], infos[i]["an"])
                ob = i % 2
                for kb, (col0, m) in enumerate(blocks):
                    P.op("tensor", lambda e, kb=kb, m=m, ob=ob, n=n: e.matmul(
                        pb[ob][:, :n], lhsT=svh[:m, kb, :], rhs=CT["aT"][:m, kb, :n], start=(kb == 0), stop=(kb == len(blocks) - 1)),
                        reads=["svh", "aT%d" % (kb // 8)], writes=["pb%d" % ob])
                if i >= 1:
                    P.op("scalar", lambda e, ob=ob, t0=t0, n=n, h=h: e.copy(mixT[:, h, t0 - NMETA:t0 - NMETA + n], pb[ob][:, :n]),
                         reads=["pb%d" % ob], writes=["mixT%d" % h])

            wg = load_w(wring, 3072 + h * 128, 128)

            def ev_g(ap, c0, m, bn, h=h):
                gate_chunk(ap, c0, m, bn, h, gate, gate2, sg_save=h)

            if STAGE >= 4:
                fm(wring, wg, 128, ev_g)
        P.barrier()
        if STAGE >= 3:
            P.dma("gpsimd", lambda e: e.dma_start(out=dbg[:, 0:NMX], in_=mixT[:, 0, :]), s_dbg, reads=["mixT0"])
            P.barrier()

        AR.reset()
        wring = [AR.get(KC * 128 * 2, BF16, [KC, 128]) for _ in range(3)]
        gqT = AR.get(NT * 2, BF16)
        gkT = AR.get(NT * 2, BF16)
        gk_tm = AR.get(18 * 128 * 2, BF16, [18, 128])
        gv_tm = AR.get(18 * 256 * 2, BF16, [18, 256])
        gaT = AR.get(NT * 2, BF16)
        walb = AR.get(512 * 2, BF16)
        Sf = AR.get(256 * 4, F32)
        Sb = AR.get(256 * 2, BF16)
        lsp = AR.get(128 * 4, F32)
        eb = AR.get(128 * 4, F32)
        enb = AR.get(128 * 4, F32)
        rex = AR.get(128 * 4, F32)
        qt = AR.get(128 * 2, BF16)
        kt = AR.get(128 * 2, BF16)
        attn = AR.get(128 * 2, BF16)
        khat = AR.get(128 * 2, BF16)
        osq = AR.get(2 * 128 * 4, F32, [2, 128])
        rstd = AR.get(128 * 4, F32)
        gate = AR.get(512 * 4, F32)
        gate2 = AR.get(512 * 4, F32)
        S0f = [AR.get(256 * 4, F32) for _ in range(2)]
        S0b = [AR.get(256 * 2, BF16) for _ in range(2)]
        snew = [AR.get(256 * 4, F32) for _ in range(2)]
        khm = AR.get(16 * 128 * 2, BF16, [16, 128])
        s_wa = P.slot("wa")
        s_s0 = [P.slot("s0a"), P.slot("s0b")]
        s_s0b = [P.slot("s0ba"), P.slot("s0bb")]
        s_sn = [P.slot("sna"), P.slot("snb")]
        s_nsp = P.slot("nsp")
        s0c = [0]
        NG = 4 if STAGE >= 6 else 0
        if NG:
            P.dma("gpsimd", lambda e: e.dma_start(out=walb[0:17, :], in_=walpha_d), s_wa, writes=["walb"])
            P.op("gpsimd", lambda e: e.memset(gaT[0:32, :], 1.0), writes=["gaT"])
            wa = load_w(wring, 7168, 16)
            fm(wring, wa, 16, lambda ap, c0, m, bn: P.op("scalar", lambda e: e.copy(gaT[0:16, c0:c0 + m], ap), reads=[bn], writes=["gaT"]))
        for g in range(NG if NH_SB == 8 else min(NG, 1)):
            wq = load_w(wring, 4096 + g * 128, 128)
            wk = load_w(wring, 4608 + g * 128, 128)
            fm(wring, wq, 128, lambda ap, c0, m, bn: P.op("scalar", lambda e: e.copy(gqT[:, c0:c0 + m], ap), reads=[bn], writes=["gqT"]))
            fm(wring, wk, 128, lambda ap, c0, m, bn: P.op("scalar", lambda e: e.copy(gkT[:, c0:c0 + m], ap), reads=[bn], writes=["gkT"]))
            tm(wring, wk, 128, lambda ap, t, bn: P.op("scalar", lambda e: e.copy(gk_tm[:TILES[t][1], t, :], ap), reads=[bn], writes=["gk_tm"]))
            for jj in range(2):
                wv = load_w(wring, 5120 + g * 256 + jj * 128, 128)
                tm(wring, wv, 128, lambda ap, t, bn, jj=jj: P.op("scalar", lambda e: e.copy(gv_tm[:TILES[t][1], t, jj * 128:(jj + 1) * 128], ap),
                                                              reads=[bn], writes=["gv_tm"]))
            P.op("vector", lambda e: e.memset(Sf[:, :], 0.0), writes=["Sf"])
            P.op("vector", lambda e: e.memset(Sb[:, :], 0.0), writes=["Sb"])
            for t, (t0, n) in enumerate(TILES):
                samp = (t == 17)
                cTU, cTL, cCA = (C_TUS, C_TLS, C_CAS) if samp else (C_TU, C_TL, C_CA)
                P.op("tensor", lambda e, t0=t0, n=n, g=g: e.matmul(pb[3][:n, 128:256], lhsT=gaT[0:17, t0:t0 + n], rhs=walb[0:17, g * 128:(g + 1) * 128],
                                                                  start=True, stop=True), reads=["gaT", "walb"], writes=["pb3"])
                P.op("scalar", lambda e, n=n: e.activation(out=lsp[:n, :], in_=pb[3][:n, 128:256], func=AF.Exp, scale=-1.0), reads=["pb3"], writes=["lsp"])
                P.op("scalar", lambda e, n=n: e.activation(out=lsp[:n, :], in_=lsp[:n, :], func=AF.Ln, bias=1.0, scale=1.0), reads=["lsp"], writes=["lsp"])
                P.op("tensor", lambda e, n=n, cTU=cTU: e.matmul(pb[2][:, 0:n], lhsT=lsp[:n, :], rhs=cst[:n, cTU:cTU + n], start=True, stop=True),
                     reads=["lsp", "cst"], writes=["pb2"])
                P.op("tensor", lambda e, n=n, cTL=cTL: e.matmul(pb[2][:n, 128:256], lhsT=cst[:n, cTL:cTL + n], rhs=lsp[:n, :], start=True, stop=True),
                     reads=["lsp", "cst"], writes=["pb2"])
                P.op("scalar", lambda e, n=n: e.activation(out=eb[:, :n], in_=pb[2][:, 0:n], func=AF.Exp), reads=["pb2"], writes=["eb"])
                P.op("scalar", lambda e, n=n: e.activation(out=enb[:, :n], in_=pb[2][:, 0:n], func=AF.Exp, scale=-1.0), reads=["pb2"], writes=["enb"])
                P.op("scalar", lambda e, n=n: e.activation(out=rex[:n, :], in_=pb[2][:n, 128:256], func=AF.Exp), reads=["pb2"], writes=["rex"])
                P.op("vector", lambda e, t0=t0, n=n: e.scalar_tensor_tensor(qt[:, :n], gqT[:, t0:t0 + n], SCALE, eb[:, :n], ALU.mult, ALU.mult),
                     reads=["gqT", "eb"], writes=["qt"])
                P.op("vector", lambda e, t0=t0, n=n: e.tensor_tensor(kt[:, :n], gkT[:, t0:t0 + n], enb[:, :n], ALU.mult), reads=["gkT", "enb"], writes=["kt"])
                P.op("vector", lambda e, t=t, n=n: e.tensor_tensor(khat[:n, :], gk_tm[:n, t, :], rex[:n, :], ALU.mult), reads=["gk_tm", "rex"], writes=["khat"])
                P.op("tensor", lambda e, n=n: e.matmul(pb[3][:n, 0:n], lhsT=kt[:, :n], rhs=qt[:, :n], start=True, stop=True), reads=["kt", "qt"], writes=["pb3"])
                P.op("vector", lambda e, n=n, cCA=cCA: e.tensor_tensor(attn[:n, :n], pb[3][:n, 0:n], cst[:n, cCA:cCA + n], ALU.mult),
                     reads=["pb3", "cst"], writes=["attn"])
                for j in range(2):
                    P.op("tensor", lambda e, j=j, t=t, n=n: e.matmul(pb[4 + j][:, 0:n], lhsT=gv_tm[:n, t, j * 128:(j + 1) * 128], rhs=attn[:n, :n],
                                                                    start=True, stop=False), reads=["gv_tm", "attn"], writes=["pb%d" % (4 + j)])
                if not samp:
                    chunks = [(0, 16)] if n == 16 else [(0, 64), (64, 128)]
                    for ci, (a_, b_) in enumerate(chunks):
                        last = ci == len(chunks) - 1
                        for j in range(2):
                            P.op("tensor", lambda e, j=j, a_=a_, b_=b_, last=last: e.matmul(
                                pb[4 + j][:, a_:b_], lhsT=Sb[:, j * 128:(j + 1) * 128], rhs=qt[:, a_:b_], start=False, stop=last),
                                reads=["Sb", "qt"], writes=["pb%d" % (4 + j)])
                        P.op("tensor", lambda e, a_=a_, b_=b_, t=t: e.matmul(pb[6][:, 0:256], lhsT=khat[a_:b_, :], rhs=gv_tm[a_:b_, t, :], start=True, stop=True),
                             reads=["khat", "gv_tm"], writes=["pb6"])
                        P.op("vector", lambda e, b_=b_: e.scalar_tensor_tensor(Sf[:, :], Sf[:, :], eb[:, b_ - 1:b_], pb[6][:, 0:256], ALU.mult, ALU.add),
                             reads=["Sf", "eb", "pb6"], writes=["Sf"])
                        P.op("scalar", lambda e: e.copy(Sb[:, :], Sf[:, :]), reads=["Sf"], writes=["Sb"])
                    if t == 16:
                        P.dma("sync", lambda e, g=g: e.dma_start(out=ns_p[g], in_=Sf[:, :]), s_nsp, reads=["Sf"])
                else:
                    for b in range(16):
                        jb = s0c[0] % 2
                        s0c[0] += 1
                        P.dma("sync", lambda e, jb=jb, b=b, g=g: e.dma_start(out=S0f[jb][:, :], in_=state[b, g]), s_s0[jb], writes=["S0f%d" % jb])
                        P.dma("gpsimd", lambda e, jb=jb, b=b, g=g: e.dma_start(out=S0b[jb][:, :], in_=state[b, g]), s_s0b[jb], writes=["S0b%d" % jb])
                        for j in range(2):
                            P.op("tensor", lambda e, j=j, jb=jb, b=b: e.matmul(
                                pb[4 + j][:, 4 * b:4 * b + 4], lhsT=S0b[jb][:, j * 128:(j + 1) * 128], rhs=qt[:, 4 * b:4 * b + 4], start=False, stop=(b == 15)),
                                reads=["S0b%d" % jb, "qt"], writes=["pb%d" % (4 + j)])
                        if b == 0:
                            P.op("vector", lambda e: e.tensor_tensor(khm[:NS, :, :], khat[:NS, :].unsqueeze(1).to_broadcast([NS, 16, 128]),
                                                                     cst[:NS, C_MB:C_MB + 16].unsqueeze(2).to_broadcast([NS, 16, 128]), ALU.mult),
                                 reads=["khat", "cst"], writes=["khm"])
                        P.op("tensor", lambda e, b=b: e.matmul(pb[6][:, 0:256], lhsT=khm[:NS, b, :], rhs=gv_tm[:NS, 17, :], start=True, stop=True),
                             reads=["khm", "gv_tm"], writes=["pb6"])
                        P.op("vector", lambda e, jb=jb, b=b: e.scalar_tensor_tensor(snew[jb][:, :], S0f[jb][:, :], eb[:, 4 * b + 3:4 * b + 4], pb[6][:, 0:256],
                                                                                    ALU.mult, ALU.add), reads=["S0f%d" % jb, "eb", "pb6"], writes=["snew%d" % jb])
                        P.dma("sync", lambda e, jb=jb, b=b, g=g: e.dma_start(out=ns_s[b, g], in_=snew[jb][:, :]), s_sn[jb], reads=["snew%d" % jb])
                if t >= 1:
                    for j in range(2):
                        P.op("scalar", lambda e, j=j, n=n: e.activation(out=osq[:, j, :n], in_=pb[4 + j][:, 0:n], func=AF.Square),
                             reads=["pb%d" % (4 + j)], writes=["osq%d" % j])
                    for j in range(2):
                        P.op("tensor", lambda e, j=j, n=n: e.matmul(pb[7][:, 0:n], lhsT=ones_f[:, 0:128], rhs=osq[:, j, :n], start=(j == 0), stop=(j == 1)),
                             reads=["ones_f", "osq%d" % j], writes=["pb7"])
                    P.op("scalar", lambda e, n=n: e.activation(out=rstd[:, :n], in_=pb[7][:, 0:n], func=AF.Ln, bias=EPS, scale=1.0 / 256), reads=["pb7"], writes=["rstd"])
                    P.op("scalar", lambda e, n=n: e.activation(out=rstd[:, :n], in_=rstd[:, :n], func=AF.Exp, scale=-0.5), reads=["rstd"], writes=["rstd"])
                    for j in range(2):
                        kidx = 8 + 2 * g + j
                        P.op("vector", lambda e, j=j, n=n, kidx=kidx, t0=t0: e.scalar_tensor_tensor(
                            mixT[:, kidx, t0 - NMETA:t0 - NMETA + n], pb[4 + j][:, 0:n], gnorm[:, j:j + 1], rstd[:, :n], ALU.mult, ALU.mult),
                            reads=["pb%d" % (4 + j), "gnorm", "rstd"], writes=["mixT%d" % kidx])
            for j in range(2):
                wgg = load_w(wring, 6144 + g * 256 + j * 128, 128)
                fm(wring, wgg, 128, lambda ap, c0, m, bn, kidx=8 + 2 * g + j: gate_chunk(ap, c0, m, bn, kidx, gate, gate2))
        P.barrier()
        if STAGE >= 6:
            P.dma("gpsimd", lambda e: e.dma_start(out=dbg[:, 0:NMX], in_=mixT[:, 8, :]), s_dbg, reads=["mixT8"])
            P.barrier()
        for kk in range(8 + 2 * (NG if NH_SB == 8 else min(NG, 1)), 16):
            P.op("gpsimd", lambda e, kk=kk: e.memset(mixT[:, kk, :], 0.0), writes=["mixT%d" % kk])
        if STAGE < 7:
            P.op("gpsimd", lambda e: e.memset(mixT[:, 0:8, LP - NMETA:NMX], 0.0), reads=["mixT%d" % h for h in range(8)],
                 writes=["mixT%d" % h for h in range(8)])
        P.barrier()

        if STAGE >= 7:
            AR.reset()
            ptb = AR.get(256 * 4, I32)
            idxt = AR.get(256 * 4, I32)
            iotf = AR.get(4, F32)
            NKV = 5
            Kbuf = [AR.get(2 * 1024 * 2, BF16, [2, 1024]) for _ in range(NKV)]
            Vbuf = Kbuf
            KT = [AR.get(8 * 256 * 2, BF16, [8, 256]) for _ in range(2)]
            qm = AR.get(8 * 128 * 2, BF16, [8, 128])
            osel = AR.get(8 * 128 * 4, F32, [8, 128])
            CT3 = chain_temps(znb=2, na=1)
            s_pt = P.slot("pt")
            s_kb = [P.slot("kb%d" % i) for i in range(NKV)]
            s_vb = s_kb
            P.dma("sync", lambda e: e.dma_start(out=ptb[:, :], in_=pt.rearrange("a n -> (a n)").partition_broadcast(128)), s_pt, writes=["ptb"])
            P.op("gpsimd", lambda e: e.iota(iotf[:, 0:1], pattern=[[0, 1]], base=0, channel_multiplier=1, allow_small_or_imprecise_dtypes=True),
                 writes=["iotf"])
            P.op("vector", lambda e: e.tensor_scalar(idxt[:, :], ptb[:, :], 128.0, iotf[:, 0:1], ALU.mult, ALU.add), reads=["ptb", "iotf"], writes=["idxt"])
            kctr = [0]
            vctr = [0]
            for G in range(4):
                P.op("vector", lambda e: e.memset(qm[:, :, :], 0.0), writes=["qm"])
                for h in range(8):
                    P.op("vector", lambda e, h=h, G=G: e.tensor_copy(
                        qm[:, h, :].rearrange("p (b x) -> p b x", b=4)[:, :, 4 * h:4 * h + 4],
                        sqT_s[:, h, 16 * G:16 * G + 16].rearrange("p (b q) -> p b q", b=4)), reads=["sqT_s"], writes=["qm"])

                def zfill(c, c0, m, bank, bn, G=G):
                    if c < 4:
                        for b4 in range(4):
                            bl = 4 * G + b4
                            for half in range(2):
                                ks = kctr[0] % NKV
                                kctr[0] += 1
                                kt_ = vctr[0] % 2
                                vctr[0] += 1
                                for p in range(2):
                                    col = bl * 16 + 4 * c + 2 * half + p
                                    P.dma("gpsimd", lambda e, ks=ks, p=p, col=col: e.indirect_dma_start(
                                        out=Kbuf[ks][:, p, :], out_offset=None, in_=ck,
                                        in_offset=bass.IndirectOffsetOnAxis(ap=idxt[:, col:col + 1], axis=0)),
                                        s_kb[ks], reads=["idxt"], writes=["Kbuf%d" % ks])
                                tb = 6 - 2 * kt_
                                for h in range(8):
                                    for p in range(2):
                                        P.op("tensor", lambda e, ks=ks, h=h, p=p, tb=tb: e.transpose(
                                            pbh[tb + h // 4][:, (h % 4) * 256 + p * 128:(h % 4) * 256 + (p + 1) * 128],
                                            Kbuf[ks][:, p, h * 128:(h + 1) * 128], identb[:, :]),
                                            reads=["Kbuf%d" % ks, "identb"], writes=["pb%d" % (tb + h // 4)])
                                P.op("scalar", lambda e, kt_=kt_, tb=tb: e.copy(KT[kt_][:, 0:4, :], pbh[tb].rearrange("p (a b) -> p a b", a=4)),
                                     reads=["pb%d" % tb], writes=["KT%da" % kt_])
                                P.op("vector", lambda e, kt_=kt_, tb=tb: e.tensor_copy(KT[kt_][:, 4:8, :], pbh[tb + 1].rearrange("p (a b) -> p a b", a=4)),
                                     reads=["pb%d" % (tb + 1)], writes=["KT%db" % kt_])
                                for h in range(8):
                                    P.op("tensor", lambda e, kt_=kt_, h=h, b4=b4, half=half: e.matmul(
                                        bank[32 * b4:32 * b4 + 32, half * 256:(half + 1) * 256], lhsT=qm[:, h, 32 * b4:32 * b4 + 32],
                                        rhs=KT[kt_][:, h, :], start=(h == 0), stop=(h == 7), tile_position=(0, 32 * b4)),
                                        reads=["qm", "KT%d%s" % (kt_, "a" if h < 4 else "b")], writes=[bn])
                    else:
                        for h in range(8):
                            P.op("tensor", lambda e, h=h: e.matmul(bank[:, 0:NS], lhsT=qm[:, h, :], rhs=skT_s[:, h, :], start=(h == 0), stop=False),
                                 reads=["qm", "skT_s"], writes=[bn])
                        P.op("tensor", lambda e: e.matmul(bank[:, 0:NS], lhsT=identb[:, :], rhs=maskSb[:, 64 * G:64 * G + 64], start=False, stop=True),
                             reads=["identb", "maskSb"], writes=[bn])

                sb_chain(CT3, 128, 2048 + NS, zfill, sbrow[:, 0:1], "sbrow")
                blocks = [(128 * j, 128) for j in range(16)] + [(2048, NS)]
                transpose_blocks(CT3, 128, blocks)
                for b4 in range(4):
                    bl = 4 * G + b4
                    for jp in range(8):
                        vs = kctr[0] % NKV
                        kctr[0] += 1
                        for p in range(2):
                            col = bl * 16 + 2 * jp + p
                            P.dma("gpsimd", lambda e, vs=vs, p=p, col=col: e.indirect_dma_start(
                                out=Vbuf[vs][:, p, :], out_offset=None, in_=cv,
                                in_offset=bass.IndirectOffsetOnAxis(ap=idxt[:, col:col + 1], axis=0)),
                                s_vb[vs], reads=["idxt"], writes=["Kbuf%d" % vs])
                        for p in range(2):
                            j = 2 * jp + p
                            for hb in range(2):
                                P.op("tensor", lambda e, vs=vs, p=p, j=j, hb=hb, b4=b4: e.matmul(
                                    pb[hb][32 * b4:32 * b4 + 32, :], lhsT=CT3["aT"][:, j, 32 * b4:32 * b4 + 32], rhs=Vbuf[vs][:, p, hb * 512:(hb + 1) * 512],
                                    start=(j == 0), stop=False, tile_position=(0, 32 * b4)),
                                    reads=["Kbuf%d" % vs, "aT%d" % (j // 8)], writes=["pb%d" % hb])
                    for hb in range(2):
                        P.op("tensor", lambda e, hb=hb, b4=b4: e.matmul(
                            pb[hb][32 * b4:32 * b4 + 32, :], lhsT=CT3["aT"][:NS, 16, 32 * b4:32 * b4 + 32],
                            rhs=sv_s[:, 4 * hb:4 * hb + 4, :].rearrange("p a b -> p (a b)"), start=False, stop=True, tile_position=(0, 32 * b4)),
                            reads=["sv_s", "aT2"], writes=["pb%d" % hb])
                for hb in range(2):
                    P.op("vector", lambda e, hb=hb: e.tensor_tensor(
                        osel[:, 4 * hb:4 * hb + 4, :], pb[hb][:, :].rearrange("p (a b) -> p a b", a=4),
                        cst[:, C_HM + 4 * hb:C_HM + 4 * hb + 4].unsqueeze(2).to_broadcast([128, 4, 128]), ALU.mult),
                        reads=["pb%d" % hb, "cst"], writes=["osel%d" % hb])
                for h in range(8):
                    P.op("tensor", lambda e, h=h: e.matmul(pb[2][:, 16 * h:16 * h + 16], lhsT=osel[:, h, :], rhs=cst[:, C_RS:C_RS + 16], start=True, stop=True),
                         reads=["osel%d" % (h // 4), "cst"], writes=["pb2"])
                P.op("vector", lambda e, G=G: e.tensor_tensor(
                    mixT[:, 0:8, LP - NMETA + 16 * G:LP - NMETA + 16 * G + 16], pb[2][:, 0:128].rearrange("p (a b) -> p a b", a=8),
                    sgate_s[:, :, 16 * G:16 * G + 16], ALU.mult), reads=["pb2", "sgate_s"], writes=["mixT%d" % h for h in range(8)])
            P.barrier()
            P.dma("gpsimd", lambda e: e.dma_start(out=dbg[:, 0:NMX], in_=mixT[:, 0, :]), s_dbg, reads=["mixT0"])
            P.barrier()

        if STAGE == 77:
            for kk in range(16):
                for hf in range(2):
                    P.dma("gpsimd", lambda e, kk=kk, hf=hf: e.dma_start(out=dbg2[:, kk, hf * 1056:(hf + 1) * 1056], in_=mixT[:, kk, hf * 1056:(hf + 1) * 1056]),
                          s_dbg, reads=["mixT%d" % kk])
            P.barrier()

        AR.reset()
        xo = [AR.get(D * 4, F32) for _ in range(2)]
        yo = [AR.get(D * 4, F32) for _ in range(2)]
        gpost = AR.get(D * 4, F32)
        s_wo = P.slot("wo")
        s_gp = P.slot("gp")
        s_xo = [P.slot("xo0"), P.slot("xo1")]
        s_yo = [P.slot("yo0"), P.slot("yo1")]
        for q4 in range(4):
            P.dma("gpsimd", lambda e, q4=q4: e.dma_start(out=wout[:, 4 * q4:4 * q4 + 4, :], in_=w_out_v[:, 4 * q4:4 * q4 + 4, :]),
                  s_wo, writes=["wout"])
        P.dma("sync", lambda e: e.dma_start(out=gpost[:, :], in_=gpost_d.partition_broadcast(128)), s_gp, writes=["gpost"])
        for t in range(1, 18 if STAGE >= 5 else 1):
            t0, n = TILES[t]
            j = t % 2
            m0 = t0 - NMETA
            P.dma("sync", lambda e, j=j, t0=t0, n=n: e.dma_start(out=xo[j][:n, :], in_=x_all[t0:t0 + n, :]), s_xo[j], writes=["xo%d" % j])
            for dc in range(4):
                bk = 4 * j + dc
                for k in range(KC):
                    P.op("tensor", lambda e, bk=bk, k=k, dc=dc, m0=m0, n=n: e.matmul(
                        pb[bk][:n, :], lhsT=mixT[:, k, m0:m0 + n], rhs=wout[:, k, dc * 512:(dc + 1) * 512], start=(k == 0), stop=(k == KC - 1)),
                        reads=["wout"] + ["mixT%d" % kk for kk in range(16)], writes=["pb%d" % bk])
                P.op("scalar", lambda e, bk=bk, dc=dc, j=j, n=n: e.activation(out=yo[j][:n, dc * 512:(dc + 1) * 512], in_=pb[bk][:n, :],
                                                                           func=AF.Square, accum_out=stat[:n, 8 + dc:9 + dc]),
                     reads=["pb%d" % bk], writes=["yo%d" % j, "st8%d" % dc])
            P.op("vector", lambda e, n=n: e.reduce_sum(stat[:n, 12:13], stat[:n, 8:12], axis=mybir.AxisListType.X),
                 reads=["st8%d" % dc for dc in range(4)], writes=["st12"])
            P.op("scalar", lambda e, n=n: e.activation(out=stat[:n, 13:14], in_=stat[:n, 12:13], func=AF.Ln, bias=EPS, scale=1.0 / D),
                 reads=["st12"], writes=["st13"])
            P.op("scalar", lambda e, n=n: e.activation(out=stat[:n, 14:15], in_=stat[:n, 13:14], func=AF.Exp, scale=-0.5),
                 reads=["st13"], writes=["st14"])
            for dc in range(4):
                bk = 4 * j + dc
                P.op("vector", lambda e, bk=bk, dc=dc, j=j, n=n: e.scalar_tensor_tensor(
                    yo[j][:n, dc * 512:(dc + 1) * 512], pb[bk][:n, :], stat[:n, 14:15], gpost[:n, dc * 512:(dc + 1) * 512], ALU.mult, ALU.mult),
                    reads=["pb%d" % bk, "st14", "gpost"], writes=["yo%d" % j])
            P.op("gpsimd", lambda e, j=j, n=n: e.tensor_tensor(yo[j][:n, :], yo[j][:n, :], xo[j][:n, :], ALU.add),
                 reads=["yo%d" % j, "xo%d" % j], writes=["yo%d" % j])
            dst = y_p[t0 - NMETA:t0 - NMETA + n, :] if t < 17 else y_s[:, :]
            P.dma("sync", lambda e, j=j, n=n, dst=dst: e.dma_start(out=dst, in_=yo[j][:n, :]), s_yo[j], reads=["yo%d" % j])

        P.emit(nc)
    return nc


def make_consts():
    c = np.zeros((128, NCST), np.float32)
    i = np.arange(128)
    c[:, C_ID:C_ID + 128] = np.eye(128, dtype=np.float32)
    c[:, C_MA:C_MA + 128] = np.where(i[None, :] < i[:, None], 0.0, NEG)
    b4, hh, qq = i // 32, (i // 4) % 8, i % 4
    col = np.arange(64)
    for G in range(4):
        ok = ((col[None, :] // 4) == (4 * G + b4)[:, None]) & ((col[None, :] % 4) < qq[:, None])
        c[:, C_MS + 64 * G:C_MS + 64 * (G + 1)] = np.where(ok, 0.0, NEG)
    same = (i[:, None] // 64) == (i[None, :] // 64)
    c[:, C_TU:C_TU + 128] = np.where(same & (i[:, None] <= i[None, :]), -1.0 / 16, 0.0)
    c[:, C_TL:C_TL + 128] = np.where(same & (i[:, None] > i[None, :]), -1.0 / 16, 0.0)
    c[:, C_CA:C_CA + 128] = np.where(same & (i[:, None] <= i[None, :]), 1.0, 0.0)
    j = np.arange(64)
    same4 = (j[:, None] // 4) == (j[None, :] // 4)
    c[:64, C_TUS:C_TUS + 64] = np.where(same4 & (j[:, None] <= j[None, :]), -1.0 / 16, 0.0)
    c[:64, C_TLS:C_TLS + 64] = np.where(same4 & (j[:, None] > j[None, :]), -1.0 / 16, 0.0)
    c[:64, C_CAS:C_CAS + 64] = np.where(same4 & (j[:, None] <= j[None, :]), 1.0, 0.0)
    c[:64, C_MB:C_MB + 16] = (j[:, None] // 4 == np.arange(16)[None, :])
    c[:, C_HM:C_HM + 8] = (hh[:, None] == np.arange(8)[None, :])
    c[:, C_RS:C_RS + 16] = ((b4 * 4 + qq)[:, None] == np.arange(16)[None, :])
    return c


_NC_CACHE = {}


def kernel(x_prompt, x_sample, cache_k, cache_v, state_gla, page_table, meta_tokens,
           norm_pre_g, w_in, sb_bias, w_alpha, b_alpha, gla_norm_g, w_out, norm_post_g):
    f32 = lambda a: np.ascontiguousarray(np.asarray(a, dtype=np.float32))
    x_prompt, x_sample, meta_tokens = f32(x_prompt), f32(x_sample), f32(meta_tokens)
    ckr = f32(cache_k).reshape(NPOOL_ROWS, 1024)
    cvr = f32(cache_v).reshape(NPOOL_ROWS, 1024)
    state_gla = f32(state_gla)
    page_table = np.ascontiguousarray(np.asarray(page_table, dtype=np.int32))
    w_in_ = f32(w_in)[0]
    w_out_ = f32(w_out)[0]
    gpre = f32(norm_pre_g)[0].reshape(KC, 128).T.copy()
    gpost = f32(norm_post_g)[0].reshape(D)
    sbias = f32(sb_bias)[0].reshape(8)
    sbrow = np.ascontiguousarray(np.tile(np.repeat(sbias, 4), 4).reshape(128, 1))
    walpha = np.concatenate([f32(w_alpha)[0], f32(b_alpha)[0].reshape(1, 512)], axis=0)
    gnorm = f32(gla_norm_g)[0].reshape(2, 128).T.copy()
    cst = make_consts()
    if "nc" not in _NC_CACHE:
        _NC_CACHE["nc"] = build_program()
    nc = _NC_CACHE["nc"]
    in_maps = []
    for c in range(8):
        x_all = np.concatenate([meta_tokens, x_prompt[c], x_sample[16 * c:16 * c + 16].reshape(NS, D)], axis=0)
        in_maps.append({
            "x_all": x_all, "ck": ckr, "cv": cvr,
            "pt": page_table[16 * c:16 * c + 16].reshape(1, 256),
            "state": state_gla[0, 16 * c:16 * c + 16],
            "w_in": w_in_, "w_out": w_out_, "gpre": gpre, "gpost": gpost, "sbias": sbias, "sbrow": sbrow,
            "walpha": walpha, "gnorm": gnorm, "cst": cst,
        })
    res = run_bass_kernel_spmd(nc, in_maps, core_ids=list(range(8)))
    R = res.results
    y_prompt = np.stack([R[c]["y_p"] for c in range(8)], axis=0)
    y_sample = np.concatenate([R[c]["y_s"].reshape(16, 4, D) for c in range(8)], axis=0)
    nk_p = np.stack([R[c]["nk_p"].reshape(LP, 8, 128) for c in range(8)], axis=0)[None]
    nv_p = np.stack([R[c]["nv_p"].reshape(LP, 8, 128) for c in range(8)], axis=0)[None]
    ns_p = np.stack([R[c]["ns_p"] for c in range(8)], axis=0)[None]
    nk_s = np.concatenate([R[c]["nk_s"].reshape(16, 4, 8, 128) for c in range(8)], axis=0)[None]
    nv_s = np.concatenate([R[c]["nv_s"].reshape(16, 4, 8, 128) for c in range(8)], axis=0)[None]
    ns_s = np.concatenate([R[c]["ns_s"] for c in range(8)], axis=0)[None]
    return (y_prompt, y_sample, nk_p, nv_p, ns_p, nk_s, nv_s, ns_s)
```

```python
import contextlib
import numpy as np
import concourse.bass as bass
import concourse.mybir as mybir
from concourse.bass_utils import run_bass_kernel_spmd

F32 = mybir.dt.float32
BF16 = mybir.dt.bfloat16
I32 = mybir.dt.int32
U8 = mybir.dt.uint8
AF = mybir.ActivationFunctionType
ALU = mybir.AluOpType

D = 2048
KC = 16
NMETA = 16
SEQ = 2048
LP = NMETA + SEQ
NS = 64
NT = LP + NS
NIN = 7184
NPOOL_ROWS = 2560 * 128
EPS = 1e-6
NEG = -30000.0
SCALE = 128 ** -0.5
TILES = [(0, 16)] + [(16 + 128 * i, 128) for i in range(16)] + [(LP, NS)]
TCH = [(0, 512), (512, 512), (1024, 512), (1536, 512), (2048, NT - 2048)]
ENGS = ["sync", "scalar", "gpsimd", "vector", "tensor"]

C_ID, C_MA, C_MS, C_TU, C_TL, C_CA, C_TUS, C_TLS, C_CAS, C_MB, C_HM, C_RS = (
    0, 128, 256, 512, 640, 768, 896, 960, 1024, 1088, 1104, 1112)
NCST = 1128
STAGE = 99
SUB = 'abcde'
NH_SB = 8


class Op:
    __slots__ = ("eng", "fn", "deps", "needed", "count", "dma", "dtok")

    def __init__(self, eng, fn):
        self.eng, self.fn = eng, fn
        self.deps, self.needed, self.count, self.dma, self.dtok = [], False, None, None, None


class DSlot:
    def __init__(self, name):
        self.name, self.count, self.sem = name, 0, None


class Prog:
    def __init__(self):
        self.q = {e: [] for e in ENGS}
        self.lastw, self.readers, self.slots = {}, {}, []
        self.nsig = {e: 0 for e in ENGS}
        self.dma_ops = []

    def slot(self, name):
        s = DSlot(name)
        self.slots.append(s)
        return s

    def _adddep(self, op, d):
        if d is op:
            return
        if d.dma is None and op.dma is None and d.eng == "tensor" and op.eng == "tensor":
            return
        op.deps.append((d, d.dma.count if d.dma is not None else None))
        if d.dma is None:
            d.needed = True

    def _track(self, op, reads, writes):
        writes = list(writes) + [r for r in reads if r.startswith("pb")]
        reads = [r for r in reads if not r.startswith("pb")]
        deps = []
        for r in reads:
            w = self.lastw.get(r)
            if w is not None:
                deps.append(w)
        for w_ in writes:
            w = self.lastw.get(w_)
            if w is not None:
                deps.append(w)
            deps.extend(self.readers.get(w_, ()))
        seen = set()
        for d in deps:
            if id(d) in seen:
                continue
            seen.add(id(d))
            self._adddep(op, d)
        for r in reads:
            self.readers.setdefault(r, []).append(op)
        for w_ in writes:
            self.lastw[w_] = op
            self.readers[w_] = []

    def op(self, eng, fn, reads=(), writes=()):
        o = Op(eng, fn)
        self._track(o, reads, writes)
        self.q[eng].append(o)
        return o

    def dma(self, eng, fn, slot, reads=(), writes=()):
        o = Op(eng, fn)
        o.dma = slot
        self._track(o, reads, writes)
        slot.count += 16
        o.dtok = slot.count
        self.q[eng].append(o)
        self.dma_ops.append(o)
        return o

    def barrier(self):
        lasts = []
        for e in ENGS:
            for o in reversed(self.q[e]):
                if o.dma is None and o.fn is not None:
                    lasts.append(o)
                    break
        dmas = list(self.dma_ops)
        self.dma_ops = []
        for e in ENGS:
            b = Op(e, None)
            for d in lasts:
                if d.eng != e:
                    b.deps.append((d, None))
                    d.needed = True
            for d in dmas:
                b.deps.append((d, d.dma.count))
            self.q[e].append(b)
        self.lastw, self.readers = {}, {}

    def emit(self, nc):
        for e in ENGS:
            c = 0
            for o in self.q[e]:
                if o.dma is None and o.needed:
                    c += 1
                    o.count = c
            self.nsig[e] = c
        with contextlib.ExitStack() as st:
            esem = {e: st.enter_context(nc.semaphore("es_" + e)) for e in ENGS}
            for s in self.slots:
                s.sem = st.enter_context(nc.semaphore("ds_" + s.name))
            block = st.enter_context(nc.Block())
            prog = self

            def make(ename):
                def body(eng):
                    seen = {}
                    for o in prog.q[ename]:
                        need = {}
                        for d, dval in o.deps:
                            if d.dma is not None:
                                key, val, sem = ("d", id(d.dma)), dval, d.dma.sem
                            else:
                                key, val, sem = ("e", d.eng), d.count, esem[d.eng]
                            if need.get(key, (None, -1))[1] < val:
                                need[key] = (sem, val)
                        for key, (sem, val) in need.items():
                            if seen.get(key, -1) >= val:
                                continue
                            seen[key] = val
                            eng.wait_ge(sem, val)
                        if o.fn is None:
                            continue
                        ins = o.fn(eng)
                        if o.dma is not None:
                            ins.then_inc(o.dma.sem, 16)
                        elif o.needed:
                            ins.then_inc(esem[ename], 1)
                    if ename == "sync":
                        for s in prog.slots:
                            if s.count > 0:
                                eng.wait_ge(s.sem, s.count)
                        for e2 in ENGS:
                            if prog.nsig[e2] > 0:
                                eng.wait_ge(esem[e2], prog.nsig[e2])
                return body

            block.sync(make("sync"))
            block.scalar(make("scalar"))
            block.gpsimd(make("gpsimd"))
            block.vector(make("vector"))
            block.tensor(make("tensor"))


class Arena:
    def __init__(self, t, nbytes):
        self.t, self.n, self.off = t, nbytes, 0

    def reset(self):
        self.off = 0

    def get(self, nbytes, dt, shape=None):
        o = (self.off + 31) // 32 * 32
        assert o + nbytes <= self.n, ("arena overflow", o + nbytes, self.n)
        self.off = o + nbytes
        v = self.t[:, o:o + nbytes].bitcast(dt)
        if shape is not None:
            names = " ".join("d%d" % i for i in range(len(shape)))
            v = v.rearrange("p (%s) -> p %s" % (names, names), **{"d%d" % i: s for i, s in enumerate(shape)})
        return v


def build_program():
    nc = bass.Bass("TRN2", target_bir_lowering=False)
    dram = lambda n, s, d, k: nc.dram_tensor(n, s, d, kind=k).ap()
    x_all = dram("x_all", [NT, D], F32, "ExternalInput")
    ck = dram("ck", [NPOOL_ROWS, 1024], F32, "ExternalInput")
    cv = dram("cv", [NPOOL_ROWS, 1024], F32, "ExternalInput")
    pt = dram("pt", [1, 256], I32, "ExternalInput")
    state = dram("state", [16, 4, 128, 256], F32, "ExternalInput")
    w_in = dram("w_in", [D, NIN], F32, "ExternalInput")
    w_out = dram("w_out", [D, D], F32, "ExternalInput")
    gpre_d = dram("gpre", [128, KC], F32, "ExternalInput")
    gpost_d = dram("gpost", [D], F32, "ExternalInput")
    sbias_d = dram("sbias", [8], F32, "ExternalInput")
    sbrow_d = dram("sbrow", [128, 1], F32, "ExternalInput")
    walpha_d = dram("walpha", [17, 512], F32, "ExternalInput")
    gnorm_d = dram("gnorm", [128, 2], F32, "ExternalInput")
    cst_d = dram("cst", [128, NCST], F32, "ExternalInput")
    y_p = dram("y_p", [SEQ, D], F32, "ExternalOutput")
    y_s = dram("y_s", [NS, D], F32, "ExternalOutput")
    nk_p = dram("nk_p", [LP, 1024], F32, "ExternalOutput")
    nv_p = dram("nv_p", [LP, 1024], F32, "ExternalOutput")
    ns_p = dram("ns_p", [4, 128, 256], F32, "ExternalOutput")
    nk_s = dram("nk_s", [NS, 1024], F32, "ExternalOutput")
    nv_s = dram("nv_s", [NS, 1024], F32, "ExternalOutput")
    ns_s = dram("ns_s", [16, 4, 128, 256], F32, "ExternalOutput")
    w_in_v = w_in.rearrange("(k p) n -> p k n", p=128)
    w_out_v = w_out.rearrange("(k p) n -> p k n", p=128)

    RC_BYTES = 63488
    dbg = dram("dbg", [128, NT], F32, "ExternalOutput")
    dbg2 = dram("dbg2", [128, 16, NT - NMETA], F32, "ExternalOutput") if STAGE == 77 else None
    with contextlib.ExitStack() as st:
        sb = lambda n, s, d: st.enter_context(nc.sbuf_tensor(n, s, d))
        RA = sb("RA", [128, KC * NT], BF16)
        hT = RA[:, :].rearrange("p (k t) -> p k t", k=KC)
        wout = RA[:, 0:KC * D].rearrange("p (k n) -> p k n", k=KC)
        RB = sb("RB", [128, KC * (NT - NMETA)], BF16)
        mixT = RB[:, :].rearrange("p (k t) -> p k t", k=KC)
        NMX = NT - NMETA
        cst = sb("cst_t", [128, NCST], F32)
        identb = sb("identb", [128, 128], BF16)
        maskab = sb("maskab", [128, 128], BF16)
        maskSb = sb("maskSb", [128, 256], BF16)
        ones_f = sb("ones_f", [128, 512], F32)
        gpre = sb("gpre_t", [128, KC], F32)
        sbias = sb("sbias_t", [128, 8], F32)
        sbrow = sb("sbrow_t", [128, 1], F32)
        gnorm = sb("gnorm_t", [128, 2], F32)
        stat = sb("stat", [128, 16], F32)
        bh = sb("bh", [128, 1], F32)
        sqT_s = sb("sqT_s", [128, 8, NS], BF16)
        skT_s = sb("skT_s", [128, 8, NS], BF16)
        sv_s = sb("sv_s", [NS, 8, 128], BF16)
        sgate_s = sb("sgate_s", [128, 8, NS], BF16)
        arena_t = sb("arena", [128, RC_BYTES], U8)
        AR = Arena(arena_t, RC_BYTES)
        pb = [st.enter_context(nc.psum_tensor("pb%d" % i, [128, 512], F32)) for i in range(8)]
        pbh = [p[:, :].bitcast(BF16) for p in pb]

        P = Prog()
        ident_f = cst[:, C_ID:C_ID + 128]

        s_c = P.slot("cst")
        P.dma("sync", lambda e: e.dma_start(out=cst[:, :], in_=cst_d), s_c, writes=["cst"])
        P.dma("sync", lambda e: e.dma_start(out=gpre[:, :], in_=gpre_d), s_c, writes=["gpre"])
        P.dma("sync", lambda e: e.dma_start(out=sbias[:, :], in_=sbias_d.partition_broadcast(128)), s_c, writes=["sbias"])
        P.dma("sync", lambda e: e.dma_start(out=sbrow[:, :], in_=sbrow_d), s_c, writes=["sbrow"])
        P.dma("sync", lambda e: e.dma_start(out=gnorm[:, :], in_=gnorm_d), s_c, writes=["gnorm"])
        P.op("vector", lambda e: e.tensor_copy(identb[:, :], cst[:, C_ID:C_ID + 128]), reads=["cst"], writes=["identb"])
        P.op("vector", lambda e: e.tensor_copy(maskab[:, :], cst[:, C_MA:C_MA + 128]), reads=["cst"], writes=["maskab"])
        P.op("vector", lambda e: e.tensor_copy(maskSb[:, :], cst[:, C_MS:C_MS + 256]), reads=["cst"], writes=["maskSb"])
        P.op("gpsimd", lambda e: e.memset(ones_f[:, :], 1.0), writes=["ones_f"])

        AR.reset()
        xt = [AR.get(D * 4, F32) for _ in range(2)]
        xn = [AR.get(D * 2, BF16) for _ in range(2)]
        s_x = [P.slot("x0"), P.slot("x1")]
        for t, (t0, n) in enumerate(TILES):
            j = t % 2
            P.dma("sync", lambda e, j=j, t0=t0, n=n: e.dma_start(out=xt[j][:n, :], in_=x_all[t0:t0 + n, :]), s_x[j],
                  writes=["xt%d" % j])
            P.op("scalar", lambda e, j=j, n=n: e.activation(out=xn[j][:n, :], in_=xt[j][:n, :], func=AF.Square,
                                                           accum_out=stat[:n, 0:1]),
                 reads=["xt%d" % j], writes=["xn%d" % j, "st0"])
            P.op("scalar", lambda e, n=n: e.activation(out=stat[:n, 1:2], in_=stat[:n, 0:1], func=AF.Ln, bias=EPS, scale=1.0 / D),
                 reads=["st0"], writes=["st1"])
            P.op("scalar", lambda e, n=n: e.activation(out=stat[:n, 2:3], in_=stat[:n, 1:2], func=AF.Exp, scale=-0.5),
                 reads=["st1"], writes=["st2"])
            P.op("vector", lambda e, j=j, n=n: e.tensor_scalar(xn[j][:n, :], xt[j][:n, :], stat[:n, 2:3], None, ALU.mult),
                 reads=["xt%d" % j, "st2"], writes=["xn%d" % j])
            for half in range(2):
                bk = 2 * j + half
                for kk in range(8):
                    k = 8 * half + kk
                    P.op("tensor", lambda e, bk=bk, kk=kk, k=k, j=j, n=n: e.transpose(
                        pbh[bk][:, kk * 128:kk * 128 + n], xn[j][:n, k * 128:(k + 1) * 128], identb[:n, :n]),
                        reads=["xn%d" % j, "identb"], writes=["pb%d" % bk])
                P.op("vector", lambda e, bk=bk, half=half, t0=t0, n=n: e.tensor_tensor(
                    hT[:, 8 * half:8 * half + 8, t0:t0 + n],
                    pbh[bk].rearrange("p (a b) -> p a b", a=8)[:, :, 0:n],
                    gpre[:, 8 * half:8 * half + 8].unsqueeze(2).to_broadcast([128, 8, n]), ALU.mult),
                    reads=["pb%d" % bk, "gpre"], writes=["hT"])
        P.barrier()
        s_dbg = P.slot("dbg")
        P.dma("gpsimd", lambda e: e.dma_start(out=dbg, in_=hT[:, 3, :]), s_dbg, reads=["hT"])

        s_w = [P.slot("w%d" % i) for i in range(3)]
        wctr = [0]
        ipb = [0]

        def load_w(wring, c0, ncols):
            i = wctr[0] % 3
            wctr[0] += 1
            P.dma("gpsimd", lambda e, i=i, c0=c0, ncols=ncols: e.dma_start(out=wring[i][:, :, 0:ncols], in_=w_in_v[:, :, c0:c0 + ncols]),
                  s_w[i], writes=["wr%d" % i])
            return i

        def fm(wring, wi, ncols, evac):
            for (c0, m) in TCH:
                bk = ipb[0] % 2
                ipb[0] += 1
                for k in range(KC):
                    P.op("tensor", lambda e, bk=bk, k=k, c0=c0, m=m: e.matmul(
                        pb[bk][:ncols, :m], lhsT=wring[wi][:, k, 0:ncols], rhs=hT[:, k, c0:c0 + m], start=(k == 0), stop=(k == KC - 1)),
                        reads=["wr%d" % wi, "hT"], writes=["pb%d" % bk])
                evac(pb[bk][:ncols, :m], c0, m, "pb%d" % bk)

        def tm(wring, wi, ncols, evac):
            for t, (t0, n) in enumerate(TILES):
                bk = ipb[0] % 2
                ipb[0] += 1
                for k in range(KC):
                    P.op("tensor", lambda e, bk=bk, k=k, t0=t0, n=n: e.matmul(
                        pb[bk][:n, :ncols], lhsT=hT[:, k, t0:t0 + n], rhs=wring[wi][:, k, 0:ncols], start=(k == 0), stop=(k == KC - 1)),
                        reads=["wr%d" % wi, "hT"], writes=["pb%d" % bk])
                evac(pb[bk][:n, :ncols], t, "pb%d" % bk)

        def sb_chain_A(T_, n, K, zfill, bias_ap, bias_reg):
            nch = (K + 511) // 512
            ai = T_["actr"][0] % len(T_["a2"])
            T_["actr"][0] += 1
            a_buf = T_["a2"][ai]
            an = "arow%d" % ai
            for c in range(nch):
                c0 = c * 512
                m = min(512, K - c0)
                bk = 2 + (T_["zctr"][0] % T_["znb"])
                T_["zctr"][0] += 1
                bn = "pb%d" % bk
                zfill(c, c0, m, pb[bk], bn)
                e_ = T_["esp"][c % 2]
                en = "esp%d" % (c % 2)
                pc = T_["pc"][c % 2]
                pn = "pc%d" % (c % 2)
                P.op("scalar", lambda e, e_=e_, bk=bk, m=m: e.activation(out=e_[:n, :m], in_=pb[bk][:n, :m], func=AF.Exp,
                                                                       bias=bias_ap, scale=SCALE),
                     reads=[bn, bias_reg], writes=[en])
                P.op("scalar", lambda e, e_=e_, m=m: e.activation(out=e_[:n, :m], in_=e_[:n, :m], func=AF.Identity, bias=1.0, scale=1.0),
                     reads=[en], writes=[en])
                if c == 0:
                    P.op("vector", lambda e, pc=pc: e.memset(pc[:n, 0:1], 1.0), writes=[pn])
                else:
                    pp = T_["pc"][(c - 1) % 2]
                    P.op("vector", lambda e, pc=pc, pp=pp: e.tensor_copy(pc[:n, 0:1], pp[:n, 512:513]),
                         reads=["pc%d" % ((c - 1) % 2)], writes=[pn])
                P.op("vector", lambda e, pc=pc, e_=e_, m=m: e.tensor_tensor_scan(
                    pc[:n, 1:m + 1], e_[:n, :m], ones_f[:n, :m], pc[:n, 0:1], ALU.mult, ALU.mult),
                    reads=[pn, en, "ones_f"], writes=[pn])
                P.op("vector", lambda e, pc=pc, c0=c0, m=m: e.tensor_tensor(
                    T_["w"][:n, c0:c0 + m], pc[:n, 1:m + 1], pc[:n, 0:m], ALU.subtract),
                    reads=[pn], writes=["wrow"])
            return {"n": n, "K": K, "nch": nch, "a": a_buf, "an": an}

        def sb_chain_B(T_, info):
            n, K, nch, a_buf, an = info["n"], info["K"], info["nch"], info["a"], info["an"]
            ml = K - (nch - 1) * 512
            pl = T_["pc"][(nch - 1) % 2]
            P.op("vector", lambda e: e.reciprocal(stat[:n, 4:5], pl[:n, ml:ml + 1]),
                 reads=["pc%d" % ((nch - 1) % 2)], writes=["st4"])
            P.op("vector", lambda e: e.tensor_scalar(a_buf[:n, :K], T_["w"][:n, :K], stat[:n, 4:5], None, ALU.mult),
                 reads=["wrow", "st4"], writes=[an])

        def sb_chain(T_, n, K, zfill, bias_ap, bias_reg):
            info = sb_chain_A(T_, n, K, zfill, bias_ap, bias_reg)
            sb_chain_B(T_, info)
            T_["a"], T_["an"] = info["a"], info["an"]

        def chain_temps(znb=4, na=2):
            return {"esp": [AR.get(512 * 4, F32) for _ in range(2)],
                    "pc": [AR.get(513 * 4, F32) for _ in range(2)],
                    "w": AR.get(2112 * 4, F32), "a2": [AR.get(2112 * 2, BF16) for _ in range(na)], "actr": [0], "a": None, "an": None,
                    "aT": AR.get(17 * 128 * 2, BF16, [17, 128]), "zctr": [0], "znb": znb}

        def transpose_blocks(T_, n, blocks, a_buf=None, an=None):
            if a_buf is None:
                a_buf, an = T_["a"], T_["an"]
            for g0 in range(0, len(blocks), 8):
                grp = blocks[g0:g0 + 8]
                bk = 6 + ((g0 // 8) % 2)
                for i, (col0, m) in enumerate(grp):
                    P.op("tensor", lambda e, bk=bk, i=i, col0=col0, m=m: e.transpose(
                        pbh[bk][:m, i * 128:i * 128 + n], a_buf[:n, col0:col0 + m], identb[:n, :n]),
                        reads=[an, "identb"], writes=["pb%d" % bk])
                ng = len(grp)
                P.op("scalar", lambda e, bk=bk, g0=g0, ng=ng: e.copy(
                    T_["aT"][:, g0:g0 + ng, :n], pbh[bk].rearrange("p (a b) -> p a b", a=8)[:, 0:ng, 0:n]),
                    reads=["pb%d" % bk], writes=["aT%d" % (g0 // 8)])

        def gate_chunk(ap, c0, m, bn, kidx, gate, gate2, sg_save=None):
            lo = max(c0, NMETA)
            mm = c0 + m - lo
            src = ap[:, lo - c0:lo - c0 + mm]
            P.op("scalar", lambda e: e.activation(out=gate[:, :mm], in_=src, func=AF.Exp, scale=-1.0), reads=[bn], writes=["gate"])
            P.op("vector", lambda e: e.tensor_scalar(gate[:, :mm], gate[:, :mm], 1.0, None, ALU.add), reads=["gate"], writes=["gate"])
            P.op("vector", lambda e: e.reciprocal(gate[:, :mm], gate[:, :mm]), reads=["gate"], writes=["gate"])
            P.op("vector", lambda e: e.tensor_tensor(gate2[:, :mm], src, gate[:, :mm], ALU.mult), reads=[bn, "gate"], writes=["gate2"])
            ma = mm
            if sg_save is not None and c0 + m > LP:
                ma = LP - lo
                P.op("vector", lambda e: e.tensor_copy(sgate_s[:, sg_save, :], gate2[:, ma:ma + NS]), reads=["gate2"], writes=["sgate_s"])
            P.op("vector", lambda e: e.tensor_tensor(mixT[:, kidx, lo - NMETA:lo - NMETA + ma], mixT[:, kidx, lo - NMETA:lo - NMETA + ma],
                                                     gate2[:, :ma], ALU.mult), reads=["gate2", "mixT%d" % kidx], writes=["mixT%d" % kidx])

        AR.reset()
        wring = [AR.get(KC * 128 * 2, BF16, [KC, 128]) for _ in range(3)]
        sqT = AR.get(NT * 2, BF16)
        skT = AR.get(NT * 2, BF16)
        svh = AR.get(18 * 128 * 2, BF16, [18, 128])
        kvst = [AR.get(128 * 4, F32) for _ in range(2)]
        gate = AR.get(512 * 4, F32)
        gate2 = AR.get(512 * 4, F32)
        CT = chain_temps()
        s_kv = [P.slot("kv0"), P.slot("kv1")]
        kvc = [0]

        for h in range(NH_SB if STAGE >= 2 else 0):
            wq = load_w(wring, h * 128, 128)
            wk = load_w(wring, 1024 + h * 128, 128)
            wv = load_w(wring, 2048 + h * 128, 128)

            def ev_q(ap, c0, m, bn, h=h):
                P.op("scalar", lambda e: e.copy(sqT[:, c0:c0 + m], ap), reads=[bn], writes=["sqT"])
                if c0 == 2048:
                    P.op("vector", lambda e: e.tensor_copy(sqT_s[:, h, :], ap[:, LP - 2048:LP - 2048 + NS]), reads=[bn], writes=["sqT_s"])

            def ev_k(ap, c0, m, bn, h=h):
                P.op("scalar", lambda e: e.copy(skT[:, c0:c0 + m], ap), reads=[bn], writes=["skT"])
                if c0 == 2048:
                    P.op("vector", lambda e: e.tensor_copy(skT_s[:, h, :], ap[:, LP - 2048:LP - 2048 + NS]), reads=[bn], writes=["skT_s"])

            if 'b' in SUB:
                fm(wring, wq, 128, ev_q)
            if 'c' in SUB:
                fm(wring, wk, 128, ev_k)

            def ev_ktm(ap, t, bn, h=h):
                t0, n = TILES[t]
                j = kvc[0] % 2
                kvc[0] += 1
                P.op("scalar", lambda e: e.copy(kvst[j][:n, :], ap), reads=[bn], writes=["kvst%d" % j])
                dst = nk_p[t0:t0 + n, h * 128:(h + 1) * 128] if t < 17 else nk_s[:, h * 128:(h + 1) * 128]
                P.dma("sync", lambda e: e.dma_start(out=dst, in_=kvst[j][:n, :]), s_kv[j], reads=["kvst%d" % j])

            def ev_vtm(ap, t, bn, h=h):
                t0, n = TILES[t]
                j = kvc[0] % 2
                kvc[0] += 1
                P.op("scalar", lambda e: e.copy(kvst[j][:n, :], ap), reads=[bn], writes=["kvst%d" % j])
                dst = nv_p[t0:t0 + n, h * 128:(h + 1) * 128] if t < 17 else nv_s[:, h * 128:(h + 1) * 128]
                P.dma("sync", lambda e: e.dma_start(out=dst, in_=kvst[j][:n, :]), s_kv[j], reads=["kvst%d" % j])
                P.op("vector", lambda e: e.tensor_copy(svh[:n, t, :], ap), reads=[bn], writes=["svh"])
                if t == 17:
                    P.op("vector", lambda e: e.tensor_copy(sv_s[:, h, :], ap), reads=[bn], writes=["sv_s"])

            if 'd' in SUB:
                tm(wring, wk, 128, ev_ktm)
            if 'e' in SUB:
                tm(wring, wv, 128, ev_vtm)

            P.op("vector", lambda e, h=h: e.tensor_copy(bh[:, :], sbias[:, h:h + 1]), reads=["sbias"], writes=["bh"])
            def mk_zfill(i):
                t0, n = TILES[i]
                K = t0 + n
                d0 = t0

                def zfill(c, c0, m, bank, bn):
                    lo, hi = max(c0, d0), min(c0 + m, K)
                    has_mask = lo < hi
                    P.op("tensor", lambda e: e.matmul(bank[:n, :m], lhsT=sqT[:, t0:t0 + n], rhs=skT[:, c0:c0 + m],
                                                      start=True, stop=not has_mask),
                         reads=["sqT", "skT"], writes=[bn])
                    if has_mask:
                        P.op("tensor", lambda e: e.matmul(bank[:n, lo - c0:hi - c0], lhsT=identb[:n, :n], rhs=maskab[:n, lo - d0:hi - d0],
                                                          start=False, stop=True),
                             reads=["identb", "maskab"], writes=[bn])
                return zfill

            NQ = 17 if STAGE >= 3 else 0
            infos = [None] * (NQ + 1)
            if NQ:
                infos[0] = sb_chain_A(CT, TILES[0][1], TILES[0][0] + TILES[0][1], mk_zfill(0), bh[:TILES[0][1], 0:1], "bh")
            for i in range(NQ):
                t0, n = TILES[i]
                sb_chain_B(CT, infos[i])
                if i + 1 < NQ:
                    t1, n1 = TILES[i + 1]
                    infos[i + 1] = sb_chain_A(CT, n1, t1 + n1, mk_zfill(i + 1), bh[:n1, 0:1], "bh")
                blocks = [(0, 16)] + [(16 + 128 * b, 128) for b in range(i)]
                transpose_blocks(CT, n, blocks, infos[i]["a"], infos[i]["an"])
                ob = i % 2
                for kb, (col0, m) in enumerate(blocks):
                    P.op("tensor", lambda e, kb=kb, m=m, ob=ob, n=n: e.matmul(
                        pb[ob][:, :n], lhsT=svh[:m, kb, :], rhs=CT["aT"][:m, kb, :n], start=(kb == 0), stop=(kb == len(blocks) - 1)),
                        reads=["svh", "aT%d" % (kb // 8)], writes=["pb%d" % ob])
                if i >= 1:
                    P.op("scalar", lambda e, ob=ob, t0=t0, n=n, h=h: e.copy(mixT[:, h, t0 - NMETA:t0 - NMETA + n], pb[ob][:, :n]),
                         reads=["pb%d" % ob], writes=["mixT%d" % h])

            wg = load_w(wring, 3072 + h * 128, 128)

            def ev_g(ap, c0, m, bn, h=h):
                gate_chunk(ap, c0, m, bn, h, gate, gate2, sg_save=h)

            if STAGE >= 4:
                fm(wring, wg, 128, ev_g)
        P.barrier()
        if STAGE >= 3:
            P.dma("gpsimd", lambda e: e.dma_start(out=dbg[:, 0:NMX], in_=mixT[:, 0, :]), s_dbg, reads=["mixT0"])
            P.barrier()

        AR.reset()
        wring = [AR.get(KC * 128 * 2, BF16, [KC, 128]) for _ in range(3)]
        gqT = AR.get(NT * 2, BF16)
        gkT = AR.get(NT * 2, BF16)
        gk_tm = AR.get(18 * 128 * 2, BF16, [18, 128])
        gv_tm = AR.get(18 * 256 * 2, BF16, [18, 256])
        gaT = AR.get(NT * 2, BF16)
        walb = AR.get(512 * 2, BF16)
        Sf = AR.get(256 * 4, F32)
        Sb = AR.get(256 * 2, BF16)
        lsp2 = [AR.get(128 * 4, F32) for _ in range(2)]
        eb2 = [AR.get(128 * 4, F32) for _ in range(2)]
        enb2 = [AR.get(128 * 4, F32) for _ in range(2)]
        rex2 = [AR.get(128 * 4, F32) for _ in range(2)]
        qt2 = [AR.get(128 * 2, BF16) for _ in range(2)]
        kt2 = [AR.get(128 * 2, BF16) for _ in range(2)]
        attn2 = [AR.get(128 * 2, BF16) for _ in range(2)]
        khat2 = [AR.get(128 * 2, BF16) for _ in range(2)]
        osq = AR.get(2 * 128 * 4, F32, [2, 128])
        rstd = AR.get(128 * 4, F32)
        gate = AR.get(512 * 4, F32)
        gate2 = AR.get(512 * 4, F32)
        S0f = [AR.get(256 * 4, F32) for _ in range(2)]
        S0b = [AR.get(256 * 2, BF16) for _ in range(2)]
        snew = [AR.get(256 * 4, F32) for _ in range(2)]
        khm = AR.get(16 * 128 * 2, BF16, [16, 128])
        s_wa = P.slot("wa")
        s_s0 = [P.slot("s0a"), P.slot("s0b")]
        s_s0b = [P.slot("s0ba"), P.slot("s0bb")]
        s_sn = [P.slot("sna"), P.slot("snb")]
        s_nsp = P.slot("nsp")
        s0c = [0]
        NG = 4 if STAGE >= 6 else 0
        if NG:
            P.dma("gpsimd", lambda e: e.dma_start(out=walb[0:17, :], in_=walpha_d), s_wa, writes=["walb"])
            P.op("gpsimd", lambda e: e.memset(gaT[0:32, :], 1.0), writes=["gaT"])
            wa = load_w(wring, 7168, 16)
            fm(wring, wa, 16, lambda ap, c0, m, bn: P.op("scalar", lambda e: e.copy(gaT[0:16, c0:c0 + m], ap), reads=[bn], writes=["gaT"]))
        for g in range(NG if NH_SB == 8 else min(NG, 1)):
            wq = load_w(wring, 4096 + g * 128, 128)
            wk = load_w(wring, 4608 + g * 128, 128)
            fm(wring, wq, 128, lambda ap, c0, m, bn: P.op("scalar", lambda e: e.copy(gqT[:, c0:c0 + m], ap), reads=[bn], writes=["gqT"]))
            fm(wring, wk, 128, lambda ap, c0, m, bn: P.op("scalar", lambda e: e.copy(gkT[:, c0:c0 + m], ap), reads=[bn], writes=["gkT"]))
            tm(wring, wk, 128, lambda ap, t, bn: P.op("scalar", lambda e: e.copy(gk_tm[:TILES[t][1], t, :], ap), reads=[bn], writes=["gk_tm"]))
            for jj in range(2):
                wv = load_w(wring, 5120 + g * 256 + jj * 128, 128)
                tm(wring, wv, 128, lambda ap, t, bn, jj=jj: P.op("scalar", lambda e: e.copy(gv_tm[:TILES[t][1], t, jj * 128:(jj + 1) * 128], ap),
                                                              reads=[bn], writes=["gv_tm"]))
            P.op("vector", lambda e: e.memset(Sf[:, :], 0.0), writes=["Sf"])
            P.op("vector", lambda e: e.memset(Sb[:, :], 0.0), writes=["Sb"])
            def gla_front(t, g=g):
                t0, n = TILES[t]
                fb = t % 2
                samp = (t == 17)
                lsp, eb, enb, rex, qt, kt, attn, khat = lsp2[fb], eb2[fb], enb2[fb], rex2[fb], qt2[fb], kt2[fb], attn2[fb], khat2[fb]
                F = lambda nm: "%s_%d" % (nm, fb)
                cTU, cTL, cCA = (C_TUS, C_TLS, C_CAS) if samp else (C_TU, C_TL, C_CA)
                P.op("tensor", lambda e, t0=t0, n=n, g=g: e.matmul(pb[3][:n, 128:256], lhsT=gaT[0:17, t0:t0 + n], rhs=walb[0:17, g * 128:(g + 1) * 128],
                                                                  start=True, stop=True), reads=["gaT", "walb"], writes=["pb3"])
                P.op("scalar", lambda e, n=n: e.activation(out=lsp[:n, :], in_=pb[3][:n, 128:256], func=AF.Exp, scale=-1.0), reads=["pb3"], writes=[F("lsp")])
                P.op("scalar", lambda e, n=n: e.activation(out=lsp[:n, :], in_=lsp[:n, :], func=AF.Ln, bias=1.0, scale=1.0), reads=[F("lsp")], writes=[F("lsp")])
                P.op("tensor", lambda e, n=n, cTU=cTU: e.matmul(pb[2][:, 0:n], lhsT=lsp[:n, :], rhs=cst[:n, cTU:cTU + n], start=True, stop=True),
                     reads=[F("lsp"), "cst"], writes=["pb2"])
                P.op("tensor", lambda e, n=n, cTL=cTL: e.matmul(pb[2][:n, 128:256], lhsT=cst[:n, cTL:cTL + n], rhs=lsp[:n, :], start=True, stop=True),
                     reads=[F("lsp"), "cst"], writes=["pb2"])
                P.op("scalar", lambda e, n=n: e.activation(out=eb[:, :n], in_=pb[2][:, 0:n], func=AF.Exp), reads=["pb2"], writes=[F("eb")])
                P.op("scalar", lambda e, n=n: e.activation(out=enb[:, :n], in_=pb[2][:, 0:n], func=AF.Exp, scale=-1.0), reads=["pb2"], writes=[F("enb")])
                P.op("scalar", lambda e, n=n: e.activation(out=rex[:n, :], in_=pb[2][:n, 128:256], func=AF.Exp), reads=["pb2"], writes=[F("rex")])
                P.op("vector", lambda e, t0=t0, n=n: e.scalar_tensor_tensor(qt[:, :n], gqT[:, t0:t0 + n], SCALE, eb[:, :n], ALU.mult, ALU.mult),
                     reads=["gqT", F("eb")], writes=[F("qt")])
                P.op("vector", lambda e, t0=t0, n=n: e.tensor_tensor(kt[:, :n], gkT[:, t0:t0 + n], enb[:, :n], ALU.mult), reads=["gkT", F("enb")], writes=[F("kt")])
                P.op("vector", lambda e, t=t, n=n: e.tensor_tensor(khat[:n, :], gk_tm[:n, t, :], rex[:n, :], ALU.mult), reads=["gk_tm", F("rex")], writes=[F("khat")])
                P.op("tensor", lambda e, n=n: e.matmul(pb[3][:n, 0:n], lhsT=kt[:, :n], rhs=qt[:, :n], start=True, stop=True), reads=[F("kt"), F("qt")], writes=["pb3"])
                P.op("vector", lambda e, n=n, cCA=cCA: e.tensor_tensor(attn[:n, :n], pb[3][:n, 0:n], cst[:n, cCA:cCA + n], ALU.mult),
                     reads=["pb3", "cst"], writes=[F("attn")])
            def gla_back(t, g=g):
                t0, n = TILES[t]
                fb = t % 2
                samp = (t == 17)
                lsp, eb, enb, rex, qt, kt, attn, khat = lsp2[fb], eb2[fb], enb2[fb], rex2[fb], qt2[fb], kt2[fb], attn2[fb], khat2[fb]
                F = lambda nm: "%s_%d" % (nm, fb)
                for j in range(2):
                    P.op("tensor", lambda e, j=j, t=t, n=n: e.matmul(pb[4 + j][:, 0:n], lhsT=gv_tm[:n, t, j * 128:(j + 1) * 128], rhs=attn[:n, :n],
                                                                    start=True, stop=False), reads=["gv_tm", F("attn")], writes=["pb%d" % (4 + j)])
                if not samp:
                    chunks = [(0, 16)] if n == 16 else [(0, 64), (64, 128)]
                    for ci, (a_, b_) in enumerate(chunks):
                        last = ci == len(chunks) - 1
                        for j in range(2):
                            P.op("tensor", lambda e, j=j, a_=a_, b_=b_, last=last: e.matmul(
                                pb[4 + j][:, a_:b_], lhsT=Sb[:, j * 128:(j + 1) * 128], rhs=qt[:, a_:b_], start=False, stop=last),
                                reads=["Sb", F("qt")], writes=["pb%d" % (4 + j)])
                        P.op("tensor", lambda e, a_=a_, b_=b_, t=t: e.matmul(pb[6][:, 0:256], lhsT=khat[a_:b_, :], rhs=gv_tm[a_:b_, t, :], start=True, stop=True),
                             reads=[F("khat"), "gv_tm"], writes=["pb6"])
                        P.op("vector", lambda e, b_=b_: e.scalar_tensor_tensor(Sf[:, :], Sf[:, :], eb[:, b_ - 1:b_], pb[6][:, 0:256], ALU.mult, ALU.add),
                             reads=["Sf", F("eb"), "pb6"], writes=["Sf"])
                        P.op("scalar", lambda e: e.copy(Sb[:, :], Sf[:, :]), reads=["Sf"], writes=["Sb"])
                    if t == 16:
                        P.dma("sync", lambda e, g=g: e.dma_start(out=ns_p[g], in_=Sf[:, :]), s_nsp, reads=["Sf"])
                else:
                    for b in range(16):
                        jb = s0c[0] % 2
                        s0c[0] += 1
                        P.dma("sync", lambda e, jb=jb, b=b, g=g: e.dma_start(out=S0f[jb][:, :], in_=state[b, g]), s_s0[jb], writes=["S0f%d" % jb])
                        P.dma("gpsimd", lambda e, jb=jb, b=b, g=g: e.dma_start(out=S0b[jb][:, :], in_=state[b, g]), s_s0b[jb], writes=["S0b%d" % jb])
                        for j in range(2):
                            P.op("tensor", lambda e, j=j, jb=jb, b=b: e.matmul(
                                pb[4 + j][:, 4 * b:4 * b + 4], lhsT=S0b[jb][:, j * 128:(j + 1) * 128], rhs=qt[:, 4 * b:4 * b + 4], start=False, stop=(b == 15)),
                                reads=["S0b%d" % jb, F("qt")], writes=["pb%d" % (4 + j)])
                        if b == 0:
                            P.op("vector", lambda e: e.tensor_tensor(khm[:NS, :, :], khat[:NS, :].unsqueeze(1).to_broadcast([NS, 16, 128]),
                                                                     cst[:NS, C_MB:C_MB + 16].unsqueeze(2).to_broadcast([NS, 16, 128]), ALU.mult),
                                 reads=[F("khat"), "cst"], writes=["khm"])
                        P.op("tensor", lambda e, b=b: e.matmul(pb[6][:, 0:256], lhsT=khm[:NS, b, :], rhs=gv_tm[:NS, 17, :], start=True, stop=True),
                             reads=["khm", "gv_tm"], writes=["pb6"])
                        P.op("vector", lambda e, jb=jb, b=b: e.scalar_tensor_tensor(snew[jb][:, :], S0f[jb][:, :], eb[:, 4 * b + 3:4 * b + 4], pb[6][:, 0:256],
                                                                                    ALU.mult, ALU.add), reads=["S0f%d" % jb, F("eb"), "pb6"], writes=["snew%d" % jb])
                        P.dma("sync", lambda e, jb=jb, b=b, g=g: e.dma_start(out=ns_s[b, g], in_=snew[jb][:, :]), s_sn[jb], reads=["snew%d" % jb])
                if t >= 1:
                    for j in range(2):
                        P.op("scalar", lambda e, j=j, n=n: e.activation(out=osq[:, j, :n], in_=pb[4 + j][:, 0:n], func=AF.Square),
                             reads=["pb%d" % (4 + j)], writes=["osq%d" % j])
                    for j in range(2):
                        P.op("tensor", lambda e, j=j, n=n: e.matmul(pb[7][:, 0:n], lhsT=ones_f[:, 0:128], rhs=osq[:, j, :n], start=(j == 0), stop=(j == 1)),
                             reads=["ones_f", "osq%d" % j], writes=["pb7"])
                    P.op("scalar", lambda e, n=n: e.activation(out=rstd[:, :n], in_=pb[7][:, 0:n], func=AF.Ln, bias=EPS, scale=1.0 / 256), reads=["pb7"], writes=["rstd"])
                    P.op("scalar", lambda e, n=n: e.activation(out=rstd[:, :n], in_=rstd[:, :n], func=AF.Exp, scale=-0.5), reads=["rstd"], writes=["rstd"])
                    for j in range(2):
                        kidx = 8 + 2 * g + j
                        P.op("vector", lambda e, j=j, n=n, kidx=kidx, t0=t0: e.scalar_tensor_tensor(
                            mixT[:, kidx, t0 - NMETA:t0 - NMETA + n], pb[4 + j][:, 0:n], gnorm[:, j:j + 1], rstd[:, :n], ALU.mult, ALU.mult),
                            reads=["pb%d" % (4 + j), "gnorm", "rstd"], writes=["mixT%d" % kidx])
            gla_front(0)
            for t in range(18):
                if t + 1 < 18:
                    gla_front(t + 1)
                gla_back(t)
            for j in range(2):
                wgg = load_w(wring, 6144 + g * 256 + j * 128, 128)
                fm(wring, wgg, 128, lambda ap, c0, m, bn, kidx=8 + 2 * g + j: gate_chunk(ap, c0, m, bn, kidx, gate, gate2))
        P.barrier()
        if STAGE >= 6:
            P.dma("gpsimd", lambda e: e.dma_start(out=dbg[:, 0:NMX], in_=mixT[:, 8, :]), s_dbg, reads=["mixT8"])
            P.barrier()
        for kk in range(8 + 2 * (NG if NH_SB == 8 else min(NG, 1)), 16):
            P.op("gpsimd", lambda e, kk=kk: e.memset(mixT[:, kk, :], 0.0), writes=["mixT%d" % kk])
        if STAGE < 7:
            P.op("gpsimd", lambda e: e.memset(mixT[:, 0:8, LP - NMETA:NMX], 0.0), reads=["mixT%d" % h for h in range(8)],
                 writes=["mixT%d" % h for h in range(8)])
        P.barrier()

        if STAGE >= 7:
            AR.reset()
            ptb = AR.get(256 * 4, I32)
            idxt = AR.get(256 * 4, I32)
            iotf = AR.get(4, F32)
            NKV = 5
            Kbuf = [AR.get(2 * 1024 * 2, BF16, [2, 1024]) for _ in range(NKV)]
            Vbuf = Kbuf
            KT = [AR.get(8 * 256 * 2, BF16, [8, 256]) for _ in range(2)]
            qm = AR.get(8 * 128 * 2, BF16, [8, 128])
            osel = AR.get(8 * 128 * 4, F32, [8, 128])
            CT3 = chain_temps(znb=2, na=1)
            s_pt = P.slot("pt")
            s_kb = [[P.slot("kb%d_%d" % (i, p)) for p in range(2)] for i in range(NKV)]
            s_vb = s_kb
            P.dma("sync", lambda e: e.dma_start(out=ptb[:, :], in_=pt.rearrange("a n -> (a n)").partition_broadcast(128)), s_pt, writes=["ptb"])
            P.op("gpsimd", lambda e: e.iota(iotf[:, 0:1], pattern=[[0, 1]], base=0, channel_multiplier=1, allow_small_or_imprecise_dtypes=True),
                 writes=["iotf"])
            P.op("vector", lambda e: e.tensor_scalar(idxt[:, :], ptb[:, :], 128.0, iotf[:, 0:1], ALU.mult, ALU.add), reads=["ptb", "iotf"], writes=["idxt"])
            kctr = [0]
            vctr = [0]
            for G in range(4):
                P.op("vector", lambda e: e.memset(qm[:, :, :], 0.0), writes=["qm"])
                for h in range(8):
                    P.op("vector", lambda e, h=h, G=G: e.tensor_copy(
                        qm[:, h, :].rearrange("p (b x) -> p b x", b=4)[:, :, 4 * h:4 * h + 4],
                        sqT_s[:, h, 16 * G:16 * G + 16].rearrange("p (b q) -> p b q", b=4)), reads=["sqT_s"], writes=["qm"])

                def zfill(c, c0, m, bank, bn, G=G):
                    if c < 4:
                        for b4 in range(4):
                            bl = 4 * G + b4
                            for half in range(2):
                                ks = kctr[0] % NKV
                                kctr[0] += 1
                                kt_ = vctr[0] % 2
                                vctr[0] += 1
                                for p in range(2):
                                    col = bl * 16 + 4 * c + 2 * half + p
                                    P.dma("gpsimd", lambda e, ks=ks, p=p, col=col: e.indirect_dma_start(
                                        out=Kbuf[ks][:, p, :], out_offset=None, in_=ck,
                                        in_offset=bass.IndirectOffsetOnAxis(ap=idxt[:, col:col + 1], axis=0)),
                                        s_kb[ks][p], reads=["idxt"], writes=["Kbuf%d_%d" % (ks, p)])
                                tb = 6 - 2 * kt_
                                for h in range(8):
                                    for p in range(2):
                                        P.op("tensor", lambda e, ks=ks, h=h, p=p, tb=tb: e.transpose(
                                            pbh[tb + h // 4][:, (h % 4) * 256 + p * 128:(h % 4) * 256 + (p + 1) * 128],
                                            Kbuf[ks][:, p, h * 128:(h + 1) * 128], identb[:, :]),
                                            reads=["Kbuf%d_%d" % (ks, p), "identb"], writes=["pb%d" % (tb + h // 4)])
                                P.op("scalar", lambda e, kt_=kt_, tb=tb: e.copy(KT[kt_][:, 0:4, :], pbh[tb].rearrange("p (a b) -> p a b", a=4)),
                                     reads=["pb%d" % tb], writes=["KT%da" % kt_])
                                P.op("vector", lambda e, kt_=kt_, tb=tb: e.tensor_copy(KT[kt_][:, 4:8, :], pbh[tb + 1].rearrange("p (a b) -> p a b", a=4)),
                                     reads=["pb%d" % (tb + 1)], writes=["KT%db" % kt_])
                                for h in range(8):
                                    P.op("tensor", lambda e, kt_=kt_, h=h, b4=b4, half=half: e.matmul(
                                        bank[32 * b4:32 * b4 + 32, half * 256:(half + 1) * 256], lhsT=qm[:, h, 32 * b4:32 * b4 + 32],
                                        rhs=KT[kt_][:, h, :], start=(h == 0), stop=(h == 7), tile_position=(0, 32 * b4)),
                                        reads=["qm", "KT%d%s" % (kt_, "a" if h < 4 else "b")], writes=[bn])
                    else:
                        for h in range(8):
                            P.op("tensor", lambda e, h=h: e.matmul(bank[:, 0:NS], lhsT=qm[:, h, :], rhs=skT_s[:, h, :], start=(h == 0), stop=False),
                                 reads=["qm", "skT_s"], writes=[bn])
                        P.op("tensor", lambda e: e.matmul(bank[:, 0:NS], lhsT=identb[:, :], rhs=maskSb[:, 64 * G:64 * G + 64], start=False, stop=True),
                             reads=["identb", "maskSb"], writes=[bn])

                sb_chain(CT3, 128, 2048 + NS, zfill, sbrow[:, 0:1], "sbrow")
                blocks = [(128 * j, 128) for j in range(16)] + [(2048, NS)]
                transpose_blocks(CT3, 128, blocks)
                for b4 in range(4):
                    bl = 4 * G + b4
                    for jp in range(8):
                        vs = kctr[0] % NKV
                        kctr[0] += 1
                        for p in range(2):
                            col = bl * 16 + 2 * jp + p
                            P.dma("gpsimd", lambda e, vs=vs, p=p, col=col: e.indirect_dma_start(
                                out=Vbuf[vs][:, p, :], out_offset=None, in_=cv,
                                in_offset=bass.IndirectOffsetOnAxis(ap=idxt[:, col:col + 1], axis=0)),
                                s_vb[vs][p], reads=["idxt"], writes=["Kbuf%d_%d" % (vs, p)])
                        for p in range(2):
                            j = 2 * jp + p
                            for hb in range(2):
                                P.op("tensor", lambda e, vs=vs, p=p, j=j, hb=hb, b4=b4: e.matmul(
                                    pb[hb][32 * b4:32 * b4 + 32, :], lhsT=CT3["aT"][:, j, 32 * b4:32 * b4 + 32], rhs=Vbuf[vs][:, p, hb * 512:(hb + 1) * 512],
                                    start=(j == 0), stop=False, tile_position=(0, 32 * b4)),
                                    reads=["Kbuf%d_%d" % (vs, p), "aT%d" % (j // 8)], writes=["pb%d" % hb])
                    for hb in range(2):
                        P.op("tensor", lambda e, hb=hb, b4=b4: e.matmul(
                            pb[hb][32 * b4:32 * b4 + 32, :], lhsT=CT3["aT"][:NS, 16, 32 * b4:32 * b4 + 32],
                            rhs=sv_s[:, 4 * hb:4 * hb + 4, :].rearrange("p a b -> p (a b)"), start=False, stop=True, tile_position=(0, 32 * b4)),
                            reads=["sv_s", "aT2"], writes=["pb%d" % hb])
                for hb in range(2):
                    P.op("vector", lambda e, hb=hb: e.tensor_tensor(
                        osel[:, 4 * hb:4 * hb + 4, :], pb[hb][:, :].rearrange("p (a b) -> p a b", a=4),
                        cst[:, C_HM + 4 * hb:C_HM + 4 * hb + 4].unsqueeze(2).to_broadcast([128, 4, 128]), ALU.mult),
                        reads=["pb%d" % hb, "cst"], writes=["osel%d" % hb])
                for h in range(8):
                    P.op("tensor", lambda e, h=h: e.matmul(pb[2][:, 16 * h:16 * h + 16], lhsT=osel[:, h, :], rhs=cst[:, C_RS:C_RS + 16], start=True, stop=True),
                         reads=["osel%d" % (h // 4), "cst"], writes=["pb2"])
                P.op("vector", lambda e, G=G: e.tensor_tensor(
                    mixT[:, 0:8, LP - NMETA + 16 * G:LP - NMETA + 16 * G + 16], pb[2][:, 0:128].rearrange("p (a b) -> p a b", a=8),
                    sgate_s[:, :, 16 * G:16 * G + 16], ALU.mult), reads=["pb2", "sgate_s"], writes=["mixT%d" % h for h in range(8)])
            P.barrier()
            P.dma("gpsimd", lambda e: e.dma_start(out=dbg[:, 0:NMX], in_=mixT[:, 0, :]), s_dbg, reads=["mixT0"])
            P.barrier()

        if STAGE == 77:
            for kk in range(16):
                for hf in range(2):
                    P.dma("gpsimd", lambda e, kk=kk, hf=hf: e.dma_start(out=dbg2[:, kk, hf * 1056:(hf + 1) * 1056], in_=mixT[:, kk, hf * 1056:(hf + 1) * 1056]),
                          s_dbg, reads=["mixT%d" % kk])
            P.barrier()

        AR.reset()
        xo = [AR.get(D * 4, F32) for _ in range(2)]
        yo = [AR.get(D * 4, F32) for _ in range(2)]
        gpost = AR.get(D * 4, F32)
        s_wo = P.slot("wo")
        s_gp = P.slot("gp")
        s_xo = [P.slot("xo0"), P.slot("xo1")]
        s_yo = [P.slot("yo0"), P.slot("yo1")]
        for q4 in range(4):
            P.dma("gpsimd", lambda e, q4=q4: e.dma_start(out=wout[:, 4 * q4:4 * q4 + 4, :], in_=w_out_v[:, 4 * q4:4 * q4 + 4, :]),
                  s_wo, writes=["wout%d" % q4])
        P.dma("sync", lambda e: e.dma_start(out=gpost[:, :], in_=gpost_d.partition_broadcast(128)), s_gp, writes=["gpost"])
        for t in range(1, 18 if STAGE >= 5 else 1):
            t0, n = TILES[t]
            j = t % 2
            m0 = t0 - NMETA
            P.dma("sync", lambda e, j=j, t0=t0, n=n: e.dma_start(out=xo[j][:n, :], in_=x_all[t0:t0 + n, :]), s_xo[j], writes=["xo%d" % j])
            for dc in range(4):
                bk = 4 * j + dc
                for k in range(KC):
                    P.op("tensor", lambda e, bk=bk, k=k, dc=dc, m0=m0, n=n: e.matmul(
                        pb[bk][:n, :], lhsT=mixT[:, k, m0:m0 + n], rhs=wout[:, k, dc * 512:(dc + 1) * 512], start=(k == 0), stop=(k == KC - 1)),
                        reads=["wout%d" % (k // 4)] + ["mixT%d" % kk for kk in range(16)], writes=["pb%d" % bk])
                P.op("scalar", lambda e, bk=bk, dc=dc, j=j, n=n: e.activation(out=yo[j][:n, dc * 512:(dc + 1) * 512], in_=pb[bk][:n, :],
                                                                           func=AF.Square, accum_out=stat[:n, 8 + dc:9 + dc]),
                     reads=["pb%d" % bk], writes=["yo%d" % j, "st8%d" % dc])
            P.op("vector", lambda e, n=n: e.reduce_sum(stat[:n, 12:13], stat[:n, 8:12], axis=mybir.AxisListType.X),
                 reads=["st8%d" % dc for dc in range(4)], writes=["st12"])
            P.op("scalar", lambda e, n=n: e.activation(out=stat[:n, 13:14], in_=stat[:n, 12:13], func=AF.Ln, bias=EPS, scale=1.0 / D),
                 reads=["st12"], writes=["st13"])
            P.op("scalar", lambda e, n=n: e.activation(out=stat[:n, 14:15], in_=stat[:n, 13:14], func=AF.Exp, scale=-0.5),
                 reads=["st13"], writes=["st14"])
            for dc in range(4):
                bk = 4 * j + dc
                P.op("vector", lambda e, bk=bk, dc=dc, j=j, n=n: e.scalar_tensor_tensor(
                    yo[j][:n, dc * 512:(dc + 1) * 512], pb[bk][:n, :], stat[:n, 14:15], gpost[:n, dc * 512:(dc + 1) * 512], ALU.mult, ALU.mult),
                    reads=["pb%d" % bk, "st14", "gpost"], writes=["yo%d" % j])
            P.op("gpsimd", lambda e, j=j, n=n: e.tensor_tensor(yo[j][:n, :], yo[j][:n, :], xo[j][:n, :], ALU.add),
                 reads=["yo%d" % j, "xo%d" % j], writes=["yo%d" % j])
            dst = y_p[t0 - NMETA:t0 - NMETA + n, :] if t < 17 else y_s[:, :]
            P.dma("sync", lambda e, j=j, n=n, dst=dst: e.dma_start(out=dst, in_=yo[j][:n, :]), s_yo[j], reads=["yo%d" % j])

        P.emit(nc)
    return nc


def make_consts():
    c = np.zeros((128, NCST), np.float32)
    i = np.arange(128)
    c[:, C_ID:C_ID + 128] = np.eye(128, dtype=np.float32)
    c[:, C_MA:C_MA + 128] = np.where(i[None, :] < i[:, None], 0.0, NEG)
    b4, hh, qq = i // 32, (i // 4) % 8, i % 4
    col = np.arange(64)
    for G in range(4):
        ok = ((col[None, :] // 4) == (4 * G + b4)[:, None]) & ((col[None, :] % 4) < qq[:, None])
        c[:, C_MS + 64 * G:C_MS + 64 * (G + 1)] = np.where(ok, 0.0, NEG)
    same = (i[:, None] // 64) == (i[None, :] // 64)
    c[:, C_TU:C_TU + 128] = np.where(same & (i[:, None] <= i[None, :]), -1.0 / 16, 0.0)
    c[:, C_TL:C_TL + 128] = np.where(same & (i[:, None] > i[None, :]), -1.0 / 16, 0.0)
    c[:, C_CA:C_CA + 128] = np.where(same & (i[:, None] <= i[None, :]), 1.0, 0.0)
    j = np.arange(64)
    same4 = (j[:, None] // 4) == (j[None, :] // 4)
    c[:64, C_TUS:C_TUS + 64] = np.where(same4 & (j[:, None] <= j[None, :]), -1.0 / 16, 0.0)
    c[:64, C_TLS:C_TLS + 64] = np.where(same4 & (j[:, None] > j[None, :]), -1.0 / 16, 0.0)
    c[:64, C_CAS:C_CAS + 64] = np.where(same4 & (j[:, None] <= j[None, :]), 1.0, 0.0)
    c[:64, C_MB:C_MB + 16] = (j[:, None] // 4 == np.arange(16)[None, :])
    c[:, C_HM:C_HM + 8] = (hh[:, None] == np.arange(8)[None, :])
    c[:, C_RS:C_RS + 16] = ((b4 * 4 + qq)[:, None] == np.arange(16)[None, :])
    return c


_NC_CACHE = {}


def kernel(x_prompt, x_sample, cache_k, cache_v, state_gla, page_table, meta_tokens,
           norm_pre_g, w_in, sb_bias, w_alpha, b_alpha, gla_norm_g, w_out, norm_post_g):
    f32 = lambda a: np.ascontiguousarray(np.asarray(a, dtype=np.float32))
    x_prompt, x_sample, meta_tokens = f32(x_prompt), f32(x_sample), f32(meta_tokens)
    ckr = f32(cache_k).reshape(NPOOL_ROWS, 1024)
    cvr = f32(cache_v).reshape(NPOOL_ROWS, 1024)
    state_gla = f32(state_gla)
    page_table = np.ascontiguousarray(np.asarray(page_table, dtype=np.int32))
    w_in_ = f32(w_in)[0]
    w_out_ = f32(w_out)[0]
    gpre = f32(norm_pre_g)[0].reshape(KC, 128).T.copy()
    gpost = f32(norm_post_g)[0].reshape(D)
    sbias = f32(sb_bias)[0].reshape(8)
    sbrow = np.ascontiguousarray(np.tile(np.repeat(sbias, 4), 4).reshape(128, 1))
    walpha = np.concatenate([f32(w_alpha)[0], f32(b_alpha)[0].reshape(1, 512)], axis=0)
    gnorm = f32(gla_norm_g)[0].reshape(2, 128).T.copy()
    cst = make_consts()
    if "nc" not in _NC_CACHE:
        _NC_CACHE["nc"] = build_program()
    nc = _NC_CACHE["nc"]
    in_maps = []
    for c in range(8):
        x_all = np.concatenate([meta_tokens, x_prompt[c], x_sample[16 * c:16 * c + 16].reshape(NS, D)], axis=0)
        in_maps.append({
            "x_all": x_all, "ck": ckr, "cv": cvr,
            "pt": page_table[16 * c:16 * c + 16].reshape(1, 256),
            "state": state_gla[0, 16 * c:16 * c + 16],
            "w_in": w_in_, "w_out": w_out_, "gpre": gpre, "gpost": gpost, "sbias": sbias, "sbrow": sbrow,
            "walpha": walpha, "gnorm": gnorm, "cst": cst,
        })
    res = run_bass_kernel_spmd(nc, in_maps, core_ids=list(range(8)))
    R = res.results
    y_prompt = np.stack([R[c]["y_p"] for c in range(8)], axis=0)
    y_sample = np.concatenate([R[c]["y_s"].reshape(16, 4, D) for c in range(8)], axis=0)
    nk_p = np.stack([R[c]["nk_p"].reshape(LP, 8, 128) for c in range(8)], axis=0)[None]
    nv_p = np.stack([R[c]["nv_p"].reshape(LP, 8, 128) for c in range(8)], axis=0)[None]
    ns_p = np.stack([R[c]["ns_p"] for c in range(8)], axis=0)[None]
    nk_s = np.concatenate([R[c]["nk_s"].reshape(16, 4, 8, 128) for c in range(8)], axis=0)[None]
    nv_s = np.concatenate([R[c]["nv_s"].reshape(16, 4, 8, 128) for c in range(8)], axis=0)[None]
    ns_s = np.concatenate([R[c]["ns_s"] for c in range(8)], axis=0)[None]
    return (y_prompt, y_sample, nk_p, nv_p, ns_p, nk_s, nv_s, ns_s)
```

```python
import contextlib
import numpy as np
import concourse.bass as bass
import concourse.mybir as mybir
from concourse.bass_utils import run_bass_kernel_spmd

F32 = mybir.dt.float32
BF16 = mybir.dt.bfloat16
I32 = mybir.dt.int32
U8 = mybir.dt.uint8
AF = mybir.ActivationFunctionType
ALU = mybir.AluOpType

D = 2048
KC = 16
NMETA = 16
SEQ = 2048
LP = NMETA + SEQ
NS = 64
NT = LP + NS
NIN = 7184
NPOOL_ROWS = 2560 * 128
EPS = 1e-6
NEG = -30000.0
SCALE = 128 ** -0.5
TILES = [(0, 16)] + [(16 + 128 * i, 128) for i in range(16)] + [(LP, NS)]
TCH = [(0, 512), (512, 512), (1024, 512), (1536, 512), (2048, NT - 2048)]
ENGS = ["sync", "scalar", "gpsimd", "vector", "tensor"]

C_ID, C_MA, C_MS, C_TU, C_TL, C_CA, C_TUS, C_TLS, C_CAS, C_MB, C_HM, C_RS = (
    0, 128, 256, 512, 640, 768, 896, 960, 1024, 1088, 1104, 1112)
NCST = 1128
STAGE = 99
SUB = 'abcde'
NH_SB = 8


class Op:
    __slots__ = ("eng", "fn", "deps", "needed", "count", "dma", "dtok")

    def __init__(self, eng, fn):
        self.eng, self.fn = eng, fn
        self.deps, self.needed, self.count, self.dma, self.dtok = [], False, None, None, None


class DSlot:
    def __init__(self, name):
        self.name, self.count, self.sem = name, 0, None


class Prog:
    def __init__(self):
        self.q = {e: [] for e in ENGS}
        self.lastw, self.readers, self.slots = {}, {}, []
        self.nsig = {e: 0 for e in ENGS}
        self.dma_ops = []

    def slot(self, name):
        s = DSlot(name)
        self.slots.append(s)
        return s

    def _adddep(self, op, d):
        if d is op:
            return
        if d.dma is None and op.dma is None and d.eng == "tensor" and op.eng == "tensor":
            return
        op.deps.append((d, d.dma.count if d.dma is not None else None))
        if d.dma is None:
            d.needed = True

    def _track(self, op, reads, writes):
        writes = list(writes) + [r for r in reads if r.startswith("pb")]
        reads = [r for r in reads if not r.startswith("pb")]
        deps = []
        for r in reads:
            w = self.lastw.get(r)
            if w is not None:
                deps.append(w)
        for w_ in writes:
            w = self.lastw.get(w_)
            if w is not None:
                deps.append(w)
            deps.extend(self.readers.get(w_, ()))
        seen = set()
        for d in deps:
            if id(d) in seen:
                continue
            seen.add(id(d))
            self._adddep(op, d)
        for r in reads:
            self.readers.setdefault(r, []).append(op)
        for w_ in writes:
            self.lastw[w_] = op
            self.readers[w_] = []

    def op(self, eng, fn, reads=(), writes=()):
        o = Op(eng, fn)
        self._track(o, reads, writes)
        self.q[eng].append(o)
        return o

    def dma(self, eng, fn, slot, reads=(), writes=()):
        o = Op(eng, fn)
        o.dma = slot
        self._track(o, reads, writes)
        slot.count += 16
        o.dtok = slot.count
        self.q[eng].append(o)
        self.dma_ops.append(o)
        return o

    def barrier(self):
        lasts = []
        for e in ENGS:
            for o in reversed(self.q[e]):
                if o.dma is None and o.fn is not None:
                    lasts.append(o)
                    break
        dmas = list(self.dma_ops)
        self.dma_ops = []
        for e in ENGS:
            b = Op(e, None)
            for d in lasts:
                if d.eng != e:
                    b.deps.append((d, None))
                    d.needed = True
            for d in dmas:
                b.deps.append((d, d.dma.count))
            self.q[e].append(b)
        self.lastw, self.readers = {}, {}

    def emit(self, nc):
        for e in ENGS:
            c = 0
            for o in self.q[e]:
                if o.dma is None and o.needed:
                    c += 1
                    o.count = c
            self.nsig[e] = c
        with contextlib.ExitStack() as st:
            esem = {e: st.enter_context(nc.semaphore("es_" + e)) for e in ENGS}
            for s in self.slots:
                s.sem = st.enter_context(nc.semaphore("ds_" + s.name))
            block = st.enter_context(nc.Block())
            prog = self

            def make(ename):
                def body(eng):
                    seen = {}
                    for o in prog.q[ename]:
                        need = {}
                        for d, dval in o.deps:
                            if d.dma is not None:
                                key, val, sem = ("d", id(d.dma)), dval, d.dma.sem
                            else:
                                key, val, sem = ("e", d.eng), d.count, esem[d.eng]
                            if need.get(key, (None, -1))[1] < val:
                                need[key] = (sem, val)
                        for key, (sem, val) in need.items():
                            if seen.get(key, -1) >= val:
                                continue
                            seen[key] = val
                            eng.wait_ge(sem, val)
                        if o.fn is None:
                            continue
                        ins = o.fn(eng)
                        if o.dma is not None:
                            ins.then_inc(o.dma.sem, 16)
                        elif o.needed:
                            ins.then_inc(esem[ename], 1)
                    if ename == "sync":
                        for s in prog.slots:
                            if s.count > 0:
                                eng.wait_ge(s.sem, s.count)
                        for e2 in ENGS:
                            if prog.nsig[e2] > 0:
                                eng.wait_ge(esem[e2], prog.nsig[e2])
                return body

            block.sync(make("sync"))
            block.scalar(make("scalar"))
            block.gpsimd(make("gpsimd"))
            block.vector(make("vector"))
            block.tensor(make("tensor"))


class Arena:
    def __init__(self, t, nbytes):
        self.t, self.n, self.off = t, nbytes, 0

    def reset(self):
        self.off = 0

    def get(self, nbytes, dt, shape=None):
        o = (self.off + 31) // 32 * 32
        assert o + nbytes <= self.n, ("arena overflow", o + nbytes, self.n)
        self.off = o + nbytes
        v = self.t[:, o:o + nbytes].bitcast(dt)
        if shape is not None:
            names = " ".join("d%d" % i for i in range(len(shape)))
            v = v.rearrange("p (%s) -> p %s" % (names, names), **{"d%d" % i: s for i, s in enumerate(shape)})
        return v


def build_program():
    nc = bass.Bass("TRN2", target_bir_lowering=False)
    dram = lambda n, s, d, k: nc.dram_tensor(n, s, d, kind=k).ap()
    x_all = dram("x_all", [NT, D], F32, "ExternalInput")
    ck = dram("ck", [NPOOL_ROWS, 1024], F32, "ExternalInput")
    cv = dram("cv", [NPOOL_ROWS, 1024], F32, "ExternalInput")
    pt = dram("pt", [1, 256], I32, "ExternalInput")
    state = dram("state", [16, 4, 128, 256], F32, "ExternalInput")
    w_in = dram("w_in", [D, NIN], F32, "ExternalInput")
    w_out = dram("w_out", [D, D], F32, "ExternalInput")
    gpre_d = dram("gpre", [128, KC], F32, "ExternalInput")
    gpost_d = dram("gpost", [D], F32, "ExternalInput")
    sbias_d = dram("sbias", [8], F32, "ExternalInput")
    sbrow_d = dram("sbrow", [128, 1], F32, "ExternalInput")
    walpha_d = dram("walpha", [17, 512], F32, "ExternalInput")
    gnorm_d = dram("gnorm", [128, 2], F32, "ExternalInput")
    cst_d = dram("cst", [128, NCST], F32, "ExternalInput")
    y_p = dram("y_p", [SEQ, D], F32, "ExternalOutput")
    y_s = dram("y_s", [NS, D], F32, "ExternalOutput")
    nk_p = dram("nk_p", [LP, 1024], F32, "ExternalOutput")
    nv_p = dram("nv_p", [LP, 1024], F32, "ExternalOutput")
    ns_p = dram("ns_p", [4, 128, 256], F32, "ExternalOutput")
    nk_s = dram("nk_s", [NS, 1024], F32, "ExternalOutput")
    nv_s = dram("nv_s", [NS, 1024], F32, "ExternalOutput")
    ns_s = dram("ns_s", [16, 4, 128, 256], F32, "ExternalOutput")
    w_in_v = w_in.rearrange("(k p) n -> p k n", p=128)
    w_out_v = w_out.rearrange("(k p) n -> p k n", p=128)

    RC_BYTES = 63488
    dbg = dram("dbg", [128, NT], F32, "ExternalOutput")
    dbg2 = dram("dbg2", [128, 16, NT - NMETA], F32, "ExternalOutput") if STAGE == 77 else None
    with contextlib.ExitStack() as st:
        sb = lambda n, s, d: st.enter_context(nc.sbuf_tensor(n, s, d))
        RA = sb("RA", [128, KC * NT], BF16)
        hT = RA[:, :].rearrange("p (k t) -> p k t", k=KC)
        wout = RA[:, 0:KC * D].rearrange("p (k n) -> p k n", k=KC)
        RB = sb("RB", [128, KC * (NT - NMETA)], BF16)
        mixT = RB[:, :].rearrange("p (k t) -> p k t", k=KC)
        NMX = NT - NMETA
        cst = sb("cst_t", [128, NCST], F32)
        identb = sb("identb", [128, 128], BF16)
        maskab = sb("maskab", [128, 128], BF16)
        maskSb = sb("maskSb", [128, 256], BF16)
        ones_f = sb("ones_f", [128, 512], F32)
        gpre = sb("gpre_t", [128, KC], F32)
        sbias = sb("sbias_t", [128, 8], F32)
        sbrow = sb("sbrow_t", [128, 1], F32)
        gnorm = sb("gnorm_t", [128, 2], F32)
        stat = sb("stat", [128, 16], F32)
        bh = sb("bh", [128, 1], F32)
        sqT_s = sb("sqT_s", [128, 8, NS], BF16)
        skT_s = sb("skT_s", [128, 8, NS], BF16)
        sv_s = sb("sv_s", [NS, 8, 128], BF16)
        sgate_s = sb("sgate_s", [128, 8, NS], BF16)
        arena_t = sb("arena", [128, RC_BYTES], U8)
        AR = Arena(arena_t, RC_BYTES)
        pb = [st.enter_context(nc.psum_tensor("pb%d" % i, [128, 512], F32)) for i in range(8)]
        pbh = [p[:, :].bitcast(BF16) for p in pb]

        P = Prog()
        ident_f = cst[:, C_ID:C_ID + 128]

        s_c = P.slot("cst")
        P.dma("sync", lambda e: e.dma_start(out=cst[:, :], in_=cst_d), s_c, writes=["cst"])
        P.dma("sync", lambda e: e.dma_start(out=gpre[:, :], in_=gpre_d), s_c, writes=["gpre"])
        P.dma("sync", lambda e: e.dma_start(out=sbias[:, :], in_=sbias_d.partition_broadcast(128)), s_c, writes=["sbias"])
        P.dma("sync", lambda e: e.dma_start(out=sbrow[:, :], in_=sbrow_d), s_c, writes=["sbrow"])
        P.dma("sync", lambda e: e.dma_start(out=gnorm[:, :], in_=gnorm_d), s_c, writes=["gnorm"])
        P.op("vector", lambda e: e.tensor_copy(identb[:, :], cst[:, C_ID:C_ID + 128]), reads=["cst"], writes=["identb"])
        P.op("vector", lambda e: e.tensor_copy(maskab[:, :], cst[:, C_MA:C_MA + 128]), reads=["cst"], writes=["maskab"])
        P.op("vector", lambda e: e.tensor_copy(maskSb[:, :], cst[:, C_MS:C_MS + 256]), reads=["cst"], writes=["maskSb"])
        P.op("gpsimd", lambda e: e.memset(ones_f[:, :], 1.0), writes=["ones_f"])

        AR.reset()
        xt = [AR.get(D * 4, F32) for _ in range(2)]
        xn = [AR.get(D * 2, BF16) for _ in range(2)]
        s_x = [P.slot("x0"), P.slot("x1")]
        for t, (t0, n) in enumerate(TILES):
            j = t % 2
            P.dma("sync", lambda e, j=j, t0=t0, n=n: e.dma_start(out=xt[j][:n, :], in_=x_all[t0:t0 + n, :]), s_x[j],
                  writes=["xt%d" % j])
            P.op("scalar", lambda e, j=j, n=n: e.activation(out=xn[j][:n, :], in_=xt[j][:n, :], func=AF.Square,
                                                           accum_out=stat[:n, 0:1]),
                 reads=["xt%d" % j], writes=["xn%d" % j, "st0"])
            P.op("scalar", lambda e, n=n: e.activation(out=stat[:n, 1:2], in_=stat[:n, 0:1], func=AF.Ln, bias=EPS, scale=1.0 / D),
                 reads=["st0"], writes=["st1"])
            P.op("scalar", lambda e, n=n: e.activation(out=stat[:n, 2:3], in_=stat[:n, 1:2], func=AF.Exp, scale=-0.5),
                 reads=["st1"], writes=["st2"])
            P.op("vector", lambda e, j=j, n=n: e.tensor_scalar(xn[j][:n, :], xt[j][:n, :], stat[:n, 2:3], None, ALU.mult),
                 reads=["xt%d" % j, "st2"], writes=["xn%d" % j])
            for half in range(2):
                bk = 2 * j + half
                for kk in range(8):
                    k = 8 * half + kk
                    P.op("tensor", lambda e, bk=bk, kk=kk, k=k, j=j, n=n: e.transpose(
                        pbh[bk][:, kk * 128:kk * 128 + n], xn[j][:n, k * 128:(k + 1) * 128], identb[:n, :n]),
                        reads=["xn%d" % j, "identb"], writes=["pb%d" % bk])
                P.op("vector", lambda e, bk=bk, half=half, t0=t0, n=n: e.tensor_tensor(
                    hT[:, 8 * half:8 * half + 8, t0:t0 + n],
                    pbh[bk].rearrange("p (a b) -> p a b", a=8)[:, :, 0:n],
                    gpre[:, 8 * half:8 * half + 8].unsqueeze(2).to_broadcast([128, 8, n]), ALU.mult),
                    reads=["pb%d" % bk, "gpre"], writes=["hT"])
        P.barrier()
        s_dbg = P.slot("dbg")
        P.dma("gpsimd", lambda e: e.dma_start(out=dbg, in_=hT[:, 3, :]), s_dbg, reads=["hT"])

        s_w = [P.slot("w%d" % i) for i in range(3)]
        wctr = [0]
        ipb = [0]

        def load_w(wring, c0, ncols):
            i = wctr[0] % 3
            wctr[0] += 1
            P.dma("gpsimd", lambda e, i=i, c0=c0, ncols=ncols: e.dma_start(out=wring[i][:, :, 0:ncols], in_=w_in_v[:, :, c0:c0 + ncols]),
                  s_w[i], writes=["wr%d" % i])
            return i

        def fm(wring, wi, ncols, evac):
            for (c0, m) in TCH:
                bk = ipb[0] % 2
                ipb[0] += 1
                for k in range(KC):
                    P.op("tensor", lambda e, bk=bk, k=k, c0=c0, m=m: e.matmul(
                        pb[bk][:ncols, :m], lhsT=wring[wi][:, k, 0:ncols], rhs=hT[:, k, c0:c0 + m], start=(k == 0), stop=(k == KC - 1)),
                        reads=["wr%d" % wi, "hT"], writes=["pb%d" % bk])
                evac(pb[bk][:ncols, :m], c0, m, "pb%d" % bk)

        def tm(wring, wi, ncols, evac):
            for t, (t0, n) in enumerate(TILES):
                bk = ipb[0] % 2
                ipb[0] += 1
                for k in range(KC):
                    P.op("tensor", lambda e, bk=bk, k=k, t0=t0, n=n: e.matmul(
                        pb[bk][:n, :ncols], lhsT=hT[:, k, t0:t0 + n], rhs=wring[wi][:, k, 0:ncols], start=(k == 0), stop=(k == KC - 1)),
                        reads=["wr%d" % wi, "hT"], writes=["pb%d" % bk])
                evac(pb[bk][:n, :ncols], t, "pb%d" % bk)

        def sb_chain_A(T_, n, K, zfill, bias_ap, bias_reg):
            nch = (K + 511) // 512
            ai = T_["actr"][0] % len(T_["a2"])
            T_["actr"][0] += 1
            a_buf = T_["a2"][ai]
            an = "arow%d" % ai
            for c in range(nch):
                c0 = c * 512
                m = min(512, K - c0)
                bk = 2 + (T_["zctr"][0] % T_["znb"])
                T_["zctr"][0] += 1
                bn = "pb%d" % bk
                zfill(c, c0, m, pb[bk], bn)
                e_ = T_["esp"][c % 2]
                en = "esp%d" % (c % 2)
                pc = T_["pc"][c % 2]
                pn = "pc%d" % (c % 2)
                P.op("scalar", lambda e, e_=e_, bk=bk, m=m: e.activation(out=e_[:n, :m], in_=pb[bk][:n, :m], func=AF.Exp,
                                                                       bias=bias_ap, scale=SCALE),
                     reads=[bn, bias_reg], writes=[en])
                P.op("scalar", lambda e, e_=e_, m=m: e.activation(out=e_[:n, :m], in_=e_[:n, :m], func=AF.Identity, bias=1.0, scale=1.0),
                     reads=[en], writes=[en])
                if c == 0:
                    P.op("vector", lambda e, pc=pc: e.memset(pc[:n, 0:1], 1.0), writes=[pn])
                else:
                    pp = T_["pc"][(c - 1) % 2]
                    P.op("vector", lambda e, pc=pc, pp=pp: e.tensor_copy(pc[:n, 0:1], pp[:n, 512:513]),
                         reads=["pc%d" % ((c - 1) % 2)], writes=[pn])
                P.op("vector", lambda e, pc=pc, e_=e_, m=m: e.tensor_tensor_scan(
                    pc[:n, 1:m + 1], e_[:n, :m], ones_f[:n, :m], pc[:n, 0:1], ALU.mult, ALU.mult),
                    reads=[pn, en, "ones_f"], writes=[pn])
                P.op("vector", lambda e, pc=pc, c0=c0, m=m: e.tensor_tensor(
                    T_["w"][:n, c0:c0 + m], pc[:n, 1:m + 1], pc[:n, 0:m], ALU.subtract),
                    reads=[pn], writes=["wrow"])
            return {"n": n, "K": K, "nch": nch, "a": a_buf, "an": an}

        def sb_chain_B(T_, info):
            n, K, nch, a_buf, an = info["n"], info["K"], info["nch"], info["a"], info["an"]
            ml = K - (nch - 1) * 512
            pl = T_["pc"][(nch - 1) % 2]
            P.op("vector", lambda e: e.reciprocal(stat[:n, 4:5], pl[:n, ml:ml + 1]),
                 reads=["pc%d" % ((nch - 1) % 2)], writes=["st4"])
            P.op("vector", lambda e: e.tensor_scalar(a_buf[:n, :K], T_["w"][:n, :K], stat[:n, 4:5], None, ALU.mult),
                 reads=["wrow", "st4"], writes=[an])

        def sb_chain(T_, n, K, zfill, bias_ap, bias_reg):
            info = sb_chain_A(T_, n, K, zfill, bias_ap, bias_reg)
            sb_chain_B(T_, info)
            T_["a"], T_["an"] = info["a"], info["an"]

        def chain_temps(znb=4, na=2):
            return {"esp": [AR.get(512 * 4, F32) for _ in range(2)],
                    "pc": [AR.get(513 * 4, F32) for _ in range(2)],
                    "w": AR.get(2112 * 4, F32), "a2": [AR.get(2112 * 2, BF16) for _ in range(na)], "actr": [0], "a": None, "an": None,
                    "aT": AR.get(17 * 128 * 2, BF16, [17, 128]), "zctr": [0], "znb": znb}

        def transpose_blocks(T_, n, blocks, a_buf=None, an=None):
            if a_buf is None:
                a_buf, an = T_["a"], T_["an"]
            for g0 in range(0, len(blocks), 8):
                grp = blocks[g0:g0 + 8]
                bk = 6 + ((g0 // 8) % 2)
                for i, (col0, m) in enumerate(grp):
                    P.op("tensor", lambda e, bk=bk, i=i, col0=col0, m=m: e.transpose(
                        pbh[bk][:m, i * 128:i * 128 + n], a_buf[:n, col0:col0 + m], identb[:n, :n]),
                        reads=[an, "identb"], writes=["pb%d" % bk])
                ng = len(grp)
                P.op("scalar", lambda e, bk=bk, g0=g0, ng=ng: e.copy(
                    T_["aT"][:, g0:g0 + ng, :n], pbh[bk].rearrange("p (a b) -> p a b", a=8)[:, 0:ng, 0:n]),
                    reads=["pb%d" % bk], writes=["aT%d" % (g0 // 8)])

        def gate_chunk(ap, c0, m, bn, kidx, gate, gate2, sg_save=None):
            lo = max(c0, NMETA)
            mm = c0 + m - lo
            src = ap[:, lo - c0:lo - c0 + mm]
            P.op("scalar", lambda e: e.activation(out=gate[:, :mm], in_=src, func=AF.Exp, scale=-1.0), reads=[bn], writes=["gate"])
            P.op("vector", lambda e: e.tensor_scalar(gate[:, :mm], gate[:, :mm], 1.0, None, ALU.add), reads=["gate"], writes=["gate"])
            P.op("vector", lambda e: e.reciprocal(gate[:, :mm], gate[:, :mm]), reads=["gate"], writes=["gate"])
            P.op("vector", lambda e: e.tensor_tensor(gate2[:, :mm], src, gate[:, :mm], ALU.mult), reads=[bn, "gate"], writes=["gate2"])
            ma = mm
            if sg_save is not None and c0 + m > LP:
                ma = LP - lo
                P.op("vector", lambda e: e.tensor_copy(sgate_s[:, sg_save, :], gate2[:, ma:ma + NS]), reads=["gate2"], writes=["sgate_s"])
            P.op("vector", lambda e: e.tensor_tensor(mixT[:, kidx, lo - NMETA:lo - NMETA + ma], mixT[:, kidx, lo - NMETA:lo - NMETA + ma],
                                                     gate2[:, :ma], ALU.mult), reads=["gate2", "mixT%d" % kidx], writes=["mixT%d" % kidx])

        AR.reset()
        wring = [AR.get(KC * 128 * 2, BF16, [KC, 128]) for _ in range(3)]
        sqT = AR.get(NT * 2, BF16)
        skT = AR.get(NT * 2, BF16)
        svh = AR.get(18 * 128 * 2, BF16, [18, 128])
        kvst = [AR.get(128 * 4, F32) for _ in range(4)]
        gate = AR.get(512 * 4, F32)
        gate2 = AR.get(512 * 4, F32)
        CT = chain_temps()
        s_kv = [P.slot("kv%d" % i) for i in range(4)]
        kvc = [0]

        for h in range(NH_SB if STAGE >= 2 else 0):
            wq = load_w(wring, h * 128, 128)
            wk = load_w(wring, 1024 + h * 128, 128)
            wv = load_w(wring, 2048 + h * 128, 128)

            def ev_q(ap, c0, m, bn, h=h):
                P.op("scalar", lambda e: e.copy(sqT[:, c0:c0 + m], ap), reads=[bn], writes=["sqT"])
                if c0 == 2048:
                    P.op("vector", lambda e: e.tensor_copy(sqT_s[:, h, :], ap[:, LP - 2048:LP - 2048 + NS]), reads=[bn], writes=["sqT_s"])

            def ev_k(ap, c0, m, bn, h=h):
                P.op("scalar", lambda e: e.copy(skT[:, c0:c0 + m], ap), reads=[bn], writes=["skT"])
                if c0 == 2048:
                    P.op("vector", lambda e: e.tensor_copy(skT_s[:, h, :], ap[:, LP - 2048:LP - 2048 + NS]), reads=[bn], writes=["skT_s"])

            if 'b' in SUB:
                fm(wring, wq, 128, ev_q)
            if 'c' in SUB:
                fm(wring, wk, 128, ev_k)

            def ev_ktm(ap, t, bn, h=h):
                t0, n = TILES[t]
                j = kvc[0] % 4
                kvc[0] += 1
                P.op("scalar", lambda e: e.copy(kvst[j][:n, :], ap), reads=[bn], writes=["kvst%d" % j])
                dst = nk_p[t0:t0 + n, h * 128:(h + 1) * 128] if t < 17 else nk_s[:, h * 128:(h + 1) * 128]
                P.dma("sync", lambda e: e.dma_start(out=dst, in_=kvst[j][:n, :]), s_kv[j], reads=["kvst%d" % j])

            def ev_vtm(ap, t, bn, h=h):
                t0, n = TILES[t]
                j = kvc[0] % 4
                kvc[0] += 1
                P.op("scalar", lambda e: e.copy(kvst[j][:n, :], ap), reads=[bn], writes=["kvst%d" % j])
                dst = nv_p[t0:t0 + n, h * 128:(h + 1) * 128] if t < 17 else nv_s[:, h * 128:(h + 1) * 128]
                P.dma("sync", lambda e: e.dma_start(out=dst, in_=kvst[j][:n, :]), s_kv[j], reads=["kvst%d" % j])
                P.op("vector", lambda e: e.tensor_copy(svh[:n, t, :], ap), reads=[bn], writes=["svh"])
                if t == 17:
                    P.op("vector", lambda e: e.tensor_copy(sv_s[:, h, :], ap), reads=[bn], writes=["sv_s"])

            if 'd' in SUB:
                tm(wring, wk, 128, ev_ktm)
            if 'e' in SUB:
                tm(wring, wv, 128, ev_vtm)

            P.op("vector", lambda e, h=h: e.tensor_copy(bh[:, :], sbias[:, h:h + 1]), reads=["sbias"], writes=["bh"])
            def mk_zfill(i):
                t0, n = TILES[i]
                K = t0 + n
                d0 = t0

                def zfill(c, c0, m, bank, bn):
                    lo, hi = max(c0, d0), min(c0 + m, K)
                    has_mask = lo < hi
                    P.op("tensor", lambda e: e.matmul(bank[:n, :m], lhsT=sqT[:, t0:t0 + n], rhs=skT[:, c0:c0 + m],
                                                      start=True, stop=not has_mask),
                         reads=["sqT", "skT"], writes=[bn])
                    if has_mask:
                        P.op("tensor", lambda e: e.matmul(bank[:n, lo - c0:hi - c0], lhsT=identb[:n, :n], rhs=maskab[:n, lo - d0:hi - d0],
                                                          start=False, stop=True),
                             reads=["identb", "maskab"], writes=[bn])
                return zfill

            NQ = 17 if STAGE >= 3 else 0
            infos = [None] * (NQ + 1)
            if NQ:
                infos[0] = sb_chain_A(CT, TILES[0][1], TILES[0][0] + TILES[0][1], mk_zfill(0), bh[:TILES[0][1], 0:1], "bh")
            for i in range(NQ):
                t0, n = TILES[i]
                sb_chain_B(CT, infos[i])
                if i + 1 < NQ:
                    t1, n1 = TILES[i + 1]
                    infos[i + 1] = sb_chain_A(CT, n1, t1 + n1, mk_zfill(i + 1), bh[:n1, 0:1], "bh")
                blocks = [(0, 16)] + [(16 + 128 * b, 128) for b in range(i)]
                transpose_blocks(CT, n, blocks, infos[i]["a"], infos[i]["an"])
                ob = i % 2
                for kb, (col0, m) in enumerate(blocks):
                    P.op("tensor", lambda e, kb=kb, m=m, ob=ob, n=n: e.matmul(
                        pb[ob][:, :n], lhsT=svh[:m, kb, :], rhs=CT["aT"][:m, kb, :n], start=(kb == 0), stop=(kb == len(blocks) - 1)),
                        reads=["svh", "aT%d" % (kb // 8)], writes=["pb%d" % ob])
                if i >= 1:
                    P.op("scalar", lambda e, ob=ob, t0=t0, n=n, h=h: e.copy(mixT[:, h, t0 - NMETA:t0 - NMETA + n], pb[ob][:, :n]),
                         reads=["pb%d" % ob], writes=["mixT%d" % h])

            wg = load_w(wring, 3072 + h * 128, 128)

            def ev_g(ap, c0, m, bn, h=h):
                gate_chunk(ap, c0, m, bn, h, gate, gate2, sg_save=h)

            if STAGE >= 4:
                fm(wring, wg, 128, ev_g)
        P.barrier()
        if STAGE >= 3:
            P.dma("gpsimd", lambda e: e.dma_start(out=dbg[:, 0:NMX], in_=mixT[:, 0, :]), s_dbg, reads=["mixT0"])
            P.barrier()

        AR.reset()
        wring = [AR.get(KC * 128 * 2, BF16, [KC, 128]) for _ in range(3)]
        gqT = AR.get(NT * 2, BF16)
        gkT = AR.get(NT * 2, BF16)
        gk_tm = AR.get(18 * 128 * 2, BF16, [18, 128])
        gv_tm = AR.get(18 * 256 * 2, BF16, [18, 256])
        gaT = AR.get(NT * 2, BF16)
        walb = AR.get(512 * 2, BF16)
        Sf = AR.get(256 * 4, F32)
        Sb = AR.get(256 * 2, BF16)
        lsp2 = [AR.get(128 * 4, F32) for _ in range(2)]
        eb2 = [AR.get(128 * 4, F32) for _ in range(2)]
        enb2 = [AR.get(128 * 4, F32) for _ in range(2)]
        rex2 = [AR.get(128 * 4, F32) for _ in range(2)]
        qt2 = [AR.get(128 * 2, BF16) for _ in range(2)]
        kt2 = [AR.get(128 * 2, BF16) for _ in range(2)]
        attn2 = [AR.get(128 * 2, BF16) for _ in range(2)]
        khat2 = [AR.get(128 * 2, BF16) for _ in range(2)]
        osq = AR.get(2 * 128 * 4, F32, [2, 128])
        rstd = AR.get(128 * 4, F32)
        gate = AR.get(512 * 4, F32)
        gate2 = AR.get(512 * 4, F32)
        S0f = [AR.get(256 * 4, F32) for _ in range(2)]
        S0b = [AR.get(256 * 2, BF16) for _ in range(2)]
        snew = [AR.get(256 * 4, F32) for _ in range(2)]
        khm = AR.get(16 * 128 * 2, BF16, [16, 128])
        s_wa = P.slot("wa")
        s_s0 = [P.slot("s0a"), P.slot("s0b")]
        s_s0b = [P.slot("s0ba"), P.slot("s0bb")]
        s_sn = [P.slot("sna"), P.slot("snb")]
        s_nsp = P.slot("nsp")
        s0c = [0]
        NG = 4 if STAGE >= 6 else 0
        if NG:
            P.dma("gpsimd", lambda e: e.dma_start(out=walb[0:17, :], in_=walpha_d), s_wa, writes=["walb"])
            P.op("gpsimd", lambda e: e.memset(gaT[0:32, :], 1.0), writes=["gaT"])
            wa = load_w(wring, 7168, 16)
            fm(wring, wa, 16, lambda ap, c0, m, bn: P.op("scalar", lambda e: e.copy(gaT[0:16, c0:c0 + m], ap), reads=[bn], writes=["gaT"]))
        for g in range(NG if NH_SB == 8 else min(NG, 1)):
            wq = load_w(wring, 4096 + g * 128, 128)
            wk = load_w(wring, 4608 + g * 128, 128)
            fm(wring, wq, 128, lambda ap, c0, m, bn: P.op("scalar", lambda e: e.copy(gqT[:, c0:c0 + m], ap), reads=[bn], writes=["gqT"]))
            fm(wring, wk, 128, lambda ap, c0, m, bn: P.op("scalar", lambda e: e.copy(gkT[:, c0:c0 + m], ap), reads=[bn], writes=["gkT"]))
            tm(wring, wk, 128, lambda ap, t, bn: P.op("scalar", lambda e: e.copy(gk_tm[:TILES[t][1], t, :], ap), reads=[bn], writes=["gk_tm"]))
            for jj in range(2):
                wv = load_w(wring, 5120 + g * 256 + jj * 128, 128)
                tm(wring, wv, 128, lambda ap, t, bn, jj=jj: P.op("scalar", lambda e: e.copy(gv_tm[:TILES[t][1], t, jj * 128:(jj + 1) * 128], ap),
                                                              reads=[bn], writes=["gv_tm"]))
            P.op("vector", lambda e: e.memset(Sf[:, :], 0.0), writes=["Sf"])
            P.op("vector", lambda e: e.memset(Sb[:, :], 0.0), writes=["Sb"])
            def gla_front(t, g=g):
                t0, n = TILES[t]
                fb = t % 2
                samp = (t == 17)
                lsp, eb, enb, rex, qt, kt, attn, khat = lsp2[fb], eb2[fb], enb2[fb], rex2[fb], qt2[fb], kt2[fb], attn2[fb], khat2[fb]
                F = lambda nm: "%s_%d" % (nm, fb)
                cTU, cTL, cCA = (C_TUS, C_TLS, C_CAS) if samp else (C_TU, C_TL, C_CA)
                P.op("tensor", lambda e, t0=t0, n=n, g=g: e.matmul(pb[3][:n, 128:256], lhsT=gaT[0:17, t0:t0 + n], rhs=walb[0:17, g * 128:(g + 1) * 128],
                                                                  start=True, stop=True), reads=["gaT", "walb"], writes=["pb3"])
                P.op("scalar", lambda e, n=n: e.activation(out=lsp[:n, :], in_=pb[3][:n, 128:256], func=AF.Exp, scale=-1.0), reads=["pb3"], writes=[F("lsp")])
                P.op("scalar", lambda e, n=n: e.activation(out=lsp[:n, :], in_=lsp[:n, :], func=AF.Ln, bias=1.0, scale=1.0), reads=[F("lsp")], writes=[F("lsp")])
                P.op("tensor", lambda e, n=n, cTU=cTU: e.matmul(pb[2][:, 0:n], lhsT=lsp[:n, :], rhs=cst[:n, cTU:cTU + n], start=True, stop=True),
                     reads=[F("lsp"), "cst"], writes=["pb2"])
                P.op("tensor", lambda e, n=n, cTL=cTL: e.matmul(pb[2][:n, 128:256], lhsT=cst[:n, cTL:cTL + n], rhs=lsp[:n, :], start=True, stop=True),
                     reads=[F("lsp"), "cst"], writes=["pb2"])
                P.op("scalar", lambda e, n=n: e.activation(out=eb[:, :n], in_=pb[2][:, 0:n], func=AF.Exp), reads=["pb2"], writes=[F("eb")])
                P.op("scalar", lambda e, n=n: e.activation(out=enb[:, :n], in_=pb[2][:, 0:n], func=AF.Exp, scale=-1.0), reads=["pb2"], writes=[F("enb")])
                P.op("scalar", lambda e, n=n: e.activation(out=rex[:n, :], in_=pb[2][:n, 128:256], func=AF.Exp), reads=["pb2"], writes=[F("rex")])
                P.op("vector", lambda e, t0=t0, n=n: e.scalar_tensor_tensor(qt[:, :n], gqT[:, t0:t0 + n], SCALE, eb[:, :n], ALU.mult, ALU.mult),
                     reads=["gqT", F("eb")], writes=[F("qt")])
                P.op("vector", lambda e, t0=t0, n=n: e.tensor_tensor(kt[:, :n], gkT[:, t0:t0 + n], enb[:, :n], ALU.mult), reads=["gkT", F("enb")], writes=[F("kt")])
                P.op("vector", lambda e, t=t, n=n: e.tensor_tensor(khat[:n, :], gk_tm[:n, t, :], rex[:n, :], ALU.mult), reads=["gk_tm", F("rex")], writes=[F("khat")])
                P.op("tensor", lambda e, n=n: e.matmul(pb[3][:n, 0:n], lhsT=kt[:, :n], rhs=qt[:, :n], start=True, stop=True), reads=[F("kt"), F("qt")], writes=["pb3"])
                P.op("vector", lambda e, n=n, cCA=cCA: e.tensor_tensor(attn[:n, :n], pb[3][:n, 0:n], cst[:n, cCA:cCA + n], ALU.mult),
                     reads=["pb3", "cst"], writes=[F("attn")])
            def gla_back(t, g=g):
                t0, n = TILES[t]
                fb = t % 2
                samp = (t == 17)
                lsp, eb, enb, rex, qt, kt, attn, khat = lsp2[fb], eb2[fb], enb2[fb], rex2[fb], qt2[fb], kt2[fb], attn2[fb], khat2[fb]
                F = lambda nm: "%s_%d" % (nm, fb)
                for j in range(2):
                    P.op("tensor", lambda e, j=j, t=t, n=n: e.matmul(pb[4 + j][:, 0:n], lhsT=gv_tm[:n, t, j * 128:(j + 1) * 128], rhs=attn[:n, :n],
                                                                    start=True, stop=False), reads=["gv_tm", F("attn")], writes=["pb%d" % (4 + j)])
                if not samp:
                    chunks = [(0, 16)] if n == 16 else [(0, 64), (64, 128)]
                    for ci, (a_, b_) in enumerate(chunks):
                        last = ci == len(chunks) - 1
                        for j in range(2):
                            P.op("tensor", lambda e, j=j, a_=a_, b_=b_, last=last: e.matmul(
                                pb[4 + j][:, a_:b_], lhsT=Sb[:, j * 128:(j + 1) * 128], rhs=qt[:, a_:b_], start=False, stop=last),
                                reads=["Sb", F("qt")], writes=["pb%d" % (4 + j)])
                        P.op("tensor", lambda e, a_=a_, b_=b_, t=t: e.matmul(pb[6][:, 0:256], lhsT=khat[a_:b_, :], rhs=gv_tm[a_:b_, t, :], start=True, stop=True),
                             reads=[F("khat"), "gv_tm"], writes=["pb6"])
                        P.op("vector", lambda e, b_=b_: e.scalar_tensor_tensor(Sf[:, :], Sf[:, :], eb[:, b_ - 1:b_], pb[6][:, 0:256], ALU.mult, ALU.add),
                             reads=["Sf", F("eb"), "pb6"], writes=["Sf"])
                        P.op("scalar", lambda e: e.copy(Sb[:, :], Sf[:, :]), reads=["Sf"], writes=["Sb"])
                    if t == 16:
                        P.dma("sync", lambda e, g=g: e.dma_start(out=ns_p[g], in_=Sf[:, :]), s_nsp, reads=["Sf"])
                else:
                    for b in range(16):
                        jb = s0c[0] % 2
                        s0c[0] += 1
                        P.dma("sync", lambda e, jb=jb, b=b, g=g: e.dma_start(out=S0f[jb][:, :], in_=state[b, g]), s_s0[jb], writes=["S0f%d" % jb])
                        P.dma("gpsimd", lambda e, jb=jb, b=b, g=g: e.dma_start(out=S0b[jb][:, :], in_=state[b, g]), s_s0b[jb], writes=["S0b%d" % jb])
                        for j in range(2):
                            P.op("tensor", lambda e, j=j, jb=jb, b=b: e.matmul(
                                pb[4 + j][:, 4 * b:4 * b + 4], lhsT=S0b[jb][:, j * 128:(j + 1) * 128], rhs=qt[:, 4 * b:4 * b + 4], start=False, stop=(b == 15)),
                                reads=["S0b%d" % jb, F("qt")], writes=["pb%d" % (4 + j)])
                        if b == 0:
                            P.op("vector", lambda e: e.tensor_tensor(khm[:NS, :, :], khat[:NS, :].unsqueeze(1).to_broadcast([NS, 16, 128]),
                                                                     cst[:NS, C_MB:C_MB + 16].unsqueeze(2).to_broadcast([NS, 16, 128]), ALU.mult),
                                 reads=[F("khat"), "cst"], writes=["khm"])
                        P.op("tensor", lambda e, b=b: e.matmul(pb[6][:, 0:256], lhsT=khm[:NS, b, :], rhs=gv_tm[:NS, 17, :], start=True, stop=True),
                             reads=["khm", "gv_tm"], writes=["pb6"])
                        P.op("vector", lambda e, jb=jb, b=b: e.scalar_tensor_tensor(snew[jb][:, :], S0f[jb][:, :], eb[:, 4 * b + 3:4 * b + 4], pb[6][:, 0:256],
                                                                                    ALU.mult, ALU.add), reads=["S0f%d" % jb, F("eb"), "pb6"], writes=["snew%d" % jb])
                        P.dma("sync", lambda e, jb=jb, b=b, g=g: e.dma_start(out=ns_s[b, g], in_=snew[jb][:, :]), s_sn[jb], reads=["snew%d" % jb])
                if t >= 1:
                    for j in range(2):
                        P.op("scalar", lambda e, j=j, n=n: e.activation(out=osq[:, j, :n], in_=pb[4 + j][:, 0:n], func=AF.Square),
                             reads=["pb%d" % (4 + j)], writes=["osq%d" % j])
                    for j in range(2):
                        P.op("tensor", lambda e, j=j, n=n: e.matmul(pb[7][:, 0:n], lhsT=ones_f[:, 0:128], rhs=osq[:, j, :n], start=(j == 0), stop=(j == 1)),
                             reads=["ones_f", "osq%d" % j], writes=["pb7"])
                    P.op("scalar", lambda e, n=n: e.activation(out=rstd[:, :n], in_=pb[7][:, 0:n], func=AF.Ln, bias=EPS, scale=1.0 / 256), reads=["pb7"], writes=["rstd"])
                    P.op("scalar", lambda e, n=n: e.activation(out=rstd[:, :n], in_=rstd[:, :n], func=AF.Exp, scale=-0.5), reads=["rstd"], writes=["rstd"])
                    for j in range(2):
                        kidx = 8 + 2 * g + j
                        P.op("vector", lambda e, j=j, n=n, kidx=kidx, t0=t0: e.scalar_tensor_tensor(
                            mixT[:, kidx, t0 - NMETA:t0 - NMETA + n], pb[4 + j][:, 0:n], gnorm[:, j:j + 1], rstd[:, :n], ALU.mult, ALU.mult),
                            reads=["pb%d" % (4 + j), "gnorm", "rstd"], writes=["mixT%d" % kidx])
            gla_front(0)
            for t in range(18):
                if t + 1 < 18:
                    gla_front(t + 1)
                gla_back(t)
            for j in range(2):
                wgg = load_w(wring, 6144 + g * 256 + j * 128, 128)
                fm(wring, wgg, 128, lambda ap, c0, m, bn, kidx=8 + 2 * g + j: gate_chunk(ap, c0, m, bn, kidx, gate, gate2))
        P.barrier()
        if STAGE >= 6:
            P.dma("gpsimd", lambda e: e.dma_start(out=dbg[:, 0:NMX], in_=mixT[:, 8, :]), s_dbg, reads=["mixT8"])
            P.barrier()
        for kk in range(8 + 2 * (NG if NH_SB == 8 else min(NG, 1)), 16):
            P.op("gpsimd", lambda e, kk=kk: e.memset(mixT[:, kk, :], 0.0), writes=["mixT%d" % kk])
        if STAGE < 7:
            P.op("gpsimd", lambda e: e.memset(mixT[:, 0:8, LP - NMETA:NMX], 0.0), reads=["mixT%d" % h for h in range(8)],
                 writes=["mixT%d" % h for h in range(8)])
        P.barrier()

        if STAGE >= 7:
            AR.reset()
            ptb = AR.get(256 * 4, I32)
            idxt = AR.get(256 * 4, I32)
            iotf = AR.get(4, F32)
            NKV = 5
            Kbuf = [AR.get(2 * 1024 * 2, BF16, [2, 1024]) for _ in range(NKV)]
            Vbuf = Kbuf
            KT = [AR.get(8 * 256 * 2, BF16, [8, 256]) for _ in range(2)]
            qm = AR.get(8 * 128 * 2, BF16, [8, 128])
            osel = AR.get(8 * 128 * 4, F32, [8, 128])
            CT3 = chain_temps(znb=2, na=1)
            s_pt = P.slot("pt")
            s_kb = [[P.slot("kb%d_%d" % (i, p)) for p in range(2)] for i in range(NKV)]
            s_vb = s_kb
            P.dma("sync", lambda e: e.dma_start(out=ptb[:, :], in_=pt.rearrange("a n -> (a n)").partition_broadcast(128)), s_pt, writes=["ptb"])
            P.op("gpsimd", lambda e: e.iota(iotf[:, 0:1], pattern=[[0, 1]], base=0, channel_multiplier=1, allow_small_or_imprecise_dtypes=True),
                 writes=["iotf"])
            P.op("vector", lambda e: e.tensor_scalar(idxt[:, :], ptb[:, :], 128.0, iotf[:, 0:1], ALU.mult, ALU.add), reads=["ptb", "iotf"], writes=["idxt"])
            kctr = [0]
            vctr = [0]
            for G in range(4):
                P.op("vector", lambda e: e.memset(qm[:, :, :], 0.0), writes=["qm"])
                for h in range(8):
                    P.op("vector", lambda e, h=h, G=G: e.tensor_copy(
                        qm[:, h, :].rearrange("p (b x) -> p b x", b=4)[:, :, 4 * h:4 * h + 4],
                        sqT_s[:, h, 16 * G:16 * G + 16].rearrange("p (b q) -> p b q", b=4)), reads=["sqT_s"], writes=["qm"])

                def zfill(c, c0, m, bank, bn, G=G):
                    if c < 4:
                        for b4 in range(4):
                            bl = 4 * G + b4
                            for half in range(2):
                                ks = kctr[0] % NKV
                                kctr[0] += 1
                                kt_ = vctr[0] % 2
                                vctr[0] += 1
                                for p in range(2):
                                    col = bl * 16 + 4 * c + 2 * half + p
                                    P.dma("gpsimd", lambda e, ks=ks, p=p, col=col: e.indirect_dma_start(
                                        out=Kbuf[ks][:, p, :], out_offset=None, in_=ck,
                                        in_offset=bass.IndirectOffsetOnAxis(ap=idxt[:, col:col + 1], axis=0)),
                                        s_kb[ks][p], reads=["idxt"], writes=["Kbuf%d_%d" % (ks, p)])
                                tb = 6 - 2 * kt_
                                for h in range(8):
                                    for p in range(2):
                                        P.op("tensor", lambda e, ks=ks, h=h, p=p, tb=tb: e.transpose(
                                            pbh[tb + h // 4][:, (h % 4) * 256 + p * 128:(h % 4) * 256 + (p + 1) * 128],
                                            Kbuf[ks][:, p, h * 128:(h + 1) * 128], identb[:, :]),
                                            reads=["Kbuf%d_%d" % (ks, p), "identb"], writes=["pb%d" % (tb + h // 4)])
                                P.op("scalar", lambda e, kt_=kt_, tb=tb: e.copy(KT[kt_][:, 0:4, :], pbh[tb].rearrange("p (a b) -> p a b", a=4)),
                                     reads=["pb%d" % tb], writes=["KT%da" % kt_])
                                P.op("vector", lambda e, kt_=kt_, tb=tb: e.tensor_copy(KT[kt_][:, 4:8, :], pbh[tb + 1].rearrange("p (a b) -> p a b", a=4)),
                                     reads=["pb%d" % (tb + 1)], writes=["KT%db" % kt_])
                                for h in range(8):
                                    P.op("tensor", lambda e, kt_=kt_, h=h, b4=b4, half=half: e.matmul(
                                        bank[32 * b4:32 * b4 + 32, half * 256:(half + 1) * 256], lhsT=qm[:, h, 32 * b4:32 * b4 + 32],
                                        rhs=KT[kt_][:, h, :], start=(h == 0), stop=(h == 7), tile_position=(0, 32 * b4)),
                                        reads=["qm", "KT%d%s" % (kt_, "a" if h < 4 else "b")], writes=[bn])
                    else:
                        for h in range(8):
                            P.op("tensor", lambda e, h=h: e.matmul(bank[:, 0:NS], lhsT=qm[:, h, :], rhs=skT_s[:, h, :], start=(h == 0), stop=False),
                                 reads=["qm", "skT_s"], writes=[bn])
                        P.op("tensor", lambda e: e.matmul(bank[:, 0:NS], lhsT=identb[:, :], rhs=maskSb[:, 64 * G:64 * G + 64], start=False, stop=True),
                             reads=["identb", "maskSb"], writes=[bn])

                sb_chain(CT3, 128, 2048 + NS, zfill, sbrow[:, 0:1], "sbrow")
                blocks = [(128 * j, 128) for j in range(16)] + [(2048, NS)]
                transpose_blocks(CT3, 128, blocks)
                for b4 in range(4):
                    bl = 4 * G + b4
                    for jp in range(8):
                        vs = kctr[0] % NKV
                        kctr[0] += 1
                        for p in range(2):
                            col = bl * 16 + 2 * jp + p
                            P.dma("gpsimd", lambda e, vs=vs, p=p, col=col: e.indirect_dma_start(
                                out=Vbuf[vs][:, p, :], out_offset=None, in_=cv,
                                in_offset=bass.IndirectOffsetOnAxis(ap=idxt[:, col:col + 1], axis=0)),
                                s_vb[vs][p], reads=["idxt"], writes=["Kbuf%d_%d" % (vs, p)])
                        for p in range(2):
                            j = 2 * jp + p
                            for hb in range(2):
                                P.op("tensor", lambda e, vs=vs, p=p, j=j, hb=hb, b4=b4: e.matmul(
                                    pb[hb][32 * b4:32 * b4 + 32, :], lhsT=CT3["aT"][:, j, 32 * b4:32 * b4 + 32], rhs=Vbuf[vs][:, p, hb * 512:(hb + 1) * 512],
                                    start=(j == 0), stop=False, tile_position=(0, 32 * b4)),
                                    reads=["Kbuf%d_%d" % (vs, p), "aT%d" % (j // 8)], writes=["pb%d" % hb])
                    for hb in range(2):
                        P.op("tensor", lambda e, hb=hb, b4=b4: e.matmul(
                            pb[hb][32 * b4:32 * b4 + 32, :], lhsT=CT3["aT"][:NS, 16, 32 * b4:32 * b4 + 32],
                            rhs=sv_s[:, 4 * hb:4 * hb + 4, :].rearrange("p a b -> p (a b)"), start=False, stop=True, tile_position=(0, 32 * b4)),
                            reads=["sv_s", "aT2"], writes=["pb%d" % hb])
                for hb in range(2):
                    P.op("vector", lambda e, hb=hb: e.tensor_tensor(
                        osel[:, 4 * hb:4 * hb + 4, :], pb[hb][:, :].rearrange("p (a b) -> p a b", a=4),
                        cst[:, C_HM + 4 * hb:C_HM + 4 * hb + 4].unsqueeze(2).to_broadcast([128, 4, 128]), ALU.mult),
                        reads=["pb%d" % hb, "cst"], writes=["osel%d" % hb])
                for h in range(8):
                    P.op("tensor", lambda e, h=h: e.matmul(pb[2][:, 16 * h:16 * h + 16], lhsT=osel[:, h, :], rhs=cst[:, C_RS:C_RS + 16], start=True, stop=True),
                         reads=["osel%d" % (h // 4), "cst"], writes=["pb2"])
                P.op("vector", lambda e, G=G: e.tensor_tensor(
                    mixT[:, 0:8, LP - NMETA + 16 * G:LP - NMETA + 16 * G + 16], pb[2][:, 0:128].rearrange("p (a b) -> p a b", a=8),
                    sgate_s[:, :, 16 * G:16 * G + 16], ALU.mult), reads=["pb2", "sgate_s"], writes=["mixT%d" % h for h in range(8)])
            P.barrier()
            P.dma("gpsimd", lambda e: e.dma_start(out=dbg[:, 0:NMX], in_=mixT[:, 0, :]), s_dbg, reads=["mixT0"])
            P.barrier()

        if STAGE == 77:
            for kk in range(16):
                for hf in range(2):
                    P.dma("gpsimd", lambda e, kk=kk, hf=hf: e.dma_start(out=dbg2[:, kk, hf * 1056:(hf + 1) * 1056], in_=mixT[:, kk, hf * 1056:(hf + 1) * 1056]),
                          s_dbg, reads=["mixT%d" % kk])
            P.barrier()

        AR.reset()
        xo = [AR.get(D * 4, F32) for _ in range(2)]
        yo = [AR.get(D * 4, F32) for _ in range(2)]
        gpost = AR.get(D * 4, F32)
        s_wo = P.slot("wo")
        s_gp = P.slot("gp")
        s_xo = [P.slot("xo0"), P.slot("xo1")]
        s_yo = [P.slot("yo0"), P.slot("yo1")]
        for q4 in range(4):
            P.dma("gpsimd", lambda e, q4=q4: e.dma_start(out=wout[:, 4 * q4:4 * q4 + 4, :], in_=w_out_v[:, 4 * q4:4 * q4 + 4, :]),
                  s_wo, writes=["wout%d" % q4])
        P.dma("sync", lambda e: e.dma_start(out=gpost[:, :], in_=gpost_d.partition_broadcast(128)), s_gp, writes=["gpost"])
        for t in range(1, 18 if STAGE >= 5 else 1):
            t0, n = TILES[t]
            j = t % 2
            m0 = t0 - NMETA
            P.dma("sync", lambda e, j=j, t0=t0, n=n: e.dma_start(out=xo[j][:n, :], in_=x_all[t0:t0 + n, :]), s_xo[j], writes=["xo%d" % j])
            for dc in range(4):
                bk = 4 * j + dc
                for k in range(KC):
                    P.op("tensor", lambda e, bk=bk, k=k, dc=dc, m0=m0, n=n: e.matmul(
                        pb[bk][:n, :], lhsT=mixT[:, k, m0:m0 + n], rhs=wout[:, k, dc * 512:(dc + 1) * 512], start=(k == 0), stop=(k == KC - 1)),
                        reads=["wout%d" % (k // 4)] + ["mixT%d" % kk for kk in range(16)], writes=["pb%d" % bk])
                P.op("scalar", lambda e, bk=bk, dc=dc, j=j, n=n: e.activation(out=yo[j][:n, dc * 512:(dc + 1) * 512], in_=pb[bk][:n, :],
                                                                           func=AF.Square, accum_out=stat[:n, 8 + dc:9 + dc]),
                     reads=["pb%d" % bk], writes=["yo%d" % j, "st8%d" % dc])
            P.op("vector", lambda e, n=n: e.reduce_sum(stat[:n, 12:13], stat[:n, 8:12], axis=mybir.AxisListType.X),
                 reads=["st8%d" % dc for dc in range(4)], writes=["st12"])
            P.op("scalar", lambda e, n=n: e.activation(out=stat[:n, 13:14], in_=stat[:n, 12:13], func=AF.Ln, bias=EPS, scale=1.0 / D),
                 reads=["st12"], writes=["st13"])
            P.op("scalar", lambda e, n=n: e.activation(out=stat[:n, 14:15], in_=stat[:n, 13:14], func=AF.Exp, scale=-0.5),
                 reads=["st13"], writes=["st14"])
            for dc in range(4):
                bk = 4 * j + dc
                P.op("vector", lambda e, bk=bk, dc=dc, j=j, n=n: e.scalar_tensor_tensor(
                    yo[j][:n, dc * 512:(dc + 1) * 512], pb[bk][:n, :], stat[:n, 14:15], gpost[:n, dc * 512:(dc + 1) * 512], ALU.mult, ALU.mult),
                    reads=["pb%d" % bk, "st14", "gpost"], writes=["yo%d" % j])
            P.op("gpsimd", lambda e, j=j, n=n: e.tensor_tensor(yo[j][:n, :], yo[j][:n, :], xo[j][:n, :], ALU.add),
                 reads=["yo%d" % j, "xo%d" % j], writes=["yo%d" % j])
            dst = y_p[t0 - NMETA:t0 - NMETA + n, :] if t < 17 else y_s[:, :]
            P.dma("sync", lambda e, j=j, n=n, dst=dst: e.dma_start(out=dst, in_=yo[j][:n, :]), s_yo[j], reads=["yo%d" % j])

        P.emit(nc)
    return nc


def make_consts():
    c = np.zeros((128, NCST), np.float32)
    i = np.arange(128)
    c[:, C_ID:C_ID + 128] = np.eye(128, dtype=np.float32)
    c[:, C_MA:C_MA + 128] = np.where(i[None, :] < i[:, None], 0.0, NEG)
    b4, hh, qq = i // 32, (i // 4) % 8, i % 4
    col = np.arange(64)
    for G in range(4):
        ok = ((col[None, :] // 4) == (4 * G + b4)[:, None]) & ((col[None, :] % 4) < qq[:, None])
        c[:, C_MS + 64 * G:C_MS + 64 * (G + 1)] = np.where(ok, 0.0, NEG)
    same = (i[:, None] // 64) == (i[None, :] // 64)
    c[:, C_TU:C_TU + 128] = np.where(same & (i[:, None] <= i[None, :]), -1.0 / 16, 0.0)
    c[:, C_TL:C_TL + 128] = np.where(same & (i[:, None] > i[None, :]), -1.0 / 16, 0.0)
    c[:, C_CA:C_CA + 128] = np.where(same & (i[:, None] <= i[None, :]), 1.0, 0.0)
    j = np.arange(64)
    same4 = (j[:, None] // 4) == (j[None, :] // 4)
    c[:64, C_TUS:C_TUS + 64] = np.where(same4 & (j[:, None] <= j[None, :]), -1.0 / 16, 0.0)
    c[:64, C_TLS:C_TLS + 64] = np.where(same4 & (j[:, None] > j[None, :]), -1.0 / 16, 0.0)
    c[:64, C_CAS:C_CAS + 64] = np.where(same4 & (j[:, None] <= j[None, :]), 1.0, 0.0)
    c[:64, C_MB:C_MB + 16] = (j[:, None] // 4 == np.arange(16)[None, :])
    c[:, C_HM:C_HM + 8] = (hh[:, None] == np.arange(8)[None, :])
    c[:, C_RS:C_RS + 16] = ((b4 * 4 + qq)[:, None] == np.arange(16)[None, :])
    return c


_NC_CACHE = {}


def kernel(x_prompt, x_sample, cache_k, cache_v, state_gla, page_table, meta_tokens,
           norm_pre_g, w_in, sb_bias, w_alpha, b_alpha, gla_norm_g, w_out, norm_post_g):
    f32 = lambda a: np.ascontiguousarray(np.asarray(a, dtype=np.float32))
    x_prompt, x_sample, meta_tokens = f32(x_prompt), f32(x_sample), f32(meta_tokens)
    ckr = f32(cache_k).reshape(NPOOL_ROWS, 1024)
    cvr = f32(cache_v).reshape(NPOOL_ROWS, 1024)
    state_gla = f32(state_gla)
    page_table = np.ascontiguousarray(np.asarray(page_table, dtype=np.int32))
    w_in_ = f32(w_in)[0]
    w_out_ = f32(w_out)[0]
    gpre = f32(norm_pre_g)[0].reshape(KC, 128).T.copy()
    gpost = f32(norm_post_g)[0].reshape(D)
    sbias = f32(sb_bias)[0].reshape(8)
    sbrow = np.ascontiguousarray(np.tile(np.repeat(sbias, 4), 4).reshape(128, 1))
    walpha = np.concatenate([f32(w_alpha)[0], f32(b_alpha)[0].reshape(1, 512)], axis=0)
    gnorm = f32(gla_norm_g)[0].reshape(2, 128).T.copy()
    cst = make_consts()
    if "nc" not in _NC_CACHE:
        _NC_CACHE["nc"] = build_program()
    nc = _NC_CACHE["nc"]
    in_maps = []
    for c in range(8):
        x_all = np.concatenate([meta_tokens, x_prompt[c], x_sample[16 * c:16 * c + 16].reshape(NS, D)], axis=0)
        in_maps.append({
            "x_all": x_all, "ck": ckr, "cv": cvr,
            "pt": page_table[16 * c:16 * c + 16].reshape(1, 256),
            "state": state_gla[0, 16 * c:16 * c + 16],
            "w_in": w_in_, "w_out": w_out_, "gpre": gpre, "gpost": gpost, "sbias": sbias, "sbrow": sbrow,
            "walpha": walpha, "gnorm": gnorm, "cst": cst,
        })
    res = run_bass_kernel_spmd(nc, in_maps, core_ids=list(range(8)))
    R = res.results
    y_prompt = np.stack([R[c]["y_p"] for c in range(8)], axis=0)
    y_sample = np.concatenate([R[c]["y_s"].reshape(16, 4, D) for c in range(8)], axis=0)
    nk_p = np.stack([R[c]["nk_p"].reshape(LP, 8, 128) for c in range(8)], axis=0)[None]
    nv_p = np.stack([R[c]["nv_p"].reshape(LP, 8, 128) for c in range(8)], axis=0)[None]
    ns_p = np.stack([R[c]["ns_p"] for c in range(8)], axis=0)[None]
    nk_s = np.concatenate([R[c]["nk_s"].reshape(16, 4, 8, 128) for c in range(8)], axis=0)[None]
    nv_s = np.concatenate([R[c]["nv_s"].reshape(16, 4, 8, 128) for c in range(8)], axis=0)[None]
    ns_s = np.concatenate([R[c]["ns_s"] for c in range(8)], axis=0)[None]
    return (y_prompt, y_sample, nk_p, nv_p, ns_p, nk_s, nv_s, ns_s)
```
